# Optimizing a Trainium2 kernel written in Bass

```python
import jax, jax.numpy as jnp
from jax import lax
import numpy as np

D_MODEL = 2048
BATCH = 16
SEQ = 2048
DEPTH = 1
DEC_BATCH = 4
DEC_SEQ = 8192
PAST_LEN = 128

HEAD_DIM = 128
ATTN_GROUPS = ((128, 1), (512, 4), (2048, 16))
HEADS_PER_GROUP = 4
N_ATTN_HEADS = HEADS_PER_GROUP * len(ATTN_GROUPS)
ATTN_WIDTH = N_ATTN_HEADS * HEAD_DIM
ATTN_OUT_WIDTH = HEADS_PER_GROUP * HEAD_DIM
ROPE_THETA = 10000.0

SSD_EXPAND = 2
SSD_INNER = SSD_EXPAND * D_MODEL
SSD_HEAD_DIM = 64
SSD_HEADS = SSD_INNER // SSD_HEAD_DIM
SSD_GROUPS = 8
SSD_STATE = 128
SSD_CONV = 5
SSD_CHUNK = 128
SSD_CONV_DIM = SSD_INNER + 2 * SSD_GROUPS * SSD_STATE

D_FF = 5632
FFN_CONV = 3

EPS = 1e-6
NEG_INF = -1e30

IN_WIDTHS = (ATTN_WIDTH, ATTN_WIDTH, ATTN_WIDTH, SSD_INNER, SSD_CONV_DIM, 2 * SSD_HEADS, D_MODEL, D_MODEL)
IN_WIDTH = int(sum(IN_WIDTHS))
IN_OFFSETS = [int(o) for o in np.cumsum(IN_WIDTHS)[:-1]]

kernel_name = "hybrid_dilated_attn_ssd_encoder"


def rmsnorm(x, w):
    x32 = x.astype(jnp.float32)
    return x32 * lax.rsqrt(jnp.mean(x32 * x32, axis=-1, keepdims=True) + EPS) * w.astype(jnp.float32)


def group_rmsnorm(y, w, groups):
    b, s, c = y.shape
    yg = y.reshape(b, s, groups, c // groups)
    yg = yg * lax.rsqrt(jnp.mean(yg * yg, axis=-1, keepdims=True) + EPS)
    return yg.reshape(b, s, c) * w.astype(jnp.float32)


def depthwise_conv_centred(x, w, b):
    k_w = w.shape[0]
    s = x.shape[1]
    pad = k_w // 2
    xp = jnp.pad(x, ((0, 0), (pad, pad), (0, 0)))
    out = xp[:, 0:s] * w[0]
    for j in range(1, k_w):
        out = out + xp[:, j:j + s] * w[j]
    return out + b


def rotary_tables(s):
    pos = jnp.arange(s, dtype=jnp.float32)
    inv_freq = ROPE_THETA ** (-jnp.arange(0, HEAD_DIM, 2, dtype=jnp.float32) / HEAD_DIM)
    ang = pos[:, None] * inv_freq[None, :]
    return jnp.cos(ang)[:, None, :], jnp.sin(ang)[:, None, :]


def apply_rotary(t, cos, sin):
    half = t.shape[-1] // 2
    t1, t2 = t[..., :half].astype(jnp.float32), t[..., half:].astype(jnp.float32)
    return jnp.concatenate([t1 * cos - t2 * sin, t1 * sin + t2 * cos], axis=-1)


def dilated_window_attention(q, k, v, dilation, radius):
    bsz, s, h, e = q.shape
    n = s // dilation
    blk = radius
    nb = -(-n // blk)
    n_pad = nb * blk

    def sub(t):
        return t.reshape(bsz, n, dilation, h, e).transpose(0, 2, 1, 3, 4)

    qb = jnp.pad(sub(q), ((0, 0), (0, 0), (0, n_pad - n), (0, 0), (0, 0))).reshape(bsz, dilation, nb, blk, h, e)

    def key_blocks(t):
        tp = jnp.pad(sub(t), ((0, 0), (0, 0), (blk, n_pad - n + blk), (0, 0), (0, 0)))
        tp = tp.reshape(bsz, dilation, nb + 2, blk, h, e)
        return jnp.concatenate([tp[:, :, :-2], tp[:, :, 1:-1], tp[:, :, 2:]], axis=3)

    kb = key_blocks(k)
    vb = key_blocks(v).astype(jnp.float32)
    q_idx = jnp.arange(nb)[:, None] * blk + jnp.arange(blk)[None, :]
    k_idx = jnp.arange(nb)[:, None] * blk - blk + jnp.arange(3 * blk)[None, :]
    rel = k_idx[:, None, :] - q_idx[:, :, None]
    valid = (jnp.abs(rel) <= radius) & (k_idx[:, None, :] >= 0) & (k_idx[:, None, :] < n)
    scores = jnp.einsum('bdnqhe,bdnkhe->bdnhqk', qb, kb).astype(jnp.float32) * (e ** -0.5)
    scores = jnp.where(valid[None, None, :, None], scores, NEG_INF)
    m = jnp.max(scores, axis=-1, keepdims=True)
    p = jnp.exp(scores - m)
    den = jnp.sum(p, axis=-1, keepdims=True)
    out = jnp.einsum('bdnhqk,bdnkhe->bdnqhe', p / den, vb)
    lse = (m + jnp.log(den))[..., 0].transpose(0, 1, 2, 4, 3)
    out = out.reshape(bsz, dilation, n_pad, h, e)[:, :, :n].transpose(0, 2, 1, 3, 4).reshape(bsz, s, h, e)
    lse = lse.reshape(bsz, dilation, n_pad, h)[:, :, :n].transpose(0, 2, 1, 3).reshape(bsz, s, h)
    return out, lse


def ssd_chunked_scan(x, dt, a, bm, cm):
    bsz, s, nh, hp = x.shape
    g, ns = bm.shape[2], bm.shape[3]
    r = nh // g
    L = SSD_CHUNK
    nc = s // L
    xs = x.reshape(bsz, nc, L, g, r, hp).transpose(1, 0, 2, 3, 4, 5)
    dts = dt.reshape(bsz, nc, L, g, r).transpose(1, 0, 2, 3, 4)
    bs = bm.reshape(bsz, nc, L, g, ns).transpose(1, 0, 2, 3, 4)
    cs = cm.reshape(bsz, nc, L, g, ns).transpose(1, 0, 2, 3, 4)
    ag = a.reshape(g, r)
    lower = jnp.tril(jnp.ones((L, L), dtype=bool))[None, :, :, None, None]

    def step(state, inp):
        xc, dtc, bc, cc = inp
        acum = jnp.cumsum(dtc * ag, axis=1)
        diff = acum[:, :, None] - acum[:, None, :]
        decay = jnp.exp(jnp.where(lower, diff, -jnp.inf))
        cb = jnp.einsum('btgn,bsgn->btsg', cc, bc)
        wts = cb[..., None] * decay * dtc[:, None]
        y_intra = jnp.einsum('btsgr,bsgrp->btgrp', wts, xc)
        y_state = jnp.einsum('btgn,bgrpn->btgrp', cc, state) * jnp.exp(acum)[..., None]
        dec_end = jnp.exp(acum[:, -1:] - acum) * dtc
        new_state = state * jnp.exp(acum[:, -1])[..., None, None] + jnp.einsum('bsgr,bsgrp,bsgn->bgrpn', dec_end, xc, bc)
        return new_state, y_intra + y_state

    state0 = jnp.zeros((bsz, g, r, hp, ns), jnp.float32)
    _, ys = lax.scan(step, state0, (xs, dts, bs, cs))
    return ys.transpose(1, 0, 2, 3, 4, 5).reshape(bsz, s, nh, hp)


def token_mixer(h, w_in, ssd_conv_w, ssd_conv_b, dt_bias_fwd, dt_bias_bwd, a_log_fwd, a_log_bwd,
                ssd_d, ssd_norm_w, w_attn_proj, w_ssd_proj, w_out):
    bsz, s, _ = h.shape
    proj = h @ w_in
    q, k, v, z, xbc, dt_raw, g_attn, g_ssd = jnp.split(proj, IN_OFFSETS, axis=-1)

    cos, sin = rotary_tables(s)
    q = apply_rotary(q.reshape(bsz, s, N_ATTN_HEADS, HEAD_DIM), cos, sin)
    k = apply_rotary(k.reshape(bsz, s, N_ATTN_HEADS, HEAD_DIM), cos, sin)
    v = v.reshape(bsz, s, N_ATTN_HEADS, HEAD_DIM)
    outs, lses = [], []
    for gi, (window, dilation) in enumerate(ATTN_GROUPS):
        sl = slice(gi * HEADS_PER_GROUP, (gi + 1) * HEADS_PER_GROUP)
        o, l = dilated_window_attention(q[:, :, sl], k[:, :, sl], v[:, :, sl], dilation, window // (2 * dilation))
        outs.append(o)
        lses.append(l)
    mix_w = jax.nn.softmax(jnp.stack(lses, axis=0), axis=0)
    attn = jnp.sum(mix_w[..., None] * jnp.stack(outs, axis=0), axis=0)
    a_branch = attn.reshape(bsz, s, ATTN_OUT_WIDTH).astype(h.dtype) @ w_attn_proj

    xbc = jax.nn.silu(depthwise_conv_centred(xbc, ssd_conv_w, ssd_conv_b)).astype(jnp.float32)
    xs, bm, cm = jnp.split(xbc, [SSD_INNER, SSD_INNER + SSD_GROUPS * SSD_STATE], axis=-1)
    xs = xs.reshape(bsz, s, SSD_HEADS, SSD_HEAD_DIM)
    bm = bm.reshape(bsz, s, SSD_GROUPS, SSD_STATE)
    cm = cm.reshape(bsz, s, SSD_GROUPS, SSD_STATE)
    dt_f, dt_b = jnp.split(dt_raw.astype(jnp.float32), 2, axis=-1)
    dt_f = jax.nn.softplus(dt_f + dt_bias_fwd.astype(jnp.float32))
    dt_b = jax.nn.softplus(dt_b + dt_bias_bwd.astype(jnp.float32))
    a_f = -jnp.exp(a_log_fwd.astype(jnp.float32))
    a_b = -jnp.exp(a_log_bwd.astype(jnp.float32))
    flip = lambda t: jnp.flip(t, axis=1)
    y_fwd = ssd_chunked_scan(xs, dt_f, a_f, bm, cm)
    y_bwd = flip(ssd_chunked_scan(flip(xs), flip(dt_b), a_b, flip(bm), flip(cm)))
    y = y_fwd + y_bwd + ssd_d.astype(jnp.float32)[:, None] * xs
    y = y.reshape(bsz, s, SSD_INNER) * jax.nn.silu(z.astype(jnp.float32))
    y = group_rmsnorm(y, ssd_norm_w, SSD_GROUPS)
    s_branch = y.astype(h.dtype) @ w_ssd_proj

    merged = jax.nn.sigmoid(g_attn) * a_branch + jax.nn.sigmoid(g_ssd) * s_branch
    return merged @ w_out


def conv_ffn(h, w_up, ffn_conv_w, ffn_conv_b, w_down):
    up = h @ w_up
    gate, val = jnp.split(up, [D_FF], axis=-1)
    gate = depthwise_conv_centred(gate, ffn_conv_w, ffn_conv_b)
    return (jax.nn.gelu(gate, approximate=False) * val) @ w_down


def encoder_trunk(x, c, w_ada, b_ada, norm_mix_w, w_in, ssd_conv_w, ssd_conv_b, dt_bias_fwd, dt_bias_bwd,
                  a_log_fwd, a_log_bwd, ssd_d, ssd_norm_w, w_attn_proj, w_ssd_proj, w_out, norm_ffn_w,
                  w_up, ffn_conv_w, ffn_conv_b, w_down, norm_f_w):
    dtype = x.dtype
    for l in range(DEPTH):
        mod = (jax.nn.silu(c) @ w_ada[l] + b_ada[l])[:, None, :].astype(jnp.float32)
        sh_m, sc_m, g_m, sh_f, sc_f, g_f = jnp.split(mod, 6, axis=-1)
        h = (rmsnorm(x, norm_mix_w[l]) * (1.0 + sc_m) + sh_m).astype(dtype)
        mix = token_mixer(h, w_in[l], ssd_conv_w[l], ssd_conv_b[l], dt_bias_fwd[l], dt_bias_bwd[l],
                          a_log_fwd[l], a_log_bwd[l], ssd_d[l], ssd_norm_w[l], w_attn_proj[l],
                          w_ssd_proj[l], w_out[l])
        x = (x + g_m * mix).astype(dtype)
        h = (rmsnorm(x, norm_ffn_w[l]) * (1.0 + sc_f) + sh_f).astype(dtype)
        x = (x + g_f * conv_ffn(h, w_up[l], ffn_conv_w[l], ffn_conv_b[l], w_down[l])).astype(dtype)
    return rmsnorm(x, norm_f_w).astype(dtype)


def setup_inputs(seed: int = 0) -> dict:
    key = jax.random.key(seed)
    ks = jax.random.split(key, 32)
    f32 = jnp.float32

    def nrm(k, shape, scale):
        return jax.random.normal(k, shape, f32) * scale

    dt0 = jnp.exp(jax.random.uniform(ks[12], (DEPTH, 2, SSD_HEADS), f32) * (np.log(0.1) - np.log(0.001)) + np.log(0.001))
    dt_bias = dt0 + jnp.log(-jnp.expm1(-dt0))
    a_log = jnp.log(jax.random.uniform(ks[13], (DEPTH, 2, SSD_HEADS), f32, 1.0, 16.0))
    return {
        "x_prompt": nrm(ks[0], (BATCH, SEQ, D_MODEL), 1.0),
        "x_sample": nrm(ks[1], (DEC_BATCH, DEC_SEQ, D_MODEL), 1.0),
        "c_prompt": nrm(ks[2], (BATCH, D_MODEL), 1.0),
        "c_sample": nrm(ks[3], (DEC_BATCH, D_MODEL), 1.0),
        "w_ada": nrm(ks[4], (DEPTH, D_MODEL, 6 * D_MODEL), 0.5 * D_MODEL ** -0.5),
        "b_ada": nrm(ks[5], (DEPTH, 6 * D_MODEL), 0.02),
        "norm_mix_w": 1.0 + nrm(ks[6], (DEPTH, D_MODEL), 0.05),
        "w_in": nrm(ks[7], (DEPTH, D_MODEL, IN_WIDTH), D_MODEL ** -0.5),
        "ssd_conv_w": nrm(ks[8], (DEPTH, SSD_CONV, SSD_CONV_DIM), SSD_CONV ** -0.5),
        "ssd_conv_b": nrm(ks[9], (DEPTH, SSD_CONV_DIM), 0.02),
        "dt_bias_fwd": dt_bias[:, 0],
        "dt_bias_bwd": dt_bias[:, 1],
        "a_log_fwd": a_log[:, 0],
        "a_log_bwd": a_log[:, 1],
        "ssd_d": 1.0 + nrm(ks[14], (DEPTH, SSD_HEADS), 0.1),
        "ssd_norm_w": 1.0 + nrm(ks[15], (DEPTH, SSD_INNER), 0.05),
        "w_attn_proj": nrm(ks[16], (DEPTH, ATTN_OUT_WIDTH, D_MODEL), ATTN_OUT_WIDTH ** -0.5),
        "w_ssd_proj": nrm(ks[17], (DEPTH, SSD_INNER, D_MODEL), SSD_INNER ** -0.5),
        "w_out": nrm(ks[18], (DEPTH, D_MODEL, D_MODEL), D_MODEL ** -0.5),
        "norm_ffn_w": 1.0 + nrm(ks[19], (DEPTH, D_MODEL), 0.05),
        "w_up": nrm(ks[20], (DEPTH, D_MODEL, 2 * D_FF), D_MODEL ** -0.5),
        "ffn_conv_w": nrm(ks[21], (DEPTH, FFN_CONV, D_FF), FFN_CONV ** -0.5),
        "ffn_conv_b": nrm(ks[22], (DEPTH, D_FF), 0.02),
        "w_down": nrm(ks[23], (DEPTH, D_FF, D_MODEL), D_FF ** -0.5),
        "norm_f_w": 1.0 + nrm(ks[24], (D_MODEL,), 0.05),
    }


def reference(x_prompt, x_sample, c_prompt, c_sample, w_ada, b_ada, norm_mix_w, w_in, ssd_conv_w, ssd_conv_b,
              dt_bias_fwd, dt_bias_bwd, a_log_fwd, a_log_bwd, ssd_d, ssd_norm_w, w_attn_proj, w_ssd_proj,
              w_out, norm_ffn_w, w_up, ffn_conv_w, ffn_conv_b, w_down, norm_f_w):
    y_prompt = encoder_trunk(x_prompt, c_prompt, w_ada, b_ada, norm_mix_w, w_in, ssd_conv_w, ssd_conv_b,
                             dt_bias_fwd, dt_bias_bwd, a_log_fwd, a_log_bwd, ssd_d, ssd_norm_w, w_attn_proj,
                             w_ssd_proj, w_out, norm_ffn_w, w_up, ffn_conv_w, ffn_conv_b, w_down, norm_f_w)
    y_sample = encoder_trunk(x_sample, c_sample, w_ada, b_ada, norm_mix_w, w_in, ssd_conv_w, ssd_conv_b,
                             dt_bias_fwd, dt_bias_bwd, a_log_fwd, a_log_bwd, ssd_d, ssd_norm_w, w_attn_proj,
                             w_ssd_proj, w_out, norm_ffn_w, w_up, ffn_conv_w, ffn_conv_b, w_down, norm_f_w)
    return (y_prompt, y_sample)
```

```python
import numpy as np
from contextlib import ExitStack
import concourse.bass as bass
import concourse.mybir as mybir
from concourse.bass_utils import run_bass_kernel_spmd

F32 = mybir.dt.float32
BF16 = mybir.dt.bfloat16
AF = mybir.ActivationFunctionType
ALU = mybir.AluOpType
AX = mybir.AxisListType

ENGS = ("pe", "act", "dve", "pool", "sp")


class Tl:
    __slots__ = ("name", "t", "lastw", "readers", "dsem", "dcnt", "ddma", "psum")

    def __init__(self, name, t):
        self.psum = False
        self.name = name
        self.t = t
        self.lastw = {}
        self.readers = {}
        self.dsem = None
        self.dcnt = 0
        self.ddma = None

    def __getitem__(self, k):
        return self.t[k]


class Op:
    __slots__ = ("eng", "fn", "deps", "need", "val", "sem", "isdma", "idx")

    def __init__(self, eng, fn):
        self.eng = eng
        self.fn = fn
        self.deps = []
        self.need = False
        self.val = None
        self.sem = None
        self.isdma = False
        self.idx = 0


class MK:
    def __init__(self, nc, es):
        self.nc = nc
        self.es = es
        self.h = {"pe": nc.tensor, "act": nc.scalar, "dve": nc.vector, "pool": nc.gpsimd, "sp": nc.sync}
        self.sem = {e: es.enter_context(nc.semaphore("s_" + e)) for e in ENGS}
        self.cnt = {e: 0 for e in ENGS}
        self.ops = {e: [] for e in ENGS}
        self.tiles = []
        self.dma_tiles = []
        self.nblock = 0
        self.n_ops = 0
        self.pes = es
        self.dpool = []

    def sb(self, name, shape, dt):
        t = self.pes.enter_context(self.nc.sbuf_tensor(name, list(shape), dt))
        tl = Tl(name, t)
        self.tiles.append(tl)
        return tl

    def ps(self, name, shape, dt=F32):
        t = self.pes.enter_context(self.nc.psum_tensor(name, list(shape), dt))
        tl = Tl(name, t)
        tl.psum = True
        self.tiles.append(tl)
        return tl

    def view(self, name, t):
        tl = Tl(name, t)
        self.tiles.append(tl)
        return tl

    def _mk(self, eng, fn, r, w):
        op = Op(eng, fn)
        deps = {}

        def add(d):
            if d is None:
                return
            if d.isdma:
                deps[("dma", id(d.sem))] = d if (("dma", id(d.sem)) not in deps or deps[("dma", id(d.sem))].val < d.val) else deps[("dma", id(d.sem))]
            else:
                if d.eng == "pe" and eng == "pe":
                    return
                k = d.eng
                if k not in deps or deps[k].idx < d.idx:
                    deps[k] = d

        for t in r:
            for d in t.lastw.values():
                add(d)
            if t.psum:
                for ke, d in t.readers.items():
                    if ke != eng:
                        add(d)
        for t in w:
            for d in t.lastw.values():
                add(d)
            for d in t.readers.values():
                add(d)
        op.deps = list(deps.values())
        for d in op.deps:
            d.need = True
        return op

    def _reg(self, op, r, w, key):
        for t in r:
            t.readers[key] = op
        for t in w:
            t.lastw[key] = op
            t.readers = {}
        op.idx = len(self.ops[op.eng])
        self.ops[op.eng].append(op)
        self.n_ops += 1

    def op(self, eng, fn, r=(), w=()):
        op = self._mk(eng, fn, r, w)
        self._reg(op, r, w, eng)
        return op

    def dma(self, out, in_, r=(), w=(), q="sp", **kw):
        tl = (list(w) + list(r))[0]
        if tl.dsem is None:
            if not self.dpool:
                self.dpool.append((self.es.enter_context(self.nc.semaphore("dq%d" % self.n_ops)), 0))
            tl.dsem, tl.dcnt = self.dpool.pop()
            self.dma_tiles.append(tl)
        op = self._mk(q, None, r, w)
        tl.dcnt += 1
        op.isdma = True
        op.sem = tl.dsem
        op.val = 16 * tl.dcnt
        sem = tl.dsem

        def fn(e, out=out, in_=in_, sem=sem, kw=kw):
            return e.dma_start(out=out, in_=in_, **kw).then_inc(sem, 16)
        op.fn = fn
        op.deps = [d for d in op.deps if not (d.isdma and d.sem is sem)]
        for t in r:
            t.readers["dma"] = op
        for t in w:
            t.lastw["dma"] = op
            t.readers = {}
        op.idx = len(self.ops[q])
        self.ops[q].append(op)
        self.n_ops += 1
        return op

    def flush(self):
        nc = self.nc
        tail = []
        for tl in self.dma_tiles:
            if tl.dcnt:
                tail.append((tl.dsem, 16 * tl.dcnt))
        for e in ENGS:
            c = self.cnt[e]
            for op in self.ops[e]:
                if op.isdma:
                    continue
                if op.need:
                    c += 1
                    op.val = c
                    op.sem = self.sem[e]
            self.cnt[e] = c
        ops = self.ops
        hsem = self.sem

        def emit(e, eng):
            seen = {}
            for op in ops[e]:
                for d in op.deps:
                    k = id(d.sem)
                    if seen.get(k, 0) >= d.val:
                        continue
                    seen[k] = d.val
                    eng.wait_ge(d.sem, d.val)
                ins = op.fn(eng)
                if (not op.isdma) and op.need:
                    ins.then_inc(hsem[e], 1)
            if e == "sp":
                for (s, v) in tail:
                    if seen.get(id(s), 0) < v:
                        eng.wait_ge(s, v)

        with nc.Block() as block:
            @block.tensor
            def _(eng):
                emit("pe", eng)

            @block.scalar
            def _(eng):
                emit("act", eng)

            @block.vector
            def _(eng):
                emit("dve", eng)

            @block.gpsimd
            def _(eng):
                emit("pool", eng)

            @block.sync
            def _(eng):
                emit("sp", eng)
        self.ops = {e: [] for e in ENGS}
        for tl in self.tiles:
            tl.lastw = {}
            tl.readers = {}
        self.nblock += 1

    def end_phase(self):
        self.flush()
        for tl in self.dma_tiles:
            self.dpool.append((tl.dsem, tl.dcnt))
        self.tiles = []
        self.dma_tiles = []


D = 2048
SEG = 2048
KC = 16
INW = 19072
OFF = dict(q=0, k=1536, v=3072, z=4608, xbc=8704, dt=14848, ga=14976, gs=17024)
WSPEC = [("w_ada", 2048, 12288), ("w_in", 2048, 19072), ("w_attn", 512, 2048), ("w_ssd", 4096, 2048),
         ("w_out", 2048, 2048), ("w_up", 2048, 11264), ("w_down", 5632, 2048)]
DIL = (1, 4, 16)
NEG = -30000.0
DFF = 5632
NF = 32


def wtile_ap(wf, c0, ncols, kc=KC):
    return wf[:, c0:c0 + ncols].rearrange("(k p) n -> p k n", p=128)


def perm_view(ap2d, d):
    if d == 1:
        return ap2d.rearrange("p (r i) -> p r i", r=1)
    return ap2d.rearrange("p (i r) -> p r i", r=d)


def perm_tile(ap2d, d, j):
    v = perm_view(ap2d, d)
    if d == 1:
        return v[:, 0, 512 * j:512 * j + 512]
    if d == 4:
        return v[:, j, :]
    return v[:, 4 * j:4 * j + 4, :]


def perm_blk(ap2d, d, J):
    v = perm_view(ap2d, d)
    n = 2048 // d
    r = (128 * J) // n
    i0 = (128 * J) % n
    return v[:, r, i0:i0 + 128]


def build(NSEG=4, gather=True, debug=(), stop_after=None):
    T = NSEG * SEG
    nc = bass.Bass("TRN2", target_bir_lowering=False)
    dbg = set(debug)

    def din(name, shape, dt=F32):
        return nc.dram_tensor(name, list(shape), dt, kind="ExternalInput").ap()

    def dscr(name, shape, dt):
        if name in dbg:
            return nc.dram_tensor(name, list(shape), dt, kind="ExternalOutput").ap()
        return nc.dram_tensor(name, list(shape), dt).ap()

    I = {}
    I["x"] = din("x", [T, D])
    I["cT"] = din("cT", [128, KC, NSEG])
    for name, K, N in WSPEC:
        I[name] = din(name, [K // 8 if gather else K, N])
    I["b_ada"] = din("b_ada", [1, 6 * D])
    I["norm_mix_w"] = din("norm_mix_w", [1, D])
    I["norm_ffn_w"] = din("norm_ffn_w", [1, D])
    I["norm_f_w"] = din("norm_f_w", [1, D])
    I["ssd_conv_w"] = din("ssd_conv_w", [128, 48, 5])
    I["ssd_conv_b"] = din("ssd_conv_b", [128, 48])
    I["dt_bias"] = din("dt_bias", [1, 128])
    I["a_log"] = din("a_log", [1, 128])
    I["ssd_d"] = din("ssd_d", [1, 64])
    I["ssd_norm_w"] = din("ssd_norm_w", [1, 4096])
    I["ffn_conv_w"] = din("ffn_conv_w", [128, 44, 3])
    I["ffn_conv_b"] = din("ffn_conv_b", [128, 44])
    I["cosT"] = din("cosT", [128, T])
    I["sinT"] = din("sinT", [128, T])
    I["cst"] = din("cst", [128, 8, 128])
    I["band"] = din("band", [128, 256])
    I["flags"] = din("flags", [1, NF])
    I["kb"] = din("kb", [4, 256])
    y = nc.dram_tensor("y", [T, D], F32, kind="ExternalOutput").ap()

    S = {}
    for name, K, N in WSPEC:
        S["f_" + name] = dscr("f_" + name, [K, N], BF16)
        if gather:
            S["s_" + name] = dscr("s_" + name, [K // 8, N], BF16)
    S["modv"] = dscr("modv", [NSEG, 6 * D], F32)
    S["cstb"] = dscr("cstb", [128, 8, 128], BF16)
    S["bandb"] = dscr("bandb", [128, 256], BF16)
    S["kbb"] = dscr("kbb", [4, 256], BF16)
    S["hT"] = dscr("hT", [KC, 128, T + 4], BF16)
    S["qT"] = dscr("qT", [12, 128, T], BF16)
    S["kT"] = dscr("kT", [12, 128, T + 128], BF16)
    S["V"] = dscr("V", [3, T + 128, 512], BF16)
    S["zs"] = dscr("zs", [T, 4096], F32)
    S["xtok"] = dscr("xtok", [T, 4096], BF16)
    S["Btok"] = dscr("Btok", [T, 1024], BF16)
    S["BT"] = dscr("BT", [8, 128, T], BF16)
    S["CT"] = dscr("CT", [8, 128, T], BF16)
    S["dts"] = dscr("dts", [T, 128], F32)
    S["gT"] = dscr("gT", [2, KC, 128, T], F32)
    S["ao"] = dscr("ao", [3, T, 4, 132], F32)
    S["yf"] = dscr("yf", [T, 4096], F32)
    S["x1"] = dscr("x1", [T, D], F32)
    S["x2"] = dscr("x2", [T, D], F32)
    S["h2T"] = dscr("h2T", [KC, 128, T + 4], BF16)
    S["ynT"] = dscr("ynT", [32, 128, T], BF16)
    S["attnT"] = dscr("attnT", [4, 128, T], BF16)

    with ExitStack() as es:
        k = MK(nc, es)
        phase0(k, nc, I, S, gather)
        if stop_after == 0:
            return nc, k
        phaseM(k, nc, I, S, NSEG)
        if stop_after == "M":
            return nc, k
        norm_phase(k, nc, I["x"], S["modv"], 1, 0, S["hT"], S["cstb"], NSEG)
        if stop_after == "N":
            return nc, k
        phase1(k, nc, I, S, NSEG)
        if stop_after == 1:
            return nc, k
        phase2(k, nc, I, S, NSEG)
        phase2b(k, nc, I, S, NSEG)
        if stop_after == 2:
            return nc, k
        ssd_sweep(k, nc, I, S, NSEG, 0)
        if stop_after == 3:
            return nc, k
        ssd_sweep(k, nc, I, S, NSEG, 1)
        if stop_after == 4:
            return nc, k
        phase5(k, nc, I, S, NSEG)
        if stop_after == 5:
            return nc, k
        norm_phase(k, nc, S["x1"], S["modv"], 4, 3, S["h2T"], S["cstb"], NSEG, tag="m")
        phase6(k, nc, I, S, NSEG)
        if stop_after == 6:
            return nc, k
        final_norm(k, nc, I, S, y, NSEG)
    return nc, k


def phase0(k, nc, I, S, gather):
    with ExitStack() as pes:
        k.pes = pes
        dummy = k.sb("dummy0", [128, 4], F32)
        zt = k.sb("zt0", [128, 1024], BF16)
        k.op("dve", lambda e: e.memset(zt[:], 0.0), w=[zt])
        for name, K, N in WSPEC:
            src = I[name]
            rows = src.shape[0]
            dst = S["s_" + name] if gather else S["f_" + name]
            rstep = max(1, (1 << 20) // N)
            for r0 in range(0, rows, rstep):
                r1 = min(rows, r0 + rstep)
                k.dma(dst[r0:r1, :], src[r0:r1, :], w=[dummy], q="pool")
        k.dma(S["cstb"][:, :, :], I["cst"][:, :, :], w=[dummy], q="pool")
        k.dma(S["bandb"][:, :], I["band"][:, :], w=[dummy], q="pool")
        k.dma(S["kbb"][:, :], I["kb"][:, :], w=[dummy], q="pool")
        if gather:
            for name, K, N in WSPEC:
                def cc(e, name=name):
                    return e.collective_compute("AllGather", ALU.bypass, replica_groups=[list(range(8))],
                                                ins=[S["s_" + name].opt()], outs=[S["f_" + name].opt()])
                k.op("pool", cc, r=[dummy], w=[dummy])
            k.op("pool", lambda e: e.memset(dummy[:], 0.0), w=[dummy])
        T4 = S["hT"].shape[2]
        hTv = S["hT"].rearrange("k p t -> p k t")
        k.dma(hTv[:, :, 0:2], zt[:, 0:32].rearrange("p (k t) -> p k t", t=2), r=[zt])
        k.dma(hTv[:, :, T4 - 2:T4], zt[:, 0:32].rearrange("p (k t) -> p k t", t=2), r=[zt])
        h2v = S["h2T"].rearrange("k p t -> p k t")
        k.dma(h2v[:, :, 0:2], zt[:, 0:32].rearrange("p (k t) -> p k t", t=2), r=[zt])
        k.dma(h2v[:, :, T4 - 2:T4], zt[:, 0:32].rearrange("p (k t) -> p k t", t=2), r=[zt])
        Tk = S["kT"].shape[2]
        kTv = S["kT"].rearrange("h p t -> p h t")
        k.dma(kTv[:, :, 0:64], zt[:, 0:768].rearrange("p (h t) -> p h t", t=64), r=[zt])
        k.dma(kTv[:, :, Tk - 64:Tk], zt[:, 0:768].rearrange("p (h t) -> p h t", t=64), r=[zt])
        Tv = S["V"].shape[1]
        for g in range(3):
            k.dma(S["V"][g, 0:64, :], zt[0:64, 0:512], r=[zt])
            k.dma(S["V"][g, Tv - 64:Tv, :], zt[0:64, 0:512], r=[zt])
        k.end_phase()


def phaseM(k, nc, I, S, NSEG):
    with ExitStack() as pes:
        k.pes = pes
        cT = k.sb("cTs", [128, KC, NSEG], F32)
        cTb = k.sb("cTb", [128, KC, NSEG], BF16)
        mod = k.sb("mod", [NSEG, 6 * D], F32)
        bada = k.sb("bada", [NSEG, 6 * D], F32)
        nw = k.sb("nw", [NSEG, 2, D], F32)
        wt = [k.sb(f"wtM{i}", [128, KC, 512], BF16) for i in range(2)]
        pm = [k.ps(f"pmM{i}", [128, 512], F32) for i in range(2)]
        k.dma(cT[:], I["cT"][:, :, :], w=[cT])
        k.dma(bada[:], I["b_ada"][0:1, :].partition_broadcast(NSEG), w=[bada])
        k.dma(nw[:, 0, :], I["norm_mix_w"][0:1, :].partition_broadcast(NSEG), w=[nw])
        k.dma(nw[:, 1, :], I["norm_ffn_w"][0:1, :].partition_broadcast(NSEG), w=[nw])
        k.op("act", lambda e: e.activation(out=cTb[:], in_=cT[:], func=AF.Silu), r=[cT], w=[cTb])
        wf = S["f_w_ada"]
        k.dma(wt[0][:], wtile_ap(wf, 0, 512), w=[wt[0]])
        for n in range(24):
            if n + 1 < 24:
                k.dma(wt[(n + 1) % 2][:], wtile_ap(wf, 512 * (n + 1), 512), w=[wt[(n + 1) % 2]])
            w_, p_ = wt[n % 2], pm[n % 2]
            for kc in range(KC):
                k.op("pe", lambda e, w_=w_, p_=p_, kc=kc: e.matmul(p_[0:NSEG, :], cTb[:, kc, :], w_[:, kc, :], start=(kc == 0), stop=(kc == KC - 1)),
                     r=[cTb, w_], w=[p_])
            k.op("dve", lambda e, p_=p_, n=n: e.tensor_tensor(out=mod[:, 512 * n:512 * n + 512], in0=p_[0:NSEG, :], in1=bada[:, 512 * n:512 * n + 512], op=ALU.add),
                 r=[p_, bada], w=[mod])
        for part, wi in ((1, 0), (4, 1)):
            k.op("dve", lambda e, part=part, wi=wi: e.scalar_tensor_tensor(out=mod[:, part * D:(part + 1) * D], in0=mod[:, part * D:(part + 1) * D],
                                                                          scalar=1.0, in1=nw[:, wi, :], op0=ALU.add, op1=ALU.mult),
                 r=[mod, nw], w=[mod])
        k.dma(S["modv"][:, :], mod[:], r=[mod])
        k.end_phase()


def norm_phase(k, nc, src, modv, iA, iB, hT_d, cstb, NSEG, tag="n"):
    with ExitStack() as pes:
        k.pes = pes
        ident = k.sb("ident" + tag, [128, 128], BF16)
        A = k.sb("A" + tag, [128, D], F32)
        B = k.sb("B" + tag, [128, D], F32)
        xt = [k.sb(f"xt{tag}{i}", [128, D], F32) for i in range(3)]
        tmp = k.sb("tmp" + tag, [128, D], F32)
        junk = k.sb("junk" + tag, [128, D], F32)
        hb = [k.sb(f"hb{tag}{i}", [128, D], BF16) for i in range(3)]
        ss = [k.sb(f"ss{tag}{i}", [128, 1], F32) for i in range(3)]
        rs = [k.sb(f"rs{tag}{i}", [128, 1], F32) for i in range(3)]
        epst = k.sb("eps" + tag, [128, 1], F32)
        ptb = [k.ps(f"ptb{tag}{i}", [128, 1024], BF16) for i in range(2)]
        hst = [k.sb(f"hst{tag}{i}", [128, KC, 512], BF16) for i in range(2)]
        k.op("pool", lambda e: e.memset(epst[:], 1e-6), w=[epst])
        k.dma(ident[:], cstb[:, 0, :], w=[ident])
        hTv = hT_d.rearrange("k p t -> p k t")
        for seg in range(NSEG):
            k.dma(A[:], modv[seg:seg + 1, iA * D:(iA + 1) * D].partition_broadcast(128), w=[A])
            k.dma(B[:], modv[seg:seg + 1, iB * D:(iB + 1) * D].partition_broadcast(128), w=[B])
            for tt in range(16):
                i = seg * 16 + tt
                x_, h_, s_, r_ = xt[i % 3], hb[i % 3], ss[i % 3], rs[i % 3]
                q = (i // 4) % 2
                k.dma(x_[:], src[i * 128:(i + 1) * 128, :], w=[x_])
                k.op("pool", lambda e, x_=x_: e.tensor_tensor(out=junk[:], in0=x_[:], in1=x_[:], op=ALU.mult), r=[x_], w=[junk])
                k.op("dve", lambda e, s_=s_: e.reduce_sum(out=s_[:], in_=junk[:], axis=AX.X), r=[junk], w=[s_])
                k.op("act", lambda e, s_=s_, r_=r_: e.activation(out=r_[:], in_=s_[:], func=AF.Ln, scale=1.0 / D, bias=epst[:]), r=[s_, epst], w=[r_])
                k.op("act", lambda e, r_=r_: e.activation(out=r_[:], in_=r_[:], func=AF.Exp, scale=-0.5), r=[r_], w=[r_])
                k.op("dve", lambda e, x_=x_, r_=r_: e.scalar_tensor_tensor(out=tmp[:], in0=x_[:], scalar=r_[:], in1=A[:], op0=ALU.mult, op1=ALU.mult),
                     r=[x_, r_, A], w=[tmp])
                k.op("dve", lambda e, h_=h_: e.tensor_tensor(out=h_[:], in0=tmp[:], in1=B[:], op=ALU.add), r=[tmp, B], w=[h_])
                for kc in range(KC):
                    p_ = ptb[kc // 8]
                    k.op("pe", lambda e, p_=p_, kc=kc, h_=h_: e.transpose(out=p_[:, (kc % 8) * 128:(kc % 8 + 1) * 128], in_=h_[:, kc * 128:(kc + 1) * 128], identity=ident[:]),
                         r=[h_, ident], w=[p_])
                t4 = tt % 4
                k.op("act", lambda e, q=q, t4=t4: e.copy(out=hst[q][:, 0:8, t4 * 128:(t4 + 1) * 128], in_=ptb[0][:].rearrange("p (k t) -> p k t", t=128)),
                     r=[ptb[0]], w=[hst[q]])
                k.op("dve", lambda e, q=q, t4=t4: e.tensor_copy(out=hst[q][:, 8:16, t4 * 128:(t4 + 1) * 128], in_=ptb[1][:].rearrange("p (k t) -> p k t", t=128)),
                     r=[ptb[1]], w=[hst[q]])
                if t4 == 3:
                    t0 = (i - 3) * 128
                    k.dma(hTv[:, :, 2 + t0:2 + t0 + 512], hst[q][:], r=[hst[q]])
        k.end_phase()


def phase1(k, nc, I, S, NSEG):
    T = NSEG * SEG
    wf = S["f_w_in"]
    with ExitStack() as pes:
        k.pes = pes
        hT = k.sb("hT1", [128, KC, SEG + 4], BF16)
        wt = [k.sb(f"wt1{i}", [128, KC, 512], BF16) for i in range(2)]
        cosS = k.sb("cosS", [128, SEG], F32)
        sinS = k.sb("sinS", [128, SEG], F32)
        flg = k.sb("flg1", [128, NF], F32)
        cst = k.sb("cst1", [128, 2, 128], BF16)
        cw = k.sb("cw1", [128, 48, 5], F32)
        cb = k.sb("cb1", [128, 48], F32)
        dtb = k.sb("dtb1", [128, 128], F32)
        pm = [k.ps(f"pm1{i}", [128, 512], F32) for i in range(5)]
        ph = k.ps("ph1", [128, 512], F32)
        ptb = [k.ps(f"ptb1{i}", [128, 1024], BF16) for i in range(2)]
        qb = [k.sb(f"qb1{i}", [128, 512], BF16) for i in range(2)]
        qo = [k.sb(f"qo1{i}", [128, 512], BF16) for i in range(2)]
        vo = [k.sb(f"vo1{i}", [128, 512], BF16) for i in range(2)]
        t1 = [k.sb(f"t11{i}", [128, 512], F32) for i in range(2)]
        t2 = [k.sb(f"t21{i}", [128, 512], F32) for i in range(2)]
        zo = [k.sb(f"zo1{i}", [128, 512], F32) for i in range(2)]
        E = k.sb("E1", [128, SEG + 4], F32)
        a1 = k.sb("a11", [128, SEG], F32)
        a2 = k.sb("a21", [128, SEG], F32)
        xo = [k.sb(f"xo1{i}", [128, SEG], BF16) for i in range(2)]
        xts = k.sb("xts1", [128, 16, 512], BF16)
        ds = [k.sb(f"ds1{i}", [128, 128], F32) for i in range(4)]

        k.dma(flg[:], I["flags"][0:1, :].partition_broadcast(128), w=[flg])
        k.dma(cst[:], S["cstb"][:, 0:2, :], w=[cst])
        k.dma(cw[:], I["ssd_conv_w"][:, :, :], w=[cw])
        k.dma(cb[:], I["ssd_conv_b"][:, :], w=[cb])
        k.dma(dtb[:], I["dt_bias"][0:1, :].partition_broadcast(128), w=[dtb])
        ident = cst[:, 0, :]
        pswap = cst[:, 1, :]

        hTd = S["hT"].rearrange("k p t -> p k t")
        sched = []
        for g in range(3):
            sched.append((OFF["q"] + 512 * g, 512, "qk", (0, g)))
        for g in range(3):
            sched.append((OFF["k"] + 512 * g, 512, "qk", (1, g)))
        for g in range(3):
            sched.append((OFF["v"] + 512 * g, 512, "v", g))
        for i in range(8):
            sched.append((OFF["z"] + 512 * i, 512, "z", i))
        for i in range(12):
            sched.append((OFF["xbc"] + 512 * i, 512, "xbc", i))
        sched.append((OFF["dt"], 128, "dt", 0))
        for i in range(4):
            sched.append((OFF["ga"] + 512 * i, 512, "g", (0, i)))
        for i in range(4):
            sched.append((OFF["gs"] + 512 * i, 512, "g", (1, i)))
        nW = len(sched)
        cnt = dict(pm=0, q=0, v=0, z=0, x=0, d=0)

        def nextpm():
            p = pm[cnt["pm"] % 5]
            cnt["pm"] += 1
            return p

        def load_w(idx, si):
            c0, ncols, _, _ = sched[si]
            k.dma(wt[idx % 2][:, :, 0:ncols], wtile_ap(wf, c0, ncols), w=[wt[idx % 2]])

        widx = 0
        pend = [None]
        for seg in range(NSEG):
            tb = seg * SEG
            if pend[0] is not None:
                pend[0]()
                pend[0] = None
            k.dma(hT[:], hTd[:, :, tb:tb + SEG + 4], w=[hT])
            k.op("dve", lambda e, seg=seg: e.tensor_scalar(out=hT[:, :, 0:2], in0=hT[:, :, 0:2], scalar1=flg[:, 2 * seg:2 * seg + 1], scalar2=None, op0=ALU.mult), r=[hT, flg], w=[hT])
            k.op("dve", lambda e, seg=seg: e.tensor_scalar(out=hT[:, :, SEG + 2:SEG + 4], in0=hT[:, :, SEG + 2:SEG + 4], scalar1=flg[:, 2 * seg + 1:2 * seg + 2], scalar2=None, op0=ALU.mult), r=[hT, flg], w=[hT])
            k.dma(cosS[:], I["cosT"][:, tb:tb + SEG], w=[cosS])
            k.dma(sinS[:], I["sinT"][:, tb:tb + SEG], w=[sinS])
            if seg == 0:
                load_w(widx, 0)
            for si in range(nW):
                c0, ncols, kind, idx = sched[si]
                w_ = wt[widx % 2]
                if si + 1 < nW:
                    load_w(widx + 1, si + 1)
                elif seg + 1 < NSEG:
                    load_w(widx + 1, 0)
                widx += 1
                if kind not in ("qk", "xbc") and pend[0] is not None:
                    pend[0]()
                    pend[0] = None
                if kind == "qk":
                    which, g = idx
                    d = DIL[g]
                    L = T // d
                    nseg = SEG // d
                    dst = S["qT"] if which == 0 else S["kT"]
                    off = 0 if which == 0 else 64
                    for hh in range(4):
                        head = 4 * g + hh
                        for j in range(4):
                            p_ = nextpm()
                            ci = cnt["q"] % 2
                            cnt["q"] += 1
                            for kc in range(KC):
                                rhs = perm_tile(hT[:, kc, 2:SEG + 2], d, j)
                                k.op("pe", lambda e, p_=p_, w_=w_, kc=kc, hh=hh, rhs=rhs: e.matmul(p_[:], w_[:, kc, hh * 128:(hh + 1) * 128], rhs, start=(kc == 0), stop=(kc == KC - 1)),
                                     r=[hT, w_], w=[p_])
                            k.op("act", lambda e, p_=p_, ci=ci: e.copy(out=qb[ci][:], in_=p_[:]), r=[p_], w=[qb[ci]])

                            def post(p_=p_, ci=ci, d=d, j=j, head=head, dst=dst, off=off, L=L, nseg=nseg, seg=seg):
                                k.op("pe", lambda e, ci=ci: e.matmul(ph[:], pswap, qb[ci][:], start=True, stop=True), r=[qb[ci], cst], w=[ph])
                                cv = perm_tile(cosS[:], d, j)
                                sv = perm_tile(sinS[:], d, j)
                                if d == 16:
                                    pv = p_[:].rearrange("p (r i) -> p r i", r=4)
                                    phv = ph[:].rearrange("p (r i) -> p r i", r=4)
                                    t1v = t1[ci][:].rearrange("p (r i) -> p r i", r=4)
                                    t2v = t2[ci][:].rearrange("p (r i) -> p r i", r=4)
                                else:
                                    pv, phv, t1v, t2v = p_[:], ph[:], t1[ci][:], t2[ci][:]
                                k.op("dve", lambda e, pv=pv, t1v=t1v, cv=cv: e.tensor_tensor(out=t1v, in0=pv, in1=cv, op=ALU.mult), r=[p_, cosS, qb[ci]], w=[t1[ci]])
                                k.op("dve", lambda e, phv=phv, t2v=t2v, sv=sv: e.tensor_tensor(out=t2v, in0=phv, in1=sv, op=ALU.mult), r=[ph, sinS], w=[t2[ci]])
                                k.op("pool", lambda e, ci=ci: e.tensor_tensor(out=qo[ci][:], in0=t1[ci][:], in1=t2[ci][:], op=ALU.add), r=[t1[ci], t2[ci]], w=[qo[ci]])
                                if d == 1:
                                    dap = dst[head, :, off + seg * nseg + 512 * j: off + seg * nseg + 512 * j + 512]
                                    sap = qo[ci][:]
                                elif d == 4:
                                    dap = dst[head, :, off + j * L + seg * nseg: off + j * L + seg * nseg + 512]
                                    sap = qo[ci][:]
                                else:
                                    base = dst[head, :, off:off + T].rearrange("p (r i) -> p r i", r=16)
                                    dap = base[:, 4 * j:4 * j + 4, seg * nseg:seg * nseg + 128]
                                    sap = qo[ci][:].rearrange("p (r i) -> p r i", r=4)
                                k.dma(dap, sap, r=[qo[ci]])
                            if pend[0] is not None:
                                pend[0]()
                            pend[0] = post
                elif kind == "v":
                    g = idx
                    d = DIL[g]
                    L = T // d
                    nseg = SEG // d
                    for J in range(16):
                        p_ = nextpm()
                        ci = cnt["v"] % 2
                        cnt["v"] += 1
                        for kc in range(KC):
                            lhs = perm_blk(hT[:, kc, 2:SEG + 2], d, J)
                            k.op("pe", lambda e, p_=p_, w_=w_, kc=kc, lhs=lhs: e.matmul(p_[:], lhs, w_[:, kc, :], start=(kc == 0), stop=(kc == KC - 1)),
                                 r=[hT, w_], w=[p_])
                        k.op("act", lambda e, p_=p_, ci=ci: e.copy(out=vo[ci][:], in_=p_[:]), r=[p_], w=[vo[ci]])
                        r = (128 * J) // nseg
                        i0 = (128 * J) % nseg
                        row0 = 64 + r * L + seg * nseg + i0
                        k.dma(S["V"][g, row0:row0 + 128, :], vo[ci][:], r=[vo[ci]])
                elif kind == "z":
                    for tt in range(16):
                        p_ = nextpm()
                        ci = cnt["z"] % 2
                        cnt["z"] += 1
                        for kc in range(KC):
                            k.op("pe", lambda e, p_=p_, w_=w_, kc=kc, tt=tt: e.matmul(p_[:], hT[:, kc, 2 + 128 * tt:2 + 128 * tt + 128], w_[:, kc, :], start=(kc == 0), stop=(kc == KC - 1)),
                                 r=[hT, w_], w=[p_])
                        k.op("act", lambda e, p_=p_, ci=ci: e.activation(out=zo[ci][:], in_=p_[:], func=AF.Silu), r=[p_], w=[zo[ci]])
                        k.dma(S["zs"][tb + 128 * tt:tb + 128 * tt + 128, 512 * idx:512 * idx + 512], zo[ci][:], r=[zo[ci]])
                elif kind == "xbc":
                    for cc in range(4):
                        cg = 4 * idx + cc
                        ps_ = []
                        for j in range(4):
                            p_ = nextpm()
                            ps_.append(p_)
                            for kc in range(KC):
                                k.op("pe", lambda e, p_=p_, w_=w_, kc=kc, cc=cc, j=j: e.matmul(p_[:], w_[:, kc, cc * 128:(cc + 1) * 128], hT[:, kc, 2 + 512 * j:2 + 512 * j + 512], start=(kc == 0), stop=(kc == KC - 1)),
                                     r=[hT, w_], w=[p_])
                        for kc in range(KC):
                            rhs = bass.AP(hT.t, kc * (SEG + 4), [[KC * (SEG + 4), 128], [SEG + 2, 2], [1, 2]])
                            k.op("pe", lambda e, w_=w_, kc=kc, cc=cc, rhs=rhs: e.matmul(ph[:, 0:4], w_[:, kc, cc * 128:(cc + 1) * 128], rhs, start=(kc == 0), stop=(kc == KC - 1)),
                                 r=[hT, w_], w=[ph])
                        for j in range(4):
                            k.op("act", lambda e, p_=ps_[j], j=j: e.copy(out=E[:, 2 + 512 * j:2 + 512 * j + 512], in_=p_[:]), r=[ps_[j]], w=[E])
                        k.op("act", lambda e: e.copy(out=E[:, 0:2], in_=ph[:, 0:2]), r=[ph], w=[E])
                        k.op("act", lambda e: e.copy(out=E[:, SEG + 2:SEG + 4], in_=ph[:, 2:4]), r=[ph], w=[E])
                        k.op("dve", lambda e, cg=cg: e.tensor_scalar(out=a1[:], in0=E[:, 0:SEG], scalar1=cw[:, cg, 0:1], scalar2=cb[:, cg:cg + 1], op0=ALU.mult, op1=ALU.add), r=[E, cw, cb], w=[a1])
                        k.op("dve", lambda e, cg=cg: e.scalar_tensor_tensor(out=a1[:], in0=E[:, 1:SEG + 1], scalar=cw[:, cg, 1:2], in1=a1[:], op0=ALU.mult, op1=ALU.add), r=[E, cw, a1], w=[a1])
                        k.op("dve", lambda e, cg=cg: e.scalar_tensor_tensor(out=a1[:], in0=E[:, 2:SEG + 2], scalar=cw[:, cg, 2:3], in1=a1[:], op0=ALU.mult, op1=ALU.add), r=[E, cw, a1], w=[a1])
                        k.op("dve", lambda e, cg=cg: e.scalar_tensor_tensor(out=a1[:], in0=E[:, 3:SEG + 3], scalar=cw[:, cg, 3:4], in1=a1[:], op0=ALU.mult, op1=ALU.add), r=[E, cw, a1], w=[a1])
                        k.op("dve", lambda e, cg=cg: e.scalar_tensor_tensor(out=a2[:], in0=E[:, 4:SEG + 4], scalar=cw[:, cg, 4:5], in1=a1[:], op0=ALU.mult, op1=ALU.add), r=[E, cw, a1], w=[a2])
                        ci = cnt["x"] % 2
                        cnt["x"] += 1
                        x_ = xo[ci]
                        k.op("act", lambda e, x_=x_: e.activation(out=x_[:], in_=a2[:], func=AF.Silu), r=[a2], w=[x_])
                        if cg >= 32:
                            gi = (cg - 32) % 8
                            dst = S["BT"] if cg < 40 else S["CT"]
                            k.dma(dst[gi, :, tb:tb + SEG], x_[:], r=[x_])

                        def postx(cg=cg, cc=cc, x_=x_, idx=idx, tb=tb):
                            if cg < 40:
                                for half in range(2):
                                    for t8 in range(8):
                                        tt = half * 8 + t8
                                        k.op("pe", lambda e, x_=x_, tt=tt, t8=t8, half=half: e.transpose(out=ptb[half][:, t8 * 128:(t8 + 1) * 128], in_=x_[:, tt * 128:(tt + 1) * 128], identity=ident),
                                             r=[x_, cst], w=[ptb[half]])
                                    outv = xts[:, half * 8:half * 8 + 8, cc * 128:(cc + 1) * 128]
                                    inv = ptb[half][:].rearrange("p (t c) -> p t c", c=128)
                                    if half == 0:
                                        k.op("act", lambda e, outv=outv, inv=inv: e.copy(out=outv, in_=inv), r=[ptb[half]], w=[xts])
                                    else:
                                        k.op("dve", lambda e, outv=outv, inv=inv: e.tensor_copy(out=outv, in_=inv), r=[ptb[half]], w=[xts])
                            if cc == 3:
                                if idx < 8:
                                    dv = S["xtok"][tb:tb + SEG, 512 * idx:512 * idx + 512].rearrange("(t p) c -> p t c", p=128)
                                    k.dma(dv, xts[:], r=[xts])
                                elif idx < 10:
                                    dv = S["Btok"][tb:tb + SEG, 512 * (idx - 8):512 * (idx - 8) + 512].rearrange("(t p) c -> p t c", p=128)
                                    k.dma(dv, xts[:], r=[xts])
                        if pend[0] is not None:
                            pend[0]()
                        pend[0] = postx
                elif kind == "dt":
                    for tt in range(16):
                        p_ = nextpm()
                        for kc in range(KC):
                            k.op("pe", lambda e, p_=p_, w_=w_, kc=kc, tt=tt: e.matmul(p_[:, 0:128], hT[:, kc, 2 + 128 * tt:2 + 128 * tt + 128], w_[:, kc, 0:128], start=(kc == 0), stop=(kc == KC - 1)),
                                 r=[hT, w_], w=[p_])
                        xb, ab, eb, ob = ds
                        k.op("dve", lambda e, p_=p_: e.tensor_tensor(out=xb[:], in0=p_[:, 0:128], in1=dtb[:], op=ALU.add), r=[p_, dtb], w=[xb])
                        k.op("act", lambda e: e.activation(out=ab[:], in_=xb[:], func=AF.Abs), r=[xb], w=[ab])
                        k.op("act", lambda e: e.activation(out=eb[:], in_=ab[:], func=AF.Exp, scale=-1.0), r=[ab], w=[eb])
                        k.op("act", lambda e: e.activation(out=eb[:], in_=eb[:], func=AF.Ln, bias=1.0), r=[eb], w=[eb])
                        k.op("dve", lambda e: e.scalar_tensor_tensor(out=ob[:], in0=xb[:], scalar=0.0, in1=eb[:], op0=ALU.max, op1=ALU.add), r=[xb, eb], w=[ob])
                        k.dma(S["dts"][tb + 128 * tt:tb + 128 * tt + 128, :], ob[:], r=[ob])
                elif kind == "g":
                    which, i4 = idx
                    for cc in range(4):
                        ch = 4 * i4 + cc
                        go = a1 if (cc % 2 == 0) else a2
                        for j in range(4):
                            p_ = nextpm()
                            for kc in range(KC):
                                k.op("pe", lambda e, p_=p_, w_=w_, kc=kc, cc=cc, j=j: e.matmul(p_[:], w_[:, kc, cc * 128:(cc + 1) * 128], hT[:, kc, 2 + 512 * j:2 + 512 * j + 512], start=(kc == 0), stop=(kc == KC - 1)),
                                     r=[hT, w_], w=[p_])
                            k.op("act", lambda e, p_=p_, go=go, j=j: e.activation(out=go[:, 512 * j:512 * j + 512], in_=p_[:], func=AF.Sigmoid), r=[p_], w=[go])
                        k.dma(S["gT"][which, ch, :, tb:tb + SEG], go[:], r=[go])
        if pend[0] is not None:
            pend[0]()
            pend[0] = None
        k.end_phase()


def phase2(k, nc, I, S, NSEG):
    T = NSEG * SEG
    NB = T // 128
    scale = 128.0 ** -0.5
    with ExitStack() as pes:
        k.pes = pes
        cst = k.sb("cst2", [128, 128], BF16)
        band = k.sb("band2", [128, 256], BF16)
        kbt = k.sb("kbt2", [1, 4, 256], BF16)
        ones = k.sb("ones2", [1, 128], BF16)
        qTt = [k.sb(f"qTt{i}", [128, T], BF16) for i in range(2)]
        kTt = [k.sb(f"kTt{i}", [128, T + 128], BF16) for i in range(2)]
        Vt = [k.sb(f"Vt{i}", [128, NB + 1, 128], BF16) for i in range(2)]
        sc = [k.ps(f"sc2{i}", [128, 512], F32) for i in range(2)]
        pT = [k.ps(f"pT2{i}", [128, 1024], BF16) for i in range(2)]
        ov = [k.ps(f"ov2{i}", [128, 512], F32) for i in range(2)]
        Pb = [k.sb(f"Pb2{i}", [128, 256], BF16) for i in range(2)]
        PT = [k.sb(f"PT2{i}", [128, 2, 128], BF16) for i in range(2)]
        mx = [k.sb(f"mx2{i}", [128, 1], F32) for i in range(2)]
        nb = [k.sb(f"nb2{i}", [128, 1], F32) for i in range(2)]
        den = [k.sb(f"den2{i}", [128, 1], F32) for i in range(2)]
        rden = [k.sb(f"rden2{i}", [128, 1], F32) for i in range(2)]
        aos = [k.sb(f"aos2{i}", [128, 132], F32) for i in range(4)]
        k.dma(cst[:], S["cstb"][:, 0, :], w=[cst])
        k.dma(band[:], S["bandb"][:, :], w=[band])
        k.dma(kbt[:], S["kbb"][:, :].rearrange("(o r) c -> o r c", o=1), w=[kbt])
        k.op("pool", lambda e: e.memset(ones[:], 1.0), w=[ones])
        for a in aos:
            k.op("pool", lambda e, a=a: e.memset(a[:], 0.0), w=[a])

        def load_head(head, bi):
            g = head // 4
            hh = head % 4
            k.dma(qTt[bi][:], S["qT"][head, :, :], w=[qTt[bi]])
            k.dma(kTt[bi][:], S["kT"][head, :, :], w=[kTt[bi]])
            k.dma(Vt[bi][:], S["V"][g, :, hh * 128:(hh + 1) * 128].rearrange("(j p) e -> p j e", p=128), w=[Vt[bi]])

        def stageA(head, J, bi, i):
            g = head // 4
            d = DIL[g]
            L = T // d
            nbr = L // 128
            nbs = max(1, (SEG // d) // 128)
            b = J % nbr
            rows = []
            if b == 0:
                rows.append(2)
            elif b % nbs == 0:
                rows.append(0)
            if b == nbr - 1:
                rows.append(3)
            elif b % nbs == nbs - 1:
                rows.append(1)
            s_ = sc[i % 2]
            q_, k_ = qTt[bi], kTt[bi]
            k.op("pe", lambda e: e.matmul(s_[:, 0:256], q_[:, 128 * J:128 * J + 128], k_[:, 128 * J:128 * J + 256], start=True, stop=False), r=[q_, k_], w=[s_])
            k.op("pe", lambda e: e.matmul(s_[:, 0:256], cst[:], band[:], start=False, stop=(len(rows) == 0)), r=[cst, band], w=[s_])
            for ri, row in enumerate(rows):
                k.op("pe", lambda e, row=row, ri=ri: e.matmul(s_[:, 0:256], ones[0:1, :], kbt[0:1, row, :], start=False, stop=(ri == len(rows) - 1)), r=[ones, kbt], w=[s_])
            m_, n_, d_, p_ = mx[i % 2], nb[i % 2], den[i % 2], Pb[i % 2]
            k.op("dve", lambda e: e.reduce_max(out=m_[:], in_=s_[:, 0:256], axis=AX.X), r=[s_], w=[m_])
            k.op("dve", lambda e: e.tensor_scalar(out=n_[:], in0=m_[:], scalar1=-scale, scalar2=None, op0=ALU.mult), r=[m_], w=[n_])
            k.op("act", lambda e: e.activation(out=p_[:], in_=s_[:, 0:256], func=AF.Exp, bias=n_[:], scale=scale, accum_out=d_[:]), r=[s_, n_], w=[p_, d_])

        def stageB(head, J, bi, i):
            g = head // 4
            hh = head % 4
            d = DIL[g]
            L = T // d
            p_, t_, o_, P_ = Pb[i % 2], pT[i % 2], ov[i % 2], PT[i % 2]
            m_, d_, r_ = mx[i % 2], den[i % 2], rden[i % 2]
            a_ = aos[i % 4]
            v_ = Vt[bi]
            for half in range(2):
                k.op("pe", lambda e, half=half: e.transpose(out=t_[:, half * 128:(half + 1) * 128], in_=p_[:, half * 128:(half + 1) * 128], identity=cst[:]), r=[p_, cst], w=[t_])
            k.op("act", lambda e: e.copy(out=P_[:], in_=t_[:, 0:256].rearrange("p (h q) -> p h q", h=2)), r=[t_], w=[P_])
            for half in range(2):
                k.op("pe", lambda e, half=half: e.matmul(o_[:, 0:128], P_[:, half, :], v_[:, J + half, :], start=(half == 0), stop=(half == 1)), r=[P_, v_], w=[o_])
            k.op("dve", lambda e: e.reciprocal(out=r_[:], in_=d_[:]), r=[d_], w=[r_])
            k.op("dve", lambda e: e.tensor_scalar(out=a_[:, 0:128], in0=o_[:, 0:128], scalar1=r_[:], scalar2=None, op0=ALU.mult), r=[o_, r_], w=[a_])
            k.op("dve", lambda e: e.tensor_scalar(out=a_[:, 128:129], in0=m_[:], scalar1=scale, scalar2=None, op0=ALU.mult), r=[m_], w=[a_])
            k.op("dve", lambda e: e.tensor_copy(out=a_[:, 129:130], in_=d_[:]), r=[d_], w=[a_])
            r = (128 * J) // L
            I0 = (128 * J) % L
            t0 = r + d * I0
            dst = bass.AP(S["ao"].tensor, ((g * T + t0) * 4 + hh) * 132, [[d * 4 * 132, 128], [1, 132]])
            k.dma(dst, a_[:], r=[a_])

        i = 0
        load_head(0, 0)
        for head in range(12):
            bi = head % 2
            if head + 1 < 12:
                load_head(head + 1, (head + 1) % 2)
            stageA(head, 0, bi, i)
            for J in range(NB):
                if J + 1 < NB:
                    stageA(head, J + 1, bi, i + 1)
                stageB(head, J, bi, i)
                i += 1
        k.end_phase()


def phase2b(k, nc, I, S, NSEG):
    T = NSEG * SEG
    NB = T // 128
    with ExitStack() as pes:
        k.pes = pes
        cst = k.sb("cst2b", [128, 128], BF16)
        ao = [k.sb(f"ao2b{i}", [128, 3, 4, 132], F32) for i in range(2)]
        ls = k.sb("ls2b", [128, 3, 4], F32)
        mm_ = k.sb("mm2b", [128, 4], F32)
        ex = k.sb("ex2b", [128, 3, 4], F32)
        sm = k.sb("sm2b", [128, 4], F32)
        acc = k.sb("acc2b", [128, 4, 128], F32)
        ab = [k.sb(f"ab2b{i}", [128, 512], BF16) for i in range(2)]
        pt = [k.ps(f"pt2b{i}", [128, 1024], BF16) for i in range(2)]
        st = [k.sb(f"st2b{i}", [128, 4, 512], BF16) for i in range(2)]
        k.dma(cst[:], S["cstb"][:, 0, :], w=[cst])
        aov = S["ao"]
        for i in range(NB):
            a_ = ao[i % 2]
            for g in range(3):
                k.dma(a_[:, g, :, :], aov[g, i * 128:(i + 1) * 128, :, :], w=[a_])
            k.op("act", lambda e, a_=a_: e.activation(out=ls[:], in_=a_[:, :, :, 129], func=AF.Ln), r=[a_], w=[ls])
            k.op("dve", lambda e, a_=a_: e.tensor_tensor(out=ls[:], in0=ls[:], in1=a_[:, :, :, 128], op=ALU.add), r=[ls, a_], w=[ls])
            k.op("dve", lambda e: e.tensor_tensor(out=mm_[:], in0=ls[:, 0, :], in1=ls[:, 1, :], op=ALU.max), r=[ls], w=[mm_])
            k.op("dve", lambda e: e.tensor_tensor(out=mm_[:], in0=mm_[:], in1=ls[:, 2, :], op=ALU.max), r=[ls, mm_], w=[mm_])
            for g in range(3):
                k.op("dve", lambda e, g=g: e.tensor_tensor(out=ex[:, g, :], in0=ls[:, g, :], in1=mm_[:], op=ALU.subtract), r=[ls, mm_], w=[ex])
            k.op("act", lambda e: e.activation(out=ex[:], in_=ex[:], func=AF.Exp), r=[ex], w=[ex])
            k.op("dve", lambda e: e.tensor_tensor(out=sm[:], in0=ex[:, 0, :], in1=ex[:, 1, :], op=ALU.add), r=[ex], w=[sm])
            k.op("dve", lambda e: e.tensor_tensor(out=sm[:], in0=sm[:], in1=ex[:, 2, :], op=ALU.add), r=[ex, sm], w=[sm])
            k.op("dve", lambda e: e.reciprocal(out=sm[:], in_=sm[:]), r=[sm], w=[sm])
            for g in range(3):
                k.op("dve", lambda e, g=g: e.tensor_tensor(out=ex[:, g, :], in0=ex[:, g, :], in1=sm[:], op=ALU.mult), r=[ex, sm], w=[ex])
            b_ = ab[i % 2]
            for h in range(4):
                k.op("dve", lambda e, h=h, a_=a_: e.tensor_scalar(out=acc[:, h, :], in0=a_[:, 0, h, 0:128], scalar1=ex[:, 0, h:h + 1], scalar2=None, op0=ALU.mult), r=[a_, ex], w=[acc])
                k.op("dve", lambda e, h=h, a_=a_: e.scalar_tensor_tensor(out=acc[:, h, :], in0=a_[:, 1, h, 0:128], scalar=ex[:, 1, h:h + 1], in1=acc[:, h, :], op0=ALU.mult, op1=ALU.add), r=[a_, ex, acc], w=[acc])
                k.op("dve", lambda e, h=h, a_=a_, b_=b_: e.scalar_tensor_tensor(out=b_[:, h * 128:(h + 1) * 128], in0=a_[:, 2, h, 0:128], scalar=ex[:, 2, h:h + 1], in1=acc[:, h, :], op0=ALU.mult, op1=ALU.add), r=[a_, ex, acc], w=[b_])
            p_ = pt[i % 2]
            for h in range(4):
                k.op("pe", lambda e, h=h, b_=b_, p_=p_: e.transpose(out=p_[:, h * 128:(h + 1) * 128], in_=b_[:, h * 128:(h + 1) * 128], identity=cst[:]), r=[b_, cst], w=[p_])
            s_ = st[(i // 4) % 2]
            t4 = i % 4
            k.op("act", lambda e, p_=p_, s_=s_, t4=t4: e.copy(out=s_[:, :, t4 * 128:(t4 + 1) * 128], in_=p_[:, 0:512].rearrange("p (h q) -> p h q", h=4)), r=[p_], w=[s_])
            if t4 == 3:
                t0 = (i - 3) * 128
                k.dma(S["attnT"].rearrange("c p t -> p c t")[:, :, t0:t0 + 512], s_[:], r=[s_])
        k.end_phase()


def ssd_sweep(k, nc, I, S, NSEG, dirn):
    T = NSEG * SEG
    NCH = T // 128
    tg = "f" if dirn == 0 else "b"
    iU, iL, iN = (2, 3, 4) if dirn == 0 else (5, 6, 7)
    with ExitStack() as pes:
        k.pes = pes
        cst = k.sb("cstS" + tg, [128, 4, 128], BF16)
        neg4 = k.sb("neg4" + tg, [128, 4, 128], BF16)
        ones = k.sb("onesS" + tg, [128, 128], BF16)
        flg = k.sb("flgS" + tg, [128, NF], F32)
        Abc = k.sb("Abc" + tg, [128, 128], F32)
        epst = k.sb("epsS" + tg, [128, 1], F32)
        xt = [k.sb(f"xtS{tg}{i}", [128, 64, 64], BF16) for i in range(2)]
        Bt = [k.sb(f"BtS{tg}{i}", [128, 1024], BF16) for i in range(2)]
        BTt = [k.sb(f"BTtS{tg}{i}", [128, 8, 128], BF16) for i in range(2)]
        CTt = [k.sb(f"CTtS{tg}{i}", [128, 8, 128], BF16) for i in range(2)]
        dtt = [k.sb(f"dttS{tg}{i}", [128, 64], F32) for i in range(2)]
        dtA = k.sb("dtA" + tg, [128, 64], F32)
        hi = [k.sb(f"hi{tg}{i}", [128, 64], BF16) for i in range(2)]
        lo = [k.sb(f"lo{tg}{i}", [128, 64], BF16) for i in range(2)]
        Rhi = [k.sb(f"Rhi{tg}{i}", [128, 64, 128], BF16) for i in range(2)]
        Etok = [k.sb(f"Etok{tg}{i}", [128, 64], F32) for i in range(2)]
        acs = k.sb("acs" + tg, [128, 64], F32)
        Ttot = [k.sb(f"Ttot{tg}{i}", [128, 64], F32) for i in range(2)]
        ELp = [k.sb(f"ELp{tg}{i}", [128, 64], F32) for i in range(2)]
        ELm = k.sb("ELm" + tg, [128, 64], F32)
        dec = k.sb("dec" + tg, [128, 64], F32)
        dtm = k.sb("dtm" + tg, [128, 64], F32)
        dd = k.sb("dd" + tg, [128, 64], F32)
        x1 = [k.sb(f"x1{tg}{i}", [128, 64, 64], BF16) for i in range(2)]
        x2 = [k.sb(f"x2{tg}{i}", [128, 64, 64], BF16) for i in range(2)]
        cbs = [k.sb(f"cbs{tg}{i}", [128, 8, 128], F32) for i in range(2)]
        dm = [k.sb(f"dm{tg}{i}", [128, 4, 128], F32) for i in range(2)]
        W = [k.sb(f"W{tg}{i}", [128, 4, 128], BF16) for i in range(2)]
        st = k.sb("st" + tg, [128, 64, 64], F32)
        stb = k.sb("stb" + tg, [128, 64, 64], BF16)
        tmp = k.sb("tmpS" + tg, [128, 8, 64], F32)
        tmp2 = k.sb("tmp2S" + tg, [128, 8, 64], F32)
        yo = [k.sb(f"yo{tg}{i}", [128, 8, 64], F32) for i in range(2)]
        pa = k.ps("pa" + tg, [128, 512], F32)
        pcb = [k.ps(f"pcb{tg}{i}", [128, 512], F32) for i in range(2)]
        pd = [k.ps(f"pd{tg}{i}", [128, 512], F32) for i in range(2)]
        pyi = k.ps("pyi" + tg, [128, 512], F32)
        pss = k.ps("pss" + tg, [128, 512], F32)
        if dirn == 1:
            ptb = k.ps("ptbS", [128, 1024], BF16)
            Dbc = k.sb("DbcS", [128, 64], F32)
            nwg = [k.sb(f"nwgS{i}", [128, 512], F32) for i in range(2)]
            yfg = [k.sb(f"yfgS{i}", [128, 8, 64], F32) for i in range(2)]
            zsg = [k.sb(f"zsgS{i}", [128, 8, 64], F32) for i in range(2)]
            tD = k.sb("tDS", [128, 8, 64], F32)
            ssq = k.sb("ssqS", [128, 1], F32)
            rsd = k.sb("rsdS", [128, 1], F32)
            junk = k.sb("junkS", [128, 512], F32)
            ynb = k.sb("ynbS", [128, 4096], BF16)
            ynT = k.sb("ynTS", [128, 32, 128], BF16)
            k.dma(Dbc[:], I["ssd_d"][0:1, :].partition_broadcast(128), w=[Dbc])
        k.dma(cst[:, 0, :], S["cstb"][:, 0, :], w=[cst])
        k.dma(cst[:, 1, :], S["cstb"][:, iU, :], w=[cst])
        k.dma(cst[:, 2, :], S["cstb"][:, iL, :], w=[cst])
        for r4 in range(4):
            k.dma(neg4[:, r4, :], S["cstb"][:, iN, :], w=[neg4])
        k.dma(flg[:], I["flags"][0:1, :].partition_broadcast(128), w=[flg])
        k.dma(Abc[:], I["a_log"][0:1, :].partition_broadcast(128), w=[Abc])
        k.op("pool", lambda e: e.memset(ones[:], 1.0), w=[ones])
        k.op("pool", lambda e: e.memset(epst[:], 1e-6), w=[epst])
        k.op("act", lambda e: e.activation(out=Abc[:], in_=Abc[:], func=AF.Exp), r=[Abc], w=[Abc])
        k.op("dve", lambda e: e.tensor_scalar(out=Abc[:], in0=Abc[:], scalar1=-1.0, scalar2=None, op0=ALU.mult), r=[Abc], w=[Abc])
        ident = cst[:, 0, :]
        Um = cst[:, 1, :]
        Lm = cst[:, 2, :]
        k.op("dve", lambda e: e.memset(st[:], 0.0), w=[st])
        k.op("pool", lambda e: e.memset(stb[:], 0.0), w=[stb])

        order = list(range(NCH)) if dirn == 0 else list(range(NCH - 1, -1, -1))

        def load(c, bi):
            t0 = c * 128
            k.dma(xt[bi][:], S["xtok"][t0:t0 + 128, :].rearrange("p (h q) -> p h q", q=64), w=[xt[bi]])
            k.dma(Bt[bi][:], S["Btok"][t0:t0 + 128, :], w=[Bt[bi]])
            k.dma(BTt[bi][:], S["BT"][:, :, t0:t0 + 128].rearrange("g n t -> n g t"), w=[BTt[bi]])
            k.dma(CTt[bi][:], S["CT"][:, :, t0:t0 + 128].rearrange("g n t -> n g t"), w=[CTt[bi]])
            k.dma(dtt[bi][:], S["dts"][t0:t0 + 128, dirn * 64:(dirn + 1) * 64], w=[dtt[bi]])

        def prep(ci):
            c = order[ci]
            b2 = ci % 2
            x_, BT_, CT_, dt_ = xt[b2], BTt[b2], CTt[b2], dtt[b2]
            hi_, lo_, R_, E_, T_, P_ = hi[b2], lo[b2], Rhi[b2], Etok[b2], Ttot[b2], ELp[b2]
            k.op("dve", lambda e: e.tensor_tensor(out=dtA[:], in0=dt_[:], in1=Abc[:, dirn * 64:(dirn + 1) * 64], op=ALU.mult), r=[dt_, Abc], w=[dtA])
            k.op("dve", lambda e: e.tensor_copy(out=hi_[:], in_=dtA[:]), r=[dtA], w=[hi_])
            k.op("dve", lambda e: e.tensor_tensor(out=lo_[:], in0=dtA[:], in1=hi_[:], op=ALU.subtract), r=[dtA, hi_], w=[lo_])
            Ub = bass.AP(cst.t, 128, [[512, 128], [0, 64], [1, 128]])
            hib = bass.AP(hi_.t, 0, [[64, 128], [1, 64], [0, 128]])
            k.op("pool", lambda e: e.tensor_tensor(out=R_[:], in0=Ub, in1=hib, op=ALU.mult), r=[cst, hi_], w=[R_])
            k.op("pe", lambda e: e.matmul(pa[:, 0:64], Um, hi_[:], start=True, stop=False), r=[cst, hi_], w=[pa])
            k.op("pe", lambda e: e.matmul(pa[:, 0:64], Um, lo_[:], start=False, stop=True), r=[cst, lo_], w=[pa])
            k.op("pe", lambda e: e.matmul(pa[:, 64:128], ones[:], hi_[:], start=True, stop=False), r=[ones, hi_], w=[pa])
            k.op("pe", lambda e: e.matmul(pa[:, 64:128], ones[:], lo_[:], start=False, stop=True), r=[ones, lo_], w=[pa])
            k.op("pe", lambda e: e.matmul(pa[:, 128:192], Um, lo_[:], start=True, stop=True), r=[cst, lo_], w=[pa])
            k.op("act", lambda e: e.activation(out=E_[:], in_=pa[:, 0:64], func=AF.Exp), r=[pa], w=[E_])
            k.op("act", lambda e: e.copy(out=acs[:], in_=pa[:, 0:64]), r=[pa], w=[acs])
            k.op("act", lambda e: e.activation(out=T_[:], in_=pa[:, 64:128], func=AF.Exp), r=[pa], w=[T_])
            k.op("act", lambda e: e.activation(out=P_[:], in_=pa[:, 128:192], func=AF.Exp), r=[pa], w=[P_])
            k.op("act", lambda e: e.activation(out=ELm[:], in_=pa[:, 128:192], func=AF.Exp, scale=-1.0), r=[pa], w=[ELm])
            k.op("dve", lambda e: e.tensor_tensor(out=dec[:], in0=pa[:, 64:128], in1=acs[:], op=ALU.subtract), r=[pa, acs], w=[dec])
            k.op("act", lambda e: e.activation(out=dec[:], in_=dec[:], func=AF.Exp), r=[dec], w=[dec])
            k.op("dve", lambda e: e.tensor_tensor(out=dtm[:], in0=dt_[:], in1=ELm[:], op=ALU.mult), r=[dt_, ELm], w=[dtm])
            k.op("dve", lambda e: e.tensor_tensor(out=dd[:], in0=dt_[:], in1=dec[:], op=ALU.mult), r=[dt_, dec], w=[dd])
            dtmb = bass.AP(dtm.t, 0, [[64, 128], [1, 64], [0, 64]])
            ddb = bass.AP(dd.t, 0, [[64, 128], [1, 64], [0, 64]])
            k.op("pool", lambda e: e.tensor_tensor(out=x1[b2][:], in0=x_[:], in1=dtmb, op=ALU.mult), r=[x_, dtm], w=[x1[b2]])
            k.op("pool", lambda e: e.tensor_tensor(out=x2[b2][:], in0=x_[:], in1=ddb, op=ALU.mult), r=[x_, dd], w=[x2[b2]])
            for g in range(8):
                p_ = pcb[g // 4]
                k.op("pe", lambda e, p_=p_, g=g: e.matmul(p_[:, (g % 4) * 128:(g % 4 + 1) * 128], BT_[:, g, :], CT_[:, g, :], start=True, stop=True), r=[BT_, CT_], w=[p_])
            Um4 = bass.AP(cst.t, 128, [[512, 128], [0, 4], [1, 128]])
            for h2 in range(2):
                k.op("dve", lambda e, h2=h2: e.tensor_tensor(out=cbs[b2][:, 4 * h2:4 * h2 + 4, :], in0=pcb[h2][:].rearrange("p (g t) -> p g t", g=4), in1=Um4, op=ALU.mult), r=[pcb[h2], cst], w=[cbs[b2]])

        load(order[0], 0)
        if NCH > 1:
            load(order[1], 1)
        prep(0)
        gcount = 0
        for ci, c in enumerate(order):
            bi = ci % 2
            t0 = c * 128
            if ci + 1 < NCH:
                prep(ci + 1)
            x_, B_, BT_, CT_, dt_ = xt[bi], Bt[bi], BTt[bi], CTt[bi], dtt[bi]
            R_, E_, T_, P_, x1_, x2_, cbs_ = Rhi[bi], Etok[bi], Ttot[bi], ELp[bi], x1[bi], x2[bi], cbs[bi]
            seg = c // 16
            bnd = (c % 16 == 0) if dirn == 0 else (c % 16 == 15)
            if bnd and ci > 0:
                fcol = 2 * seg + dirn
                k.op("dve", lambda e, fcol=fcol: e.tensor_scalar(out=st[:], in0=st[:], scalar1=flg[:, fcol:fcol + 1], scalar2=None, op0=ALU.mult), r=[st, flg], w=[st])
                k.op("act", lambda e: e.copy(out=stb[:], in_=st[:]), r=[st], w=[stb])
            for g in range(8):
                for q2 in range(2):
                    hq = 2 * g + q2
                    p_ = pd[hq % 2]
                    d_ = dm[hq % 2]
                    w_ = W[hq % 2]
                    k.op("pe", lambda e, p_=p_, hq=hq, R_=R_: e.matmul(p_[:], Lm, R_[:, 4 * hq:4 * hq + 4, :], start=True, stop=True), r=[cst, R_], w=[p_])
                    k.op("act", lambda e, p_=p_, d_=d_: e.activation(out=d_[:], in_=p_[:].rearrange("p (h t) -> p h t", h=4), func=AF.Exp), r=[p_], w=[d_])
                    cbb = bass.AP(cbs_.t, g * 128, [[1024, 128], [0, 4], [1, 128]])
                    k.op("dve", lambda e, d_=d_, w_=w_, cbb=cbb: e.tensor_tensor(out=w_[:], in0=d_[:], in1=cbb, op=ALU.mult), r=[d_, cbs_], w=[w_])
                    for hl in range(4):
                        h = 4 * hq + hl
                        k.op("pe", lambda e, w_=w_, hl=hl, h=h, x1_=x1_: e.matmul(pyi[:, (h % 8) * 64:(h % 8 + 1) * 64], w_[:, hl, :], x1_[:, h, :], start=True, stop=True), r=[w_, x1_], w=[pyi])
                k.op("pe", lambda e, g=g, CT_=CT_: e.matmul(pss[:], CT_[:, g, :], stb[:, 8 * g:8 * g + 8, :], start=True, stop=True), r=[CT_, stb], w=[pss])
                Eb = bass.AP(E_.t, 8 * g, [[64, 128], [1, 8], [0, 64]])
                Pb_ = bass.AP(P_.t, 8 * g, [[64, 128], [1, 8], [0, 64]])
                y_ = yo[gcount % 2]
                k.op("dve", lambda e, Eb=Eb: e.tensor_tensor(out=tmp[:], in0=pss[:].rearrange("p (h q) -> p h q", h=8), in1=Eb, op=ALU.mult), r=[pss, E_], w=[tmp])
                k.op("dve", lambda e, Pb_=Pb_: e.tensor_tensor(out=tmp2[:], in0=pyi[:].rearrange("p (h q) -> p h q", h=8), in1=Pb_, op=ALU.mult), r=[pyi, P_], w=[tmp2])
                k.op("pool", lambda e, y_=y_: e.tensor_tensor(out=y_[:], in0=tmp[:], in1=tmp2[:], op=ALU.add), r=[tmp, tmp2], w=[y_])
                k.op("pe", lambda e, g=g, B_=B_, x2_=x2_: e.matmul(pss[:], B_[:, g * 128:(g + 1) * 128], x2_[:, 8 * g:8 * g + 8, :], start=True, stop=True), r=[B_, x2_], w=[pss])
                Tb = bass.AP(T_.t, 8 * g, [[64, 128], [1, 8], [0, 64]])
                stg = st[:, 8 * g:8 * g + 8, :]
                k.op("pool", lambda e, stg=stg, Tb=Tb: e.tensor_tensor(out=stg, in0=stg, in1=Tb, op=ALU.mult), r=[st, T_, stb], w=[st])
                k.op("dve", lambda e, stg=stg: e.tensor_tensor(out=stg, in0=stg, in1=pss[:].rearrange("p (h q) -> p h q", h=8), op=ALU.add), r=[st, pss], w=[st])
                k.op("act", lambda e, g=g, stg=stg: e.copy(out=stb[:, 8 * g:8 * g + 8, :], in_=stg), r=[st], w=[stb])
                if dirn == 0:
                    k.dma(S["yf"][t0:t0 + 128, g * 512:(g + 1) * 512].rearrange("p (h q) -> p h q", h=8), y_[:], r=[y_])
                else:
                    f_, z_ = yfg[gcount % 2], zsg[gcount % 2]
                    k.dma(f_[:], S["yf"][t0:t0 + 128, g * 512:(g + 1) * 512].rearrange("p (h q) -> p h q", h=8), w=[f_])
                    k.dma(z_[:], S["zs"][t0:t0 + 128, g * 512:(g + 1) * 512].rearrange("p (h q) -> p h q", h=8), w=[z_])
                    n_ = nwg[gcount % 2]
                    k.dma(n_[:], I["ssd_norm_w"][0:1, g * 512:(g + 1) * 512].partition_broadcast(128), w=[n_])
                    Db = bass.AP(Dbc.t, 8 * g, [[64, 128], [1, 8], [0, 64]])
                    k.op("pool", lambda e, x_=x_, g=g, Db=Db: e.tensor_tensor(out=tD[:], in0=x_[:, 8 * g:8 * g + 8, :], in1=Db, op=ALU.mult), r=[x_, Dbc], w=[tD])
                    k.op("pool", lambda e, f_=f_: e.tensor_tensor(out=tD[:], in0=tD[:], in1=f_[:], op=ALU.add), r=[tD, f_], w=[tD])
                    k.op("dve", lambda e, y_=y_: e.tensor_tensor(out=y_[:], in0=y_[:], in1=tD[:], op=ALU.add), r=[y_, tD], w=[y_])
                    k.op("dve", lambda e, y_=y_, z_=z_: e.tensor_tensor(out=y_[:], in0=y_[:], in1=z_[:], op=ALU.mult), r=[y_, z_], w=[y_])
                    yf2 = y_[:].rearrange("p h q -> p (h q)")
                    k.op("dve", lambda e, yf2=yf2: e.scalar_tensor_tensor(out=junk[:], in0=yf2, scalar=1.0, in1=yf2, op0=ALU.mult, op1=ALU.mult, accum_out=ssq[:]), r=[y_], w=[junk, ssq])
                    k.op("act", lambda e: e.activation(out=rsd[:], in_=ssq[:], func=AF.Ln, scale=1.0 / 512, bias=epst[:]), r=[ssq, epst], w=[rsd])
                    k.op("act", lambda e: e.activation(out=rsd[:], in_=rsd[:], func=AF.Exp, scale=-0.5), r=[rsd], w=[rsd])
                    k.op("dve", lambda e, yf2=yf2, g=g, n_=n_: e.scalar_tensor_tensor(out=ynb[:, g * 512:(g + 1) * 512], in0=yf2, scalar=rsd[:], in1=n_[:], op0=ALU.mult, op1=ALU.mult), r=[y_, rsd, n_], w=[ynb])
                    for q4 in range(4):
                        kc = 4 * g + q4
                        k.op("pe", lambda e, kc=kc: e.transpose(out=ptb[:, (kc % 8) * 128:(kc % 8 + 1) * 128], in_=ynb[:, kc * 128:(kc + 1) * 128], identity=ident), r=[ynb, cst], w=[ptb])
                    if g % 2 == 1:
                        k0 = 4 * (g - 1)
                        k.op("act", lambda e, k0=k0: e.copy(out=ynT[:, k0:k0 + 8, :], in_=ptb[:].rearrange("p (c t) -> p c t", c=8)), r=[ptb], w=[ynT])
                gcount += 1
            if dirn == 1:
                k.dma(S["ynT"].rearrange("c p t -> p c t")[:, :, t0:t0 + 128], ynT[:], r=[ynT])
            if ci + 2 < NCH:
                load(order[ci + 2], bi)
        k.end_phase()


def phase5(k, nc, I, S, NSEG):
    T = NSEG * SEG
    NBK = T // 512
    with ExitStack() as pes:
        k.pes = pes
        wat = k.sb("wat5", [128, 4, D], BF16)
        wss = [k.sb(f"wss5{i}", [128, 32, 256], BF16) for i in range(2)]
        wo = [k.sb(f"wo5{i}", [128, KC, 512], BF16) for i in range(2)]
        aT = [k.sb(f"aT5{i}", [128, 4, 512], BF16) for i in range(2)]
        yT = [k.sb(f"yT5{i}", [128, 32, 512], BF16) for i in range(2)]
        ga = [k.sb(f"ga5{i}", [128, 512], F32) for i in range(2)]
        gs = [k.sb(f"gs5{i}", [128, 512], F32) for i in range(2)]
        t1 = k.sb("t15", [128, 512], F32)
        t2 = k.sb("t25", [128, 512], F32)
        mT = k.sb("mT5", [128, KC, 512], BF16)
        gm = k.sb("gm5", [128, D], F32)
        xt = [k.sb(f"xt5{i}", [128, 512], F32) for i in range(2)]
        xo = [k.sb(f"xo5{i}", [128, 512], F32) for i in range(2)]
        pa = [k.ps(f"pa5{i}", [128, 512], F32) for i in range(2)]
        pb = [k.ps(f"pb5{i}", [128, 512], F32) for i in range(2)]
        po = [k.ps(f"po5{i}", [128, 512], F32) for i in range(4)]
        k.dma(wat[:], S["f_w_attn"].rearrange("(k p) n -> p k n", p=128), w=[wat])
        aTd = S["attnT"].rearrange("c p t -> p c t")
        yTd = S["ynT"].rearrange("c p t -> p c t")
        wsd = S["f_w_ssd"]
        wod = S["f_w_out"]
        nws = 0
        nwo = 0
        cnt = 0

        def load_blk(b, bi):
            t0 = b * 512
            k.dma(aT[bi][:], aTd[:, :, t0:t0 + 512], w=[aT[bi]])
            k.dma(yT[bi][:], yTd[:, :, t0:t0 + 512], w=[yT[bi]])

        load_blk(0, 0)
        k.dma(wss[0][:], wsd[:, 0:256].rearrange("(k p) n -> p k n", p=128), w=[wss[0]])
        k.dma(wo[0][:], wtile_ap(wod, 0, 512), w=[wo[0]])
        for b in range(NBK):
            bi = b % 2
            t0 = b * 512
            seg = t0 // SEG
            if b % 4 == 0:
                k.dma(gm[:], S["modv"][seg:seg + 1, 2 * D:3 * D].partition_broadcast(128), w=[gm])
            if b + 1 < NBK:
                load_blk(b + 1, (b + 1) % 2)
            a_, y_ = aT[bi], yT[bi]
            for w8 in range(8):
                ws_ = wss[nws % 2]
                nxt = (w8 + 1) % 8
                if not (b == NBK - 1 and w8 == 7):
                    k.dma(wss[(nws + 1) % 2][:], wsd[:, 256 * nxt:256 * nxt + 256].rearrange("(k p) n -> p k n", p=128), w=[wss[(nws + 1) % 2]])
                nws += 1
                for c2 in range(2):
                    dc = 2 * w8 + c2
                    pa_, pb_ = pa[cnt % 2], pb[cnt % 2]
                    ga_, gs_ = ga[cnt % 2], gs[cnt % 2]
                    cnt += 1
                    k.dma(ga_[:], S["gT"][0, dc, :, t0:t0 + 512], w=[ga_])
                    k.dma(gs_[:], S["gT"][1, dc, :, t0:t0 + 512], w=[gs_])
                    for kc in range(4):
                        k.op("pe", lambda e, pa_=pa_, kc=kc, dc=dc, a_=a_: e.matmul(pa_[:], wat[:, kc, dc * 128:(dc + 1) * 128], a_[:, kc, :], start=(kc == 0), stop=(kc == 3)), r=[wat, a_], w=[pa_])
                    for kc in range(32):
                        k.op("pe", lambda e, pb_=pb_, kc=kc, c2=c2, ws_=ws_, y_=y_: e.matmul(pb_[:], ws_[:, kc, c2 * 128:(c2 + 1) * 128], y_[:, kc, :], start=(kc == 0), stop=(kc == 31)), r=[ws_, y_], w=[pb_])
                    k.op("dve", lambda e, pa_=pa_, ga_=ga_: e.tensor_tensor(out=t1[:], in0=pa_[:], in1=ga_[:], op=ALU.mult), r=[pa_, ga_], w=[t1])
                    k.op("dve", lambda e, pb_=pb_, gs_=gs_: e.tensor_tensor(out=t2[:], in0=pb_[:], in1=gs_[:], op=ALU.mult), r=[pb_, gs_], w=[t2])
                    k.op("pool", lambda e, dc=dc: e.tensor_tensor(out=mT[:, dc, :], in0=t1[:], in1=t2[:], op=ALU.add), r=[t1, t2], w=[mT])
            for d4 in range(4):
                wo_ = wo[nwo % 2]
                nxt = (d4 + 1) % 4
                if not (b == NBK - 1 and d4 == 3):
                    k.dma(wo[(nwo + 1) % 2][:], wtile_ap(wod, 512 * nxt, 512), w=[wo[(nwo + 1) % 2]])
                nwo += 1
                for ts in range(4):
                    p_ = po[ts]
                    x_ = xt[ts % 2]
                    o_ = xo[ts % 2]
                    r0 = t0 + ts * 128
                    k.dma(x_[:], I["x"][r0:r0 + 128, d4 * 512:(d4 + 1) * 512], w=[x_])
                    for kc in range(KC):
                        k.op("pe", lambda e, p_=p_, kc=kc, ts=ts, wo_=wo_: e.matmul(p_[:], mT[:, kc, ts * 128:(ts + 1) * 128], wo_[:, kc, :], start=(kc == 0), stop=(kc == KC - 1)), r=[mT, wo_], w=[p_])
                    k.op("dve", lambda e, p_=p_, o_=o_, d4=d4: e.tensor_tensor(out=o_[:], in0=p_[:], in1=gm[:, d4 * 512:(d4 + 1) * 512], op=ALU.mult), r=[p_, gm], w=[o_])
                    k.op("pool", lambda e, o_=o_, x_=x_: e.tensor_tensor(out=o_[:], in0=o_[:], in1=x_[:], op=ALU.add), r=[o_, x_], w=[o_])
                    k.dma(S["x1"][r0:r0 + 128, d4 * 512:(d4 + 1) * 512], o_[:], r=[o_])
        k.end_phase()


def phase6(k, nc, I, S, NSEG):
    T = NSEG * SEG
    NBK = T // 512
    NJ = DFF // 128
    with ExitStack() as pes:
        k.pes = pes
        hT = k.sb("hT6", [128, KC, 514], BF16)
        flg = k.sb("flg6", [128, NF], F32)
        cw = k.sb("cw6", [128, NJ, 3], F32)
        cb = k.sb("cb6", [128, NJ], F32)
        wg = [k.sb(f"wg6{i}", [128, KC, 256], BF16) for i in range(2)]
        wv = [k.sb(f"wv6{i}", [128, KC, 256], BF16) for i in range(2)]
        wd = [k.sb(f"wd6{i}", [128, 22, 256], BF16) for i in range(4)]
        E = k.sb("E6", [128, 514], F32)
        a = k.sb("a6", [128, 512], F32)
        gl = k.sb("gl6", [128, 512], F32)
        uT = k.sb("uT6", [128, NJ, 512], BF16)
        gf = k.sb("gf6", [128, D], F32)
        xt = [k.sb(f"xt6{i}", [128, 256], F32) for i in range(2)]
        xo = [k.sb(f"xo6{i}", [128, 256], F32) for i in range(2)]
        pg = [k.ps(f"pg6{i}", [128, 512], F32) for i in range(2)]
        pv = [k.ps(f"pv6{i}", [128, 512], F32) for i in range(2)]
        ph = k.ps("ph6", [128, 512], F32)
        po = [k.ps(f"po6{i}", [128, 512], F32) for i in range(2)]
        k.dma(flg[:], I["flags"][0:1, :].partition_broadcast(128), w=[flg])
        k.dma(cw[:], I["ffn_conv_w"][:, :, :], w=[cw])
        k.dma(cb[:], I["ffn_conv_b"][:, :], w=[cb])
        hTd = S["h2T"].rearrange("k p t -> p k t")
        wud = S["f_w_up"]
        wdd = S["f_w_down"]
        nwu = 0
        nwd = 0
        cnt = 0

        def load_up(i2, slot):
            k.dma(wg[slot][:], wtile_ap(wud, 256 * i2, 256), w=[wg[slot]])
            k.dma(wv[slot][:], wtile_ap(wud, DFF + 256 * i2, 256), w=[wv[slot]])

        def load_down(idx, slot):
            d8, kh = idx // 2, idx % 2
            k.dma(wd[slot][:], wdd[kh * 2816:(kh + 1) * 2816, d8 * 256:(d8 + 1) * 256].rearrange("(k p) n -> p k n", p=128), w=[wd[slot]])

        load_up(0, 0)
        load_down(0, 0)
        load_down(1, 1)
        for b in range(NBK):
            t0 = b * 512
            seg = t0 // SEG
            if b % 4 == 0:
                k.dma(gf[:], S["modv"][seg:seg + 1, 5 * D:6 * D].partition_broadcast(128), w=[gf])
            k.dma(hT[:], hTd[:, :, t0 + 1:t0 + 515], w=[hT])
            if b % 4 == 0:
                k.op("dve", lambda e, seg=seg: e.tensor_scalar(out=hT[:, :, 0:1], in0=hT[:, :, 0:1], scalar1=flg[:, 2 * seg:2 * seg + 1], scalar2=None, op0=ALU.mult), r=[hT, flg], w=[hT])
            if b % 4 == 3:
                k.op("dve", lambda e, seg=seg: e.tensor_scalar(out=hT[:, :, 513:514], in0=hT[:, :, 513:514], scalar1=flg[:, 2 * seg + 1:2 * seg + 2], scalar2=None, op0=ALU.mult), r=[hT, flg], w=[hT])
            for i2 in range(22):
                slot = nwu % 2
                nxt = (i2 + 1) % 22
                if not (b == NBK - 1 and i2 == 21):
                    load_up(nxt, (nwu + 1) % 2)
                nwu += 1
                for c2 in range(2):
                    j = 2 * i2 + c2
                    pg_, pv_ = pg[cnt % 2], pv[cnt % 2]
                    cnt += 1
                    for kc in range(KC):
                        k.op("pe", lambda e, pg_=pg_, kc=kc, c2=c2, slot=slot: e.matmul(pg_[:], wg[slot][:, kc, c2 * 128:(c2 + 1) * 128], hT[:, kc, 1:513], start=(kc == 0), stop=(kc == KC - 1)), r=[wg[slot], hT], w=[pg_])
                    for kc in range(KC):
                        rhs = bass.AP(hT.t, kc * 514, [[KC * 514, 128], [513, 2]])
                        k.op("pe", lambda e, kc=kc, c2=c2, slot=slot, rhs=rhs: e.matmul(ph[:, 0:2], wg[slot][:, kc, c2 * 128:(c2 + 1) * 128], rhs, start=(kc == 0), stop=(kc == KC - 1)), r=[wg[slot], hT], w=[ph])
                    for kc in range(KC):
                        k.op("pe", lambda e, pv_=pv_, kc=kc, c2=c2, slot=slot: e.matmul(pv_[:], wv[slot][:, kc, c2 * 128:(c2 + 1) * 128], hT[:, kc, 1:513], start=(kc == 0), stop=(kc == KC - 1)), r=[wv[slot], hT], w=[pv_])
                    k.op("act", lambda e, pg_=pg_: e.copy(out=E[:, 1:513], in_=pg_[:]), r=[pg_], w=[E])
                    k.op("act", lambda e: e.copy(out=E[:, 0:1], in_=ph[:, 0:1]), r=[ph], w=[E])
                    k.op("act", lambda e: e.copy(out=E[:, 513:514], in_=ph[:, 1:2]), r=[ph], w=[E])
                    k.op("dve", lambda e, j=j: e.tensor_scalar(out=a[:], in0=E[:, 0:512], scalar1=cw[:, j, 0:1], scalar2=cb[:, j:j + 1], op0=ALU.mult, op1=ALU.add), r=[E, cw, cb], w=[a])
                    k.op("dve", lambda e, j=j: e.scalar_tensor_tensor(out=a[:], in0=E[:, 1:513], scalar=cw[:, j, 1:2], in1=a[:], op0=ALU.mult, op1=ALU.add), r=[E, cw, a], w=[a])
                    k.op("dve", lambda e, j=j: e.scalar_tensor_tensor(out=a[:], in0=E[:, 2:514], scalar=cw[:, j, 2:3], in1=a[:], op0=ALU.mult, op1=ALU.add), r=[E, cw, a], w=[a])
                    k.op("act", lambda e: e.activation(out=gl[:], in_=a[:], func=AF.Gelu), r=[a], w=[gl])
                    k.op("dve", lambda e, j=j, pv_=pv_: e.tensor_tensor(out=uT[:, j, :], in0=gl[:], in1=pv_[:], op=ALU.mult), r=[gl, pv_], w=[uT])
            for d8 in range(8):
                s0 = (2 * nwd) % 4
                if not (b == NBK - 1 and d8 == 7):
                    nd = (d8 + 1) % 8
                    load_down(2 * nd, (s0 + 2) % 4)
                    load_down(2 * nd + 1, (s0 + 3) % 4)
                nwd += 1
                for ts in range(4):
                    p_ = po[ts % 2]
                    for kh in range(2):
                        w_ = wd[s0 + kh]
                        for kc in range(22):
                            k.op("pe", lambda e, p_=p_, kc=kc, ts=ts, kh=kh, w_=w_: e.matmul(p_[:, 0:256], uT[:, kh * 22 + kc, ts * 128:(ts + 1) * 128], w_[:, kc, :], start=(kh == 0 and kc == 0), stop=(kh == 1 and kc == 21)), r=[uT, w_], w=[p_])
                    x_, o_ = xt[ts % 2], xo[ts % 2]
                    r0 = t0 + ts * 128
                    k.dma(x_[:], S["x1"][r0:r0 + 128, d8 * 256:(d8 + 1) * 256], w=[x_])
                    k.op("dve", lambda e, p_=p_, o_=o_, d8=d8: e.tensor_tensor(out=o_[:], in0=p_[:, 0:256], in1=gf[:, d8 * 256:(d8 + 1) * 256], op=ALU.mult), r=[p_, gf], w=[o_])
                    k.op("pool", lambda e, o_=o_, x_=x_: e.tensor_tensor(out=o_[:], in0=o_[:], in1=x_[:], op=ALU.add), r=[o_, x_], w=[o_])
                    k.dma(S["x2"][r0:r0 + 128, d8 * 256:(d8 + 1) * 256], o_[:], r=[o_])
        k.end_phase()


def final_norm(k, nc, I, S, y, NSEG):
    T = NSEG * SEG
    with ExitStack() as pes:
        k.pes = pes
        nw = k.sb("nwF", [128, D], F32)
        epst = k.sb("epsF", [128, 1], F32)
        xt = [k.sb(f"xtF{i}", [128, D], F32) for i in range(3)]
        yo = [k.sb(f"yoF{i}", [128, D], F32) for i in range(3)]
        junk = k.sb("junkF", [128, D], F32)
        ss = [k.sb(f"ssF{i}", [128, 1], F32) for i in range(3)]
        k.dma(nw[:], I["norm_f_w"][0:1, :].partition_broadcast(128), w=[nw])
        k.op("pool", lambda e: e.memset(epst[:], 1e-6), w=[epst])
        for i in range(T // 128):
            x_, o_, s_ = xt[i % 3], yo[i % 3], ss[i % 3]
            k.dma(x_[:], S["x2"][i * 128:(i + 1) * 128, :], w=[x_])
            k.op("pool", lambda e, x_=x_: e.tensor_tensor(out=junk[:], in0=x_[:], in1=x_[:], op=ALU.mult), r=[x_], w=[junk])
            k.op("dve", lambda e, s_=s_: e.reduce_sum(out=s_[:], in_=junk[:], axis=AX.X), r=[junk], w=[s_])
            k.op("act", lambda e, s_=s_: e.activation(out=s_[:], in_=s_[:], func=AF.Ln, scale=1.0 / D, bias=epst[:]), r=[s_, epst], w=[s_])
            k.op("act", lambda e, s_=s_: e.activation(out=s_[:], in_=s_[:], func=AF.Exp, scale=-0.5), r=[s_], w=[s_])
            k.op("dve", lambda e, x_=x_, o_=o_, s_=s_: e.scalar_tensor_tensor(out=o_[:], in0=x_[:], scalar=s_[:], in1=nw[:], op0=ALU.mult, op1=ALU.mult), r=[x_, s_, nw], w=[o_])
            k.dma(y[i * 128:(i + 1) * 128, :], o_[:], r=[o_])
        k.end_phase()


SEG = 2048
NEG = -30000.0
NF = 32
WNAMES = ["w_ada", "w_in", "w_attn_proj", "w_ssd_proj", "w_out", "w_up", "w_down"]
WKEY = dict(w_ada="w_ada", w_in="w_in", w_attn_proj="w_attn", w_ssd_proj="w_ssd", w_out="w_out", w_up="w_up", w_down="w_down")


def make_consts():
    cst = np.zeros((128, 8, 128), np.float32)
    i = np.arange(128)
    cst[:, 0, :] = np.eye(128)
    cst[i, 1, (i + 64) % 128] = 1.0
    kk, tt = np.meshgrid(i, i, indexing="ij")
    cst[:, 2, :] = (kk <= tt)
    cst[:, 3, :] = (kk > tt)
    cst[:, 4, :] = np.where(tt < kk, NEG, 0.0)
    cst[:, 5, :] = (kk >= tt)
    cst[:, 6, :] = (kk < tt)
    cst[:, 7, :] = np.where(tt > kk, NEG, 0.0)
    qq = np.arange(128)[:, None]
    k2 = np.arange(256)[None, :]
    band = np.where((k2 >= qq) & (k2 <= qq + 128), 0.0, NEG).astype(np.float32)
    return cst, band


def rope_tables(pos):
    inv_freq = (10000.0 ** (-(np.arange(0, 128, 2, dtype=np.float32) / np.float32(128)))).astype(np.float32)
    ang = (pos.astype(np.float32)[None, :] * inv_freq[:, None]).astype(np.float32)
    c = np.cos(ang.astype(np.float64)).astype(np.float32)
    s = np.sin(ang.astype(np.float64)).astype(np.float32)
    cos = np.concatenate([c, c], 0)
    sin = np.concatenate([-s, s], 0)
    return np.ascontiguousarray(cos), np.ascontiguousarray(sin)


def core_inputs(x_core, c_core, stype, params, nseg, wshard=None, ncores=8):
    T = nseg * SEG
    m = {}
    m["x"] = np.ascontiguousarray(x_core, dtype=np.float32)
    m["cT"] = np.asarray(c_core, dtype=np.float32).reshape(nseg, 16, 128).transpose(2, 1, 0)
    for name in WNAMES:
        w = params[name]
        if wshard is not None:
            r = w.shape[0] // ncores
            w = w[wshard * r:(wshard + 1) * r]
        m[WKEY[name]] = np.ascontiguousarray(w, dtype=np.float32)
    m["b_ada"] = params["b_ada"].reshape(1, -1)
    m["norm_mix_w"] = params["norm_mix_w"].reshape(1, -1)
    m["norm_ffn_w"] = params["norm_ffn_w"].reshape(1, -1)
    m["norm_f_w"] = params["norm_f_w"].reshape(1, -1)
    m["ssd_conv_w"] = params["ssd_conv_w"].reshape(5, 48, 128).transpose(2, 1, 0)
    m["ssd_conv_b"] = params["ssd_conv_b"].reshape(48, 128).T
    m["dt_bias"] = np.concatenate([params["dt_bias_fwd"].reshape(-1), params["dt_bias_bwd"].reshape(-1)]).reshape(1, 128)
    m["a_log"] = np.concatenate([params["a_log_fwd"].reshape(-1), params["a_log_bwd"].reshape(-1)]).reshape(1, 128)
    m["ssd_d"] = params["ssd_d"].reshape(1, 64)
    m["ssd_norm_w"] = params["ssd_norm_w"].reshape(1, -1)
    m["ffn_conv_w"] = params["ffn_conv_w"].reshape(3, 44, 128).transpose(2, 1, 0)
    m["ffn_conv_b"] = params["ffn_conv_b"].reshape(44, 128).T
    t = np.arange(T)
    pos = t if stype else (t % SEG)
    m["cosT"], m["sinT"] = rope_tables(pos.astype(np.float32))
    cst, band = make_consts()
    m["cst"] = cst
    m["band"] = band
    fl = np.zeros((1, NF), np.float32)
    if stype:
        for s in range(nseg):
            fl[0, 2 * s] = 1.0 if s > 0 else 0.0
            fl[0, 2 * s + 1] = 1.0 if s < nseg - 1 else 0.0
    nblk = 2 * nseg
    for b in range(nblk):
        s = b // 2
        fl[0, 8 + 2 * b] = 1.0 if (b % 2 == 1) else fl[0, 2 * s]
        fl[0, 8 + 2 * b + 1] = 1.0 if (b % 2 == 0) else fl[0, 2 * s + 1]
    m["flags"] = fl
    kb = np.zeros((4, 256), np.float32)
    nb = 0.0 if stype else NEG
    kb[0, :64] = nb
    kb[1, 192:] = nb
    kb[2, :64] = NEG
    kb[3, 192:] = NEG
    m["kb"] = kb
    return {k_: np.ascontiguousarray(v, dtype=np.float32) for k_, v in m.items()}


_PNAMES = ["w_ada", "b_ada", "norm_mix_w", "w_in", "ssd_conv_w", "ssd_conv_b", "dt_bias_fwd", "dt_bias_bwd", "a_log_fwd",
           "a_log_bwd", "ssd_d", "ssd_norm_w", "w_attn_proj", "w_ssd_proj", "w_out", "norm_ffn_w", "w_up", "ffn_conv_w",
           "ffn_conv_b", "w_down"]


def kernel(x_prompt, x_sample, c_prompt, c_sample, **params):
    p = {}
    for n in _PNAMES:
        p[n] = np.asarray(params[n])[0]
    p["norm_f_w"] = np.asarray(params["norm_f_w"])
    x_prompt = np.asarray(x_prompt)
    x_sample = np.asarray(x_sample)
    c_prompt = np.asarray(c_prompt)
    c_sample = np.asarray(c_sample)
    nseg = 4
    in_maps = []
    for core in range(8):
        if core < 4:
            xc = x_prompt[4 * core:4 * core + 4].reshape(nseg * SEG, D)
            cc = c_prompt[4 * core:4 * core + 4]
            st = False
        else:
            xc = x_sample[core - 4].reshape(nseg * SEG, D)
            cc = np.repeat(c_sample[core - 4:core - 3], nseg, axis=0)
            st = True
        in_maps.append(core_inputs(xc, cc, st, p, nseg, wshard=core))
    nc, _ = build(NSEG=nseg, gather=True)
    res = run_bass_kernel_spmd(nc, in_maps, core_ids=list(range(8)))
    y_prompt = np.empty((16, SEG, D), np.float32)
    y_sample = np.empty((4, 4 * SEG, D), np.float32)
    for core in range(8):
        yv = np.asarray(res.results[core]["y"], dtype=np.float32)
        if core < 4:
            y_prompt[4 * core:4 * core + 4] = yv.reshape(4, SEG, D)
        else:
            y_sample[core - 4] = yv
    return (y_prompt, y_sample)
```

```python
import numpy as np
from contextlib import ExitStack
import concourse.bass as bass
import concourse.mybir as mybir
from concourse.bass_utils import run_bass_kernel_spmd

F32 = mybir.dt.float32
BF16 = mybir.dt.bfloat16
AF = mybir.ActivationFunctionType
ALU = mybir.AluOpType
AX = mybir.AxisListType

ENGS = ("pe", "act", "dve", "pool", "sp")


class Tl:
    __slots__ = ("name", "t", "lastw", "readers", "dsem", "dcnt", "ddma", "psum")

    def __init__(self, name, t):
        self.psum = False
        self.name = name
        self.t = t
        self.lastw = {}
        self.readers = {}
        self.dsem = None
        self.dcnt = 0
        self.ddma = None

    def __getitem__(self, k):
        return self.t[k]


class Op:
    __slots__ = ("eng", "fn", "deps", "need", "val", "sem", "isdma", "idx")

    def __init__(self, eng, fn):
        self.eng = eng
        self.fn = fn
        self.deps = []
        self.need = False
        self.val = None
        self.sem = None
        self.isdma = False
        self.idx = 0


class MK:
    def __init__(self, nc, es):
        self.nc = nc
        self.es = es
        self.h = {"pe": nc.tensor, "act": nc.scalar, "dve": nc.vector, "pool": nc.gpsimd, "sp": nc.sync}
        self.sem = {e: es.enter_context(nc.semaphore("s_" + e)) for e in ENGS}
        self.cnt = {e: 0 for e in ENGS}
        self.ops = {e: [] for e in ENGS}
        self.tiles = []
        self.dma_tiles = []
        self.nblock = 0
        self.n_ops = 0
        self.pes = es
        self.dpool = []

    def sb(self, name, shape, dt):
        t = self.pes.enter_context(self.nc.sbuf_tensor(name, list(shape), dt))
        tl = Tl(name, t)
        self.tiles.append(tl)
        return tl

    def ps(self, name, shape, dt=F32):
        t = self.pes.enter_context(self.nc.psum_tensor(name, list(shape), dt))
        tl = Tl(name, t)
        tl.psum = True
        self.tiles.append(tl)
        return tl

    def view(self, name, t):
        tl = Tl(name, t)
        self.tiles.append(tl)
        return tl

    def _mk(self, eng, fn, r, w):
        op = Op(eng, fn)
        deps = {}

        def add(d):
            if d is None:
                return
            if d.isdma:
                deps[("dma", id(d.sem))] = d if (("dma", id(d.sem)) not in deps or deps[("dma", id(d.sem))].val < d.val) else deps[("dma", id(d.sem))]
            else:
                if d.eng == "pe" and eng == "pe":
                    return
                k = d.eng
                if k not in deps or deps[k].idx < d.idx:
                    deps[k] = d

        for t in r:
            for d in t.lastw.values():
                add(d)
            if t.psum:
                for ke, d in t.readers.items():
                    if ke != eng:
                        add(d)
        for t in w:
            for d in t.lastw.values():
                add(d)
            for d in t.readers.values():
                add(d)
        op.deps = list(deps.values())
        for d in op.deps:
            d.need = True
        return op

    def _reg(self, op, r, w, key):
        for t in r:
            t.readers[key] = op
        for t in w:
            t.lastw[key] = op
            t.readers = {}
        op.idx = len(self.ops[op.eng])
        self.ops[op.eng].append(op)
        self.n_ops += 1

    def op(self, eng, fn, r=(), w=()):
        op = self._mk(eng, fn, r, w)
        self._reg(op, r, w, eng)
        return op

    def dma(self, out, in_, r=(), w=(), q="sp", **kw):
        tl = (list(w) + list(r))[0]
        if tl.dsem is None:
            if not self.dpool:
                self.dpool.append((self.es.enter_context(self.nc.semaphore("dq%d" % self.n_ops)), 0))
            tl.dsem, tl.dcnt = self.dpool.pop()
            self.dma_tiles.append(tl)
        op = self._mk(q, None, r, w)
        tl.dcnt += 1
        op.isdma = True
        op.sem = tl.dsem
        op.val = 16 * tl.dcnt
        sem = tl.dsem

        def fn(e, out=out, in_=in_, sem=sem, kw=kw):
            return e.dma_start(out=out, in_=in_, **kw).then_inc(sem, 16)
        op.fn = fn
        op.deps = [d for d in op.deps if not (d.isdma and d.sem is sem)]
        for t in r:
            t.readers["dma"] = op
        for t in w:
            t.lastw["dma"] = op
            t.readers = {}
        op.idx = len(self.ops[q])
        self.ops[q].append(op)
        self.n_ops += 1
        return op

    def flush(self):
        nc = self.nc
        tail = []
        for tl in self.dma_tiles:
            if tl.dcnt:
                tail.append((tl.dsem, 16 * tl.dcnt))
        for e in ENGS:
            c = self.cnt[e]
            for op in self.ops[e]:
                if op.isdma:
                    continue
                if op.need:
                    c += 1
                    op.val = c
                    op.sem = self.sem[e]
            self.cnt[e] = c
        ops = self.ops
        hsem = self.sem

        def emit(e, eng):
            seen = {}
            for op in ops[e]:
                for d in op.deps:
                    k = id(d.sem)
                    if seen.get(k, 0) >= d.val:
                        continue
                    seen[k] = d.val
                    eng.wait_ge(d.sem, d.val)
                ins = op.fn(eng)
                if (not op.isdma) and op.need:
                    ins.then_inc(hsem[e], 1)
            if e == "sp":
                for (s, v) in tail:
                    if seen.get(id(s), 0) < v:
                        eng.wait_ge(s, v)

        with nc.Block() as block:
            @block.tensor
            def _(eng):
                emit("pe", eng)

            @block.scalar
            def _(eng):
                emit("act", eng)

            @block.vector
            def _(eng):
                emit("dve", eng)

            @block.gpsimd
            def _(eng):
                emit("pool", eng)

            @block.sync
            def _(eng):
                emit("sp", eng)
        self.ops = {e: [] for e in ENGS}
        for tl in self.tiles:
            tl.lastw = {}
            tl.readers = {}
        self.nblock += 1

    def end_phase(self):
        self.flush()
        for tl in self.dma_tiles:
            self.dpool.append((tl.dsem, tl.dcnt))
        self.tiles = []
        self.dma_tiles = []


D = 2048
SEG = 2048
KC = 16
INW = 19072
OFF = dict(q=0, k=1536, v=3072, z=4608, xbc=8704, dt=14848, ga=14976, gs=17024)
WSPEC = [("w_ada", 2048, 12288), ("w_in", 2048, 19072), ("w_attn", 512, 2048), ("w_ssd", 4096, 2048),
         ("w_out", 2048, 2048), ("w_up", 2048, 11264), ("w_down", 5632, 2048)]
DIL = (1, 4, 16)
NEG = -30000.0
DFF = 5632
NF = 32


def wtile_ap(wf, c0, ncols, kc=KC):
    return wf[:, c0:c0 + ncols].rearrange("(k p) n -> p k n", p=128)


def perm_view(ap2d, d):
    if d == 1:
        return ap2d.rearrange("p (r i) -> p r i", r=1)
    return ap2d.rearrange("p (i r) -> p r i", r=d)


def perm_tile(ap2d, d, j):
    v = perm_view(ap2d, d)
    if d == 1:
        return v[:, 0, 512 * j:512 * j + 512]
    if d == 4:
        return v[:, j, :]
    return v[:, 4 * j:4 * j + 4, :]


def perm_blk(ap2d, d, J):
    v = perm_view(ap2d, d)
    n = 2048 // d
    r = (128 * J) // n
    i0 = (128 * J) % n
    return v[:, r, i0:i0 + 128]


def build(NSEG=4, gather=True, debug=(), stop_after=None):
    T = NSEG * SEG
    nc = bass.Bass("TRN2", target_bir_lowering=False)
    dbg = set(debug)

    def din(name, shape, dt=F32):
        return nc.dram_tensor(name, list(shape), dt, kind="ExternalInput").ap()

    def dscr(name, shape, dt):
        if name in dbg:
            return nc.dram_tensor(name, list(shape), dt, kind="ExternalOutput").ap()
        return nc.dram_tensor(name, list(shape), dt).ap()

    I = {}
    I["x"] = din("x", [T, D])
    I["cT"] = din("cT", [128, KC, NSEG])
    for name, K, N in WSPEC:
        I[name] = din(name, [K // 8 if gather else K, N])
    I["b_ada"] = din("b_ada", [1, 6 * D])
    I["norm_mix_w"] = din("norm_mix_w", [1, D])
    I["norm_ffn_w"] = din("norm_ffn_w", [1, D])
    I["norm_f_w"] = din("norm_f_w", [1, D])
    I["ssd_conv_w"] = din("ssd_conv_w", [128, 48, 5])
    I["ssd_conv_b"] = din("ssd_conv_b", [128, 48])
    I["dt_bias"] = din("dt_bias", [1, 128])
    I["a_log"] = din("a_log", [1, 128])
    I["ssd_d"] = din("ssd_d", [1, 64])
    I["ssd_norm_w"] = din("ssd_norm_w", [1, 4096])
    I["ffn_conv_w"] = din("ffn_conv_w", [128, 44, 3])
    I["ffn_conv_b"] = din("ffn_conv_b", [128, 44])
    I["cosT"] = din("cosT", [128, T])
    I["sinT"] = din("sinT", [128, T])
    I["cst"] = din("cst", [128, 8, 128])
    I["band"] = din("band", [128, 256])
    I["flags"] = din("flags", [1, NF])
    I["kb"] = din("kb", [4, 256])
    y = nc.dram_tensor("y", [T, D], F32, kind="ExternalOutput").ap()

    S = {}
    for name, K, N in WSPEC:
        S["f_" + name] = dscr("f_" + name, [K, N], BF16)
        if gather:
            S["s_" + name] = dscr("s_" + name, [K // 8, N], BF16)
    S["modv"] = dscr("modv", [NSEG, 6 * D], F32)
    S["cstb"] = dscr("cstb", [128, 8, 128], BF16)
    S["bandb"] = dscr("bandb", [128, 256], BF16)
    S["kbb"] = dscr("kbb", [4, 256], BF16)
    S["hT"] = dscr("hT", [KC, 128, T + 4], BF16)
    S["qT"] = dscr("qT", [12, 128, T], BF16)
    S["kT"] = dscr("kT", [12, 128, T + 128], BF16)
    S["V"] = dscr("V", [3, T + 128, 512], BF16)
    S["zs"] = dscr("zs", [T, 4096], F32)
    S["xtok"] = dscr("xtok", [T, 4096], BF16)
    S["Btok"] = dscr("Btok", [T, 1024], BF16)
    S["BT"] = dscr("BT", [8, 128, T], BF16)
    S["CT"] = dscr("CT", [8, 128, T], BF16)
    S["dts"] = dscr("dts", [T, 128], F32)
    S["gT"] = dscr("gT", [2, KC, 128, T], F32)
    S["ao"] = dscr("ao", [3, T, 4, 132], F32)
    S["yf"] = dscr("yf", [T, 4096], F32)
    S["x1"] = dscr("x1", [T, D], F32)
    S["x2"] = dscr("x2", [T, D], F32)
    S["h2T"] = dscr("h2T", [KC, 128, T + 4], BF16)
    S["ynT"] = dscr("ynT", [32, 128, T], BF16)
    S["attnT"] = dscr("attnT", [4, 128, T], BF16)

    with ExitStack() as es:
        k = MK(nc, es)
        phase0(k, nc, I, S, gather)
        if stop_after == 0:
            return nc, k
        phaseM(k, nc, I, S, NSEG)
        if stop_after == "M":
            return nc, k
        norm_phase(k, nc, I["x"], S["modv"], 1, 0, S["hT"], S["cstb"], NSEG)
        if stop_after == "N":
            return nc, k
        phase1(k, nc, I, S, NSEG)
        if stop_after == 1:
            return nc, k
        phase2(k, nc, I, S, NSEG)
        phase2b(k, nc, I, S, NSEG)
        if stop_after == 2:
            return nc, k
        ssd_sweep(k, nc, I, S, NSEG, 0)
        if stop_after == 3:
            return nc, k
        ssd_sweep(k, nc, I, S, NSEG, 1)
        if stop_after == 4:
            return nc, k
        phase5(k, nc, I, S, NSEG)
        if stop_after == 5:
            return nc, k
        norm_phase(k, nc, S["x1"], S["modv"], 4, 3, S["h2T"], S["cstb"], NSEG, tag="m")
        phase6(k, nc, I, S, NSEG)
        if stop_after == 6:
            return nc, k
        final_norm(k, nc, I, S, y, NSEG)
    return nc, k


def phase0(k, nc, I, S, gather):
    with ExitStack() as pes:
        k.pes = pes
        dummy = k.sb("dummy0", [128, 4], F32)
        zt = k.sb("zt0", [128, 1024], BF16)
        k.op("dve", lambda e: e.memset(zt[:], 0.0), w=[zt])
        for name, K, N in WSPEC:
            src = I[name]
            rows = src.shape[0]
            dst = S["s_" + name] if gather else S["f_" + name]
            rstep = max(1, (1 << 20) // N)
            for r0 in range(0, rows, rstep):
                r1 = min(rows, r0 + rstep)
                k.dma(dst[r0:r1, :], src[r0:r1, :], w=[dummy], q="pool")
        k.dma(S["cstb"][:, :, :], I["cst"][:, :, :], w=[dummy], q="pool")
        k.dma(S["bandb"][:, :], I["band"][:, :], w=[dummy], q="pool")
        k.dma(S["kbb"][:, :], I["kb"][:, :], w=[dummy], q="pool")
        if gather:
            for name, K, N in WSPEC:
                def cc(e, name=name):
                    return e.collective_compute("AllGather", ALU.bypass, replica_groups=[list(range(8))],
                                                ins=[S["s_" + name].opt()], outs=[S["f_" + name].opt()])
                k.op("pool", cc, r=[dummy], w=[dummy])
            k.op("pool", lambda e: e.memset(dummy[:], 0.0), w=[dummy])
        T4 = S["hT"].shape[2]
        hTv = S["hT"].rearrange("k p t -> p k t")
        k.dma(hTv[:, :, 0:2], zt[:, 0:32].rearrange("p (k t) -> p k t", t=2), r=[zt])
        k.dma(hTv[:, :, T4 - 2:T4], zt[:, 0:32].rearrange("p (k t) -> p k t", t=2), r=[zt])
        h2v = S["h2T"].rearrange("k p t -> p k t")
        k.dma(h2v[:, :, 0:2], zt[:, 0:32].rearrange("p (k t) -> p k t", t=2), r=[zt])
        k.dma(h2v[:, :, T4 - 2:T4], zt[:, 0:32].rearrange("p (k t) -> p k t", t=2), r=[zt])
        Tk = S["kT"].shape[2]
        kTv = S["kT"].rearrange("h p t -> p h t")
        k.dma(kTv[:, :, 0:64], zt[:, 0:768].rearrange("p (h t) -> p h t", t=64), r=[zt])
        k.dma(kTv[:, :, Tk - 64:Tk], zt[:, 0:768].rearrange("p (h t) -> p h t", t=64), r=[zt])
        Tv = S["V"].shape[1]
        for g in range(3):
            k.dma(S["V"][g, 0:64, :], zt[0:64, 0:512], r=[zt])
            k.dma(S["V"][g, Tv - 64:Tv, :], zt[0:64, 0:512], r=[zt])
        k.end_phase()


def phaseM(k, nc, I, S, NSEG):
    with ExitStack() as pes:
        k.pes = pes
        cT = k.sb("cTs", [128, KC, NSEG], F32)
        cTb = k.sb("cTb", [128, KC, NSEG], BF16)
        mod = k.sb("mod", [NSEG, 6 * D], F32)
        bada = k.sb("bada", [NSEG, 6 * D], F32)
        nw = k.sb("nw", [NSEG, 2, D], F32)
        wt = [k.sb(f"wtM{i}", [128, KC, 512], BF16) for i in range(2)]
        pm = [k.ps(f"pmM{i}", [128, 512], F32) for i in range(2)]
        k.dma(cT[:], I["cT"][:, :, :], w=[cT])
        k.dma(bada[:], I["b_ada"][0:1, :].partition_broadcast(NSEG), w=[bada])
        k.dma(nw[:, 0, :], I["norm_mix_w"][0:1, :].partition_broadcast(NSEG), w=[nw])
        k.dma(nw[:, 1, :], I["norm_ffn_w"][0:1, :].partition_broadcast(NSEG), w=[nw])
        k.op("act", lambda e: e.activation(out=cTb[:], in_=cT[:], func=AF.Silu), r=[cT], w=[cTb])
        wf = S["f_w_ada"]
        k.dma(wt[0][:], wtile_ap(wf, 0, 512), w=[wt[0]])
        for n in range(24):
            if n + 1 < 24:
                k.dma(wt[(n + 1) % 2][:], wtile_ap(wf, 512 * (n + 1), 512), w=[wt[(n + 1) % 2]])
            w_, p_ = wt[n % 2], pm[n % 2]
            for kc in range(KC):
                k.op("pe", lambda e, w_=w_, p_=p_, kc=kc: e.matmul(p_[0:NSEG, :], cTb[:, kc, :], w_[:, kc, :], start=(kc == 0), stop=(kc == KC - 1)),
                     r=[cTb, w_], w=[p_])
            k.op("dve", lambda e, p_=p_, n=n: e.tensor_tensor(out=mod[:, 512 * n:512 * n + 512], in0=p_[0:NSEG, :], in1=bada[:, 512 * n:512 * n + 512], op=ALU.add),
                 r=[p_, bada], w=[mod])
        for part, wi in ((1, 0), (4, 1)):
            k.op("dve", lambda e, part=part, wi=wi: e.scalar_tensor_tensor(out=mod[:, part * D:(part + 1) * D], in0=mod[:, part * D:(part + 1) * D],
                                                                          scalar=1.0, in1=nw[:, wi, :], op0=ALU.add, op1=ALU.mult),
                 r=[mod, nw], w=[mod])
        k.dma(S["modv"][:, :], mod[:], r=[mod])
        k.end_phase()


def norm_phase(k, nc, src, modv, iA, iB, hT_d, cstb, NSEG, tag="n"):
    with ExitStack() as pes:
        k.pes = pes
        ident = k.sb("ident" + tag, [128, 128], BF16)
        A = k.sb("A" + tag, [128, D], F32)
        B = k.sb("B" + tag, [128, D], F32)
        xt = [k.sb(f"xt{tag}{i}", [128, D], F32) for i in range(3)]
        tmp = k.sb("tmp" + tag, [128, D], F32)
        junk = k.sb("junk" + tag, [128, D], F32)
        hb = [k.sb(f"hb{tag}{i}", [128, D], BF16) for i in range(3)]
        ss = [k.sb(f"ss{tag}{i}", [128, 1], F32) for i in range(3)]
        rs = [k.sb(f"rs{tag}{i}", [128, 1], F32) for i in range(3)]
        epst = k.sb("eps" + tag, [128, 1], F32)
        ptb = [k.ps(f"ptb{tag}{i}", [128, 1024], BF16) for i in range(2)]
        hst = [k.sb(f"hst{tag}{i}", [128, KC, 512], BF16) for i in range(2)]
        k.op("pool", lambda e: e.memset(epst[:], 1e-6), w=[epst])
        k.dma(ident[:], cstb[:, 0, :], w=[ident])
        hTv = hT_d.rearrange("k p t -> p k t")
        for seg in range(NSEG):
            k.dma(A[:], modv[seg:seg + 1, iA * D:(iA + 1) * D].partition_broadcast(128), w=[A])
            k.dma(B[:], modv[seg:seg + 1, iB * D:(iB + 1) * D].partition_broadcast(128), w=[B])
            for tt in range(16):
                i = seg * 16 + tt
                x_, h_, s_, r_ = xt[i % 3], hb[i % 3], ss[i % 3], rs[i % 3]
                q = (i // 4) % 2
                k.dma(x_[:], src[i * 128:(i + 1) * 128, :], w=[x_])
                k.op("pool", lambda e, x_=x_: e.tensor_tensor(out=junk[:], in0=x_[:], in1=x_[:], op=ALU.mult), r=[x_], w=[junk])
                k.op("dve", lambda e, s_=s_: e.reduce_sum(out=s_[:], in_=junk[:], axis=AX.X), r=[junk], w=[s_])
                k.op("act", lambda e, s_=s_, r_=r_: e.activation(out=r_[:], in_=s_[:], func=AF.Ln, scale=1.0 / D, bias=epst[:]), r=[s_, epst], w=[r_])
                k.op("act", lambda e, r_=r_: e.activation(out=r_[:], in_=r_[:], func=AF.Exp, scale=-0.5), r=[r_], w=[r_])
                k.op("dve", lambda e, x_=x_, r_=r_: e.scalar_tensor_tensor(out=tmp[:], in0=x_[:], scalar=r_[:], in1=A[:], op0=ALU.mult, op1=ALU.mult),
                     r=[x_, r_, A], w=[tmp])
                k.op("dve", lambda e, h_=h_: e.tensor_tensor(out=h_[:], in0=tmp[:], in1=B[:], op=ALU.add), r=[tmp, B], w=[h_])
                for kc in range(KC):
                    p_ = ptb[kc // 8]
                    k.op("pe", lambda e, p_=p_, kc=kc, h_=h_: e.transpose(out=p_[:, (kc % 8) * 128:(kc % 8 + 1) * 128], in_=h_[:, kc * 128:(kc + 1) * 128], identity=ident[:]),
                         r=[h_, ident], w=[p_])
                t4 = tt % 4
                k.op("act", lambda e, q=q, t4=t4: e.copy(out=hst[q][:, 0:8, t4 * 128:(t4 + 1) * 128], in_=ptb[0][:].rearrange("p (k t) -> p k t", t=128)),
                     r=[ptb[0]], w=[hst[q]])
                k.op("dve", lambda e, q=q, t4=t4: e.tensor_copy(out=hst[q][:, 8:16, t4 * 128:(t4 + 1) * 128], in_=ptb[1][:].rearrange("p (k t) -> p k t", t=128)),
                     r=[ptb[1]], w=[hst[q]])
                if t4 == 3:
                    t0 = (i - 3) * 128
                    k.dma(hTv[:, :, 2 + t0:2 + t0 + 512], hst[q][:], r=[hst[q]])
        k.end_phase()


def phase1(k, nc, I, S, NSEG):
    T = NSEG * SEG
    wf = S["f_w_in"]
    with ExitStack() as pes:
        k.pes = pes
        hT = k.sb("hT1", [128, KC, SEG + 4], BF16)
        wt = [k.sb(f"wt1{i}", [128, KC, 512], BF16) for i in range(2)]
        cosS = k.sb("cosS", [128, SEG], F32)
        sinS = k.sb("sinS", [128, SEG], F32)
        flg = k.sb("flg1", [128, NF], F32)
        cst = k.sb("cst1", [128, 2, 128], BF16)
        cw = k.sb("cw1", [128, 48, 5], F32)
        cb = k.sb("cb1", [128, 48], F32)
        dtb = k.sb("dtb1", [128, 128], F32)
        pm = [k.ps(f"pm1{i}", [128, 512], F32) for i in range(5)]
        ph = k.ps("ph1", [128, 512], F32)
        ptb = [k.ps(f"ptb1{i}", [128, 1024], BF16) for i in range(2)]
        qb = [k.sb(f"qb1{i}", [128, 512], BF16) for i in range(2)]
        qo = [k.sb(f"qo1{i}", [128, 512], BF16) for i in range(2)]
        vo = [k.sb(f"vo1{i}", [128, 512], BF16) for i in range(2)]
        t1 = [k.sb(f"t11{i}", [128, 512], F32) for i in range(2)]
        t2 = [k.sb(f"t21{i}", [128, 512], F32) for i in range(2)]
        zo = [k.sb(f"zo1{i}", [128, 512], F32) for i in range(2)]
        E = k.sb("E1", [128, SEG + 4], F32)
        a1 = k.sb("a11", [128, SEG], F32)
        a2 = k.sb("a21", [128, SEG], F32)
        xo = [k.sb(f"xo1{i}", [128, SEG], BF16) for i in range(2)]
        xts = k.sb("xts1", [128, 16, 512], BF16)
        ds = [k.sb(f"ds1{i}", [128, 128], F32) for i in range(4)]

        k.dma(flg[:], I["flags"][0:1, :].partition_broadcast(128), w=[flg])
        k.dma(cst[:], S["cstb"][:, 0:2, :], w=[cst])
        k.dma(cw[:], I["ssd_conv_w"][:, :, :], w=[cw])
        k.dma(cb[:], I["ssd_conv_b"][:, :], w=[cb])
        k.dma(dtb[:], I["dt_bias"][0:1, :].partition_broadcast(128), w=[dtb])
        ident = cst[:, 0, :]
        pswap = cst[:, 1, :]

        hTd = S["hT"].rearrange("k p t -> p k t")
        sched = []
        for g in range(3):
            sched.append((OFF["q"] + 512 * g, 512, "qk", (0, g)))
        for g in range(3):
            sched.append((OFF["k"] + 512 * g, 512, "qk", (1, g)))
        for g in range(3):
            sched.append((OFF["v"] + 512 * g, 512, "v", g))
        for i in range(8):
            sched.append((OFF["z"] + 512 * i, 512, "z", i))
        for i in range(12):
            sched.append((OFF["xbc"] + 512 * i, 512, "xbc", i))
        sched.append((OFF["dt"], 128, "dt", 0))
        for i in range(4):
            sched.append((OFF["ga"] + 512 * i, 512, "g", (0, i)))
        for i in range(4):
            sched.append((OFF["gs"] + 512 * i, 512, "g", (1, i)))
        nW = len(sched)
        cnt = dict(pm=0, q=0, v=0, z=0, x=0, d=0)

        def nextpm():
            p = pm[cnt["pm"] % 5]
            cnt["pm"] += 1
            return p

        def load_w(idx, si):
            c0, ncols, _, _ = sched[si]
            k.dma(wt[idx % 2][:, :, 0:ncols], wtile_ap(wf, c0, ncols), w=[wt[idx % 2]])

        widx = 0
        pend = [None]
        for seg in range(NSEG):
            tb = seg * SEG
            if pend[0] is not None:
                pend[0]()
                pend[0] = None
            k.dma(hT[:], hTd[:, :, tb:tb + SEG + 4], w=[hT])
            k.op("dve", lambda e, seg=seg: e.tensor_scalar(out=hT[:, :, 0:2], in0=hT[:, :, 0:2], scalar1=flg[:, 2 * seg:2 * seg + 1], scalar2=None, op0=ALU.mult), r=[hT, flg], w=[hT])
            k.op("dve", lambda e, seg=seg: e.tensor_scalar(out=hT[:, :, SEG + 2:SEG + 4], in0=hT[:, :, SEG + 2:SEG + 4], scalar1=flg[:, 2 * seg + 1:2 * seg + 2], scalar2=None, op0=ALU.mult), r=[hT, flg], w=[hT])
            k.dma(cosS[:], I["cosT"][:, tb:tb + SEG], w=[cosS])
            k.dma(sinS[:], I["sinT"][:, tb:tb + SEG], w=[sinS])
            if seg == 0:
                load_w(widx, 0)
            for si in range(nW):
                c0, ncols, kind, idx = sched[si]
                w_ = wt[widx % 2]
                if si + 1 < nW:
                    load_w(widx + 1, si + 1)
                elif seg + 1 < NSEG:
                    load_w(widx + 1, 0)
                widx += 1
                if kind not in ("qk", "xbc") and pend[0] is not None:
                    pend[0]()
                    pend[0] = None
                if kind == "qk":
                    which, g = idx
                    d = DIL[g]
                    L = T // d
                    nseg = SEG // d
                    dst = S["qT"] if which == 0 else S["kT"]
                    off = 0 if which == 0 else 64
                    for hh in range(4):
                        head = 4 * g + hh
                        for j in range(4):
                            p_ = nextpm()
                            ci = cnt["q"] % 2
                            cnt["q"] += 1
                            for kc in range(KC):
                                rhs = perm_tile(hT[:, kc, 2:SEG + 2], d, j)
                                k.op("pe", lambda e, p_=p_, w_=w_, kc=kc, hh=hh, rhs=rhs: e.matmul(p_[:], w_[:, kc, hh * 128:(hh + 1) * 128], rhs, start=(kc == 0), stop=(kc == KC - 1)),
                                     r=[hT, w_], w=[p_])
                            k.op("act", lambda e, p_=p_, ci=ci: e.copy(out=qb[ci][:], in_=p_[:]), r=[p_], w=[qb[ci]])

                            def post(p_=p_, ci=ci, d=d, j=j, head=head, dst=dst, off=off, L=L, nseg=nseg, seg=seg):
                                k.op("pe", lambda e, ci=ci: e.matmul(ph[:], pswap, qb[ci][:], start=True, stop=True), r=[qb[ci], cst], w=[ph])
                                cv = perm_tile(cosS[:], d, j)
                                sv = perm_tile(sinS[:], d, j)
                                if d == 16:
                                    pv = p_[:].rearrange("p (r i) -> p r i", r=4)
                                    phv = ph[:].rearrange("p (r i) -> p r i", r=4)
                                    t1v = t1[ci][:].rearrange("p (r i) -> p r i", r=4)
                                    t2v = t2[ci][:].rearrange("p (r i) -> p r i", r=4)
                                else:
                                    pv, phv, t1v, t2v = p_[:], ph[:], t1[ci][:], t2[ci][:]
                                k.op("dve", lambda e, pv=pv, t1v=t1v, cv=cv: e.tensor_tensor(out=t1v, in0=pv, in1=cv, op=ALU.mult), r=[p_, cosS, qb[ci]], w=[t1[ci]])
                                k.op("dve", lambda e, phv=phv, t2v=t2v, sv=sv: e.tensor_tensor(out=t2v, in0=phv, in1=sv, op=ALU.mult), r=[ph, sinS], w=[t2[ci]])
                                k.op("pool", lambda e, ci=ci: e.tensor_tensor(out=qo[ci][:], in0=t1[ci][:], in1=t2[ci][:], op=ALU.add), r=[t1[ci], t2[ci]], w=[qo[ci]])
                                if d == 1:
                                    dap = dst[head, :, off + seg * nseg + 512 * j: off + seg * nseg + 512 * j + 512]
                                    sap = qo[ci][:]
                                elif d == 4:
                                    dap = dst[head, :, off + j * L + seg * nseg: off + j * L + seg * nseg + 512]
                                    sap = qo[ci][:]
                                else:
                                    base = dst[head, :, off:off + T].rearrange("p (r i) -> p r i", r=16)
                                    dap = base[:, 4 * j:4 * j + 4, seg * nseg:seg * nseg + 128]
                                    sap = qo[ci][:].rearrange("p (r i) -> p r i", r=4)
                                k.dma(dap, sap, r=[qo[ci]])
                            if pend[0] is not None:
                                pend[0]()
                            pend[0] = post
                elif kind == "v":
                    g = idx
                    d = DIL[g]
                    L = T // d
                    nseg = SEG // d
                    for J in range(16):
                        p_ = nextpm()
                        ci = cnt["v"] % 2
                        cnt["v"] += 1
                        for kc in range(KC):
                            lhs = perm_blk(hT[:, kc, 2:SEG + 2], d, J)
                            k.op("pe", lambda e, p_=p_, w_=w_, kc=kc, lhs=lhs: e.matmul(p_[:], lhs, w_[:, kc, :], start=(kc == 0), stop=(kc == KC - 1)),
                                 r=[hT, w_], w=[p_])
                        k.op("act", lambda e, p_=p_, ci=ci: e.copy(out=vo[ci][:], in_=p_[:]), r=[p_], w=[vo[ci]])
                        r = (128 * J) // nseg
                        i0 = (128 * J) % nseg
                        row0 = 64 + r * L + seg * nseg + i0
                        k.dma(S["V"][g, row0:row0 + 128, :], vo[ci][:], r=[vo[ci]])
                elif kind == "z":
                    for tt in range(16):
                        p_ = nextpm()
                        ci = cnt["z"] % 2
                        cnt["z"] += 1
                        for kc in range(KC):
                            k.op("pe", lambda e, p_=p_, w_=w_, kc=kc, tt=tt: e.matmul(p_[:], hT[:, kc, 2 + 128 * tt:2 + 128 * tt + 128], w_[:, kc, :], start=(kc == 0), stop=(kc == KC - 1)),
                                 r=[hT, w_], w=[p_])
                        k.op("act", lambda e, p_=p_, ci=ci: e.activation(out=zo[ci][:], in_=p_[:], func=AF.Silu), r=[p_], w=[zo[ci]])
                        k.dma(S["zs"][tb + 128 * tt:tb + 128 * tt + 128, 512 * idx:512 * idx + 512], zo[ci][:], r=[zo[ci]])
                elif kind == "xbc":
                    for cc in range(4):
                        cg = 4 * idx + cc
                        ps_ = []
                        for j in range(4):
                            p_ = nextpm()
                            ps_.append(p_)
                            for kc in range(KC):
                                k.op("pe", lambda e, p_=p_, w_=w_, kc=kc, cc=cc, j=j: e.matmul(p_[:], w_[:, kc, cc * 128:(cc + 1) * 128], hT[:, kc, 2 + 512 * j:2 + 512 * j + 512], start=(kc == 0), stop=(kc == KC - 1)),
                                     r=[hT, w_], w=[p_])
                        for kc in range(KC):
                            rhs = bass.AP(hT.t, kc * (SEG + 4), [[KC * (SEG + 4), 128], [SEG + 2, 2], [1, 2]])
                            k.op("pe", lambda e, w_=w_, kc=kc, cc=cc, rhs=rhs: e.matmul(ph[:, 0:4], w_[:, kc, cc * 128:(cc + 1) * 128], rhs, start=(kc == 0), stop=(kc == KC - 1)),
                                 r=[hT, w_], w=[ph])
                        for j in range(4):
                            k.op("act", lambda e, p_=ps_[j], j=j: e.copy(out=E[:, 2 + 512 * j:2 + 512 * j + 512], in_=p_[:]), r=[ps_[j]], w=[E])
                        k.op("act", lambda e: e.copy(out=E[:, 0:2], in_=ph[:, 0:2]), r=[ph], w=[E])
                        k.op("act", lambda e: e.copy(out=E[:, SEG + 2:SEG + 4], in_=ph[:, 2:4]), r=[ph], w=[E])
                        k.op("dve", lambda e, cg=cg: e.tensor_scalar(out=a1[:], in0=E[:, 0:SEG], scalar1=cw[:, cg, 0:1], scalar2=cb[:, cg:cg + 1], op0=ALU.mult, op1=ALU.add), r=[E, cw, cb], w=[a1])
                        k.op("dve", lambda e, cg=cg: e.scalar_tensor_tensor(out=a1[:], in0=E[:, 1:SEG + 1], scalar=cw[:, cg, 1:2], in1=a1[:], op0=ALU.mult, op1=ALU.add), r=[E, cw, a1], w=[a1])
                        k.op("dve", lambda e, cg=cg: e.scalar_tensor_tensor(out=a1[:], in0=E[:, 2:SEG + 2], scalar=cw[:, cg, 2:3], in1=a1[:], op0=ALU.mult, op1=ALU.add), r=[E, cw, a1], w=[a1])
                        k.op("dve", lambda e, cg=cg: e.scalar_tensor_tensor(out=a1[:], in0=E[:, 3:SEG + 3], scalar=cw[:, cg, 3:4], in1=a1[:], op0=ALU.mult, op1=ALU.add), r=[E, cw, a1], w=[a1])
                        k.op("dve", lambda e, cg=cg: e.scalar_tensor_tensor(out=a2[:], in0=E[:, 4:SEG + 4], scalar=cw[:, cg, 4:5], in1=a1[:], op0=ALU.mult, op1=ALU.add), r=[E, cw, a1], w=[a2])
                        ci = cnt["x"] % 2
                        cnt["x"] += 1
                        x_ = xo[ci]
                        k.op("act", lambda e, x_=x_: e.activation(out=x_[:], in_=a2[:], func=AF.Silu), r=[a2], w=[x_])
                        if cg >= 32:
                            gi = (cg - 32) % 8
                            dst = S["BT"] if cg < 40 else S["CT"]
                            k.dma(dst[gi, :, tb:tb + SEG], x_[:], r=[x_])

                        def postx(cg=cg, cc=cc, x_=x_, idx=idx, tb=tb):
                            if cg < 40:
                                for half in range(2):
                                    for t8 in range(8):
                                        tt = half * 8 + t8
                                        k.op("pe", lambda e, x_=x_, tt=tt, t8=t8, half=half: e.transpose(out=ptb[half][:, t8 * 128:(t8 + 1) * 128], in_=x_[:, tt * 128:(tt + 1) * 128], identity=ident),
                                             r=[x_, cst], w=[ptb[half]])
                                    outv = xts[:, half * 8:half * 8 + 8, cc * 128:(cc + 1) * 128]
                                    inv = ptb[half][:].rearrange("p (t c) -> p t c", c=128)
                                    if half == 0:
                                        k.op("act", lambda e, outv=outv, inv=inv: e.copy(out=outv, in_=inv), r=[ptb[half]], w=[xts])
                                    else:
                                        k.op("dve", lambda e, outv=outv, inv=inv: e.tensor_copy(out=outv, in_=inv), r=[ptb[half]], w=[xts])
                            if cc == 3:
                                if idx < 8:
                                    dv = S["xtok"][tb:tb + SEG, 512 * idx:512 * idx + 512].rearrange("(t p) c -> p t c", p=128)
                                    k.dma(dv, xts[:], r=[xts])
                                elif idx < 10:
                                    dv = S["Btok"][tb:tb + SEG, 512 * (idx - 8):512 * (idx - 8) + 512].rearrange("(t p) c -> p t c", p=128)
                                    k.dma(dv, xts[:], r=[xts])
                        if pend[0] is not None:
                            pend[0]()
                        pend[0] = postx
                elif kind == "dt":
                    for tt in range(16):
                        p_ = nextpm()
                        for kc in range(KC):
                            k.op("pe", lambda e, p_=p_, w_=w_, kc=kc, tt=tt: e.matmul(p_[:, 0:128], hT[:, kc, 2 + 128 * tt:2 + 128 * tt + 128], w_[:, kc, 0:128], start=(kc == 0), stop=(kc == KC - 1)),
                                 r=[hT, w_], w=[p_])
                        xb, ab, eb, ob = ds
                        k.op("dve", lambda e, p_=p_: e.tensor_tensor(out=xb[:], in0=p_[:, 0:128], in1=dtb[:], op=ALU.add), r=[p_, dtb], w=[xb])
                        k.op("act", lambda e: e.activation(out=ab[:], in_=xb[:], func=AF.Abs), r=[xb], w=[ab])
                        k.op("act", lambda e: e.activation(out=eb[:], in_=ab[:], func=AF.Exp, scale=-1.0), r=[ab], w=[eb])
                        k.op("act", lambda e: e.activation(out=eb[:], in_=eb[:], func=AF.Ln, bias=1.0), r=[eb], w=[eb])
                        k.op("dve", lambda e: e.scalar_tensor_tensor(out=ob[:], in0=xb[:], scalar=0.0, in1=eb[:], op0=ALU.max, op1=ALU.add), r=[xb, eb], w=[ob])
                        k.dma(S["dts"][tb + 128 * tt:tb + 128 * tt + 128, :], ob[:], r=[ob])
                elif kind == "g":
                    which, i4 = idx
                    for cc in range(4):
                        ch = 4 * i4 + cc
                        go = a1 if (cc % 2 == 0) else a2
                        for j in range(4):
                            p_ = nextpm()
                            for kc in range(KC):
                                k.op("pe", lambda e, p_=p_, w_=w_, kc=kc, cc=cc, j=j: e.matmul(p_[:], w_[:, kc, cc * 128:(cc + 1) * 128], hT[:, kc, 2 + 512 * j:2 + 512 * j + 512], start=(kc == 0), stop=(kc == KC - 1)),
                                     r=[hT, w_], w=[p_])
                            k.op("act", lambda e, p_=p_, go=go, j=j: e.activation(out=go[:, 512 * j:512 * j + 512], in_=p_[:], func=AF.Sigmoid), r=[p_], w=[go])
                        k.dma(S["gT"][which, ch, :, tb:tb + SEG], go[:], r=[go])
        if pend[0] is not None:
            pend[0]()
            pend[0] = None
        k.end_phase()


def phase2(k, nc, I, S, NSEG):
    T = NSEG * SEG
    NB = T // 128
    scale = 128.0 ** -0.5
    with ExitStack() as pes:
        k.pes = pes
        cst = k.sb("cst2", [128, 128], BF16)
        band = k.sb("band2", [128, 256], BF16)
        kbt = k.sb("kbt2", [1, 4, 256], BF16)
        ones = k.sb("ones2", [1, 128], BF16)
        qTt = [k.sb(f"qTt{i}", [128, T], BF16) for i in range(2)]
        kTt = [k.sb(f"kTt{i}", [128, T + 128], BF16) for i in range(2)]
        Vt = [k.sb(f"Vt{i}", [128, NB + 1, 128], BF16) for i in range(2)]
        sc = [k.ps(f"sc2{i}", [128, 512], F32) for i in range(2)]
        pT = [k.ps(f"pT2{i}", [128, 1024], BF16) for i in range(2)]
        ov = [k.ps(f"ov2{i}", [128, 512], F32) for i in range(2)]
        Pb = [k.sb(f"Pb2{i}", [128, 256], BF16) for i in range(2)]
        PT = [k.sb(f"PT2{i}", [128, 2, 128], BF16) for i in range(2)]
        mx = [k.sb(f"mx2{i}", [128, 1], F32) for i in range(2)]
        nb = [k.sb(f"nb2{i}", [128, 1], F32) for i in range(2)]
        den = [k.sb(f"den2{i}", [128, 1], F32) for i in range(2)]
        rden = [k.sb(f"rden2{i}", [128, 1], F32) for i in range(2)]
        aos = [k.sb(f"aos2{i}", [128, 132], F32) for i in range(4)]
        k.dma(cst[:], S["cstb"][:, 0, :], w=[cst])
        k.dma(band[:], S["bandb"][:, :], w=[band])
        k.dma(kbt[:], S["kbb"][:, :].rearrange("(o r) c -> o r c", o=1), w=[kbt])
        k.op("pool", lambda e: e.memset(ones[:], 1.0), w=[ones])
        for a in aos:
            k.op("pool", lambda e, a=a: e.memset(a[:], 0.0), w=[a])

        def load_head(head, bi):
            g = head // 4
            hh = head % 4
            k.dma(qTt[bi][:], S["qT"][head, :, :], w=[qTt[bi]])
            k.dma(kTt[bi][:], S["kT"][head, :, :], w=[kTt[bi]])
            k.dma(Vt[bi][:], S["V"][g, :, hh * 128:(hh + 1) * 128].rearrange("(j p) e -> p j e", p=128), w=[Vt[bi]])

        def stageA(head, J, bi, i):
            g = head // 4
            d = DIL[g]
            L = T // d
            nbr = L // 128
            nbs = max(1, (SEG // d) // 128)
            b = J % nbr
            rows = []
            if b == 0:
                rows.append(2)
            elif b % nbs == 0:
                rows.append(0)
            if b == nbr - 1:
                rows.append(3)
            elif b % nbs == nbs - 1:
                rows.append(1)
            s_ = sc[i % 2]
            q_, k_ = qTt[bi], kTt[bi]
            k.op("pe", lambda e: e.matmul(s_[:, 0:256], q_[:, 128 * J:128 * J + 128], k_[:, 128 * J:128 * J + 256], start=True, stop=False), r=[q_, k_], w=[s_])
            k.op("pe", lambda e: e.matmul(s_[:, 0:256], cst[:], band[:], start=False, stop=(len(rows) == 0)), r=[cst, band], w=[s_])
            for ri, row in enumerate(rows):
                k.op("pe", lambda e, row=row, ri=ri: e.matmul(s_[:, 0:256], ones[0:1, :], kbt[0:1, row, :], start=False, stop=(ri == len(rows) - 1)), r=[ones, kbt], w=[s_])
            m_, n_, d_, p_ = mx[i % 2], nb[i % 2], den[i % 2], Pb[i % 2]
            k.op("dve", lambda e: e.reduce_max(out=m_[:], in_=s_[:, 0:256], axis=AX.X), r=[s_], w=[m_])
            k.op("dve", lambda e: e.tensor_scalar(out=n_[:], in0=m_[:], scalar1=-scale, scalar2=None, op0=ALU.mult), r=[m_], w=[n_])
            k.op("act", lambda e: e.activation(out=p_[:], in_=s_[:, 0:256], func=AF.Exp, bias=n_[:], scale=scale, accum_out=d_[:]), r=[s_, n_], w=[p_, d_])

        def stageB(head, J, bi, i):
            g = head // 4
            hh = head % 4
            d = DIL[g]
            L = T // d
            p_, t_, o_, P_ = Pb[i % 2], pT[i % 2], ov[i % 2], PT[i % 2]
            m_, d_, r_ = mx[i % 2], den[i % 2], rden[i % 2]
            a_ = aos[i % 4]
            v_ = Vt[bi]
            for half in range(2):
                k.op("pe", lambda e, half=half: e.transpose(out=t_[:, half * 128:(half + 1) * 128], in_=p_[:, half * 128:(half + 1) * 128], identity=cst[:]), r=[p_, cst], w=[t_])
            k.op("act", lambda e: e.copy(out=P_[:], in_=t_[:, 0:256].rearrange("p (h q) -> p h q", h=2)), r=[t_], w=[P_])
            for half in range(2):
                k.op("pe", lambda e, half=half: e.matmul(o_[:, 0:128], P_[:, half, :], v_[:, J + half, :], start=(half == 0), stop=(half == 1)), r=[P_, v_], w=[o_])
            k.op("dve", lambda e: e.reciprocal(out=r_[:], in_=d_[:]), r=[d_], w=[r_])
            k.op("dve", lambda e: e.tensor_scalar(out=a_[:, 0:128], in0=o_[:, 0:128], scalar1=r_[:], scalar2=None, op0=ALU.mult), r=[o_, r_], w=[a_])
            k.op("dve", lambda e: e.tensor_scalar(out=a_[:, 128:129], in0=m_[:], scalar1=scale, scalar2=None, op0=ALU.mult), r=[m_], w=[a_])
            k.op("dve", lambda e: e.tensor_copy(out=a_[:, 129:130], in_=d_[:]), r=[d_], w=[a_])
            r = (128 * J) // L
            I0 = (128 * J) % L
            t0 = r + d * I0
            dst = bass.AP(S["ao"].tensor, ((g * T + t0) * 4 + hh) * 132, [[d * 4 * 132, 128], [1, 132]])
            k.dma(dst, a_[:], r=[a_])

        i = 0
        load_head(0, 0)
        for head in range(12):
            bi = head % 2
            if head + 1 < 12:
                load_head(head + 1, (head + 1) % 2)
            stageA(head, 0, bi, i)
            for J in range(NB):
                if J + 1 < NB:
                    stageA(head, J + 1, bi, i + 1)
                stageB(head, J, bi, i)
                i += 1
        k.end_phase()


def phase2b(k, nc, I, S, NSEG):
    T = NSEG * SEG
    NB = T // 128
    with ExitStack() as pes:
        k.pes = pes
        cst = k.sb("cst2b", [128, 128], BF16)
        ao = [k.sb(f"ao2b{i}", [128, 3, 4, 132], F32) for i in range(2)]
        ls = k.sb("ls2b", [128, 3, 4], F32)
        mm_ = k.sb("mm2b", [128, 4], F32)
        ex = k.sb("ex2b", [128, 3, 4], F32)
        sm = k.sb("sm2b", [128, 4], F32)
        acc = k.sb("acc2b", [128, 4, 128], F32)
        ab = [k.sb(f"ab2b{i}", [128, 512], BF16) for i in range(2)]
        pt = [k.ps(f"pt2b{i}", [128, 1024], BF16) for i in range(2)]
        st = [k.sb(f"st2b{i}", [128, 4, 512], BF16) for i in range(2)]
        k.dma(cst[:], S["cstb"][:, 0, :], w=[cst])
        aov = S["ao"]
        for i in range(NB):
            a_ = ao[i % 2]
            for g in range(3):
                k.dma(a_[:, g, :, :], aov[g, i * 128:(i + 1) * 128, :, :], w=[a_])
            k.op("act", lambda e, a_=a_: e.activation(out=ls[:], in_=a_[:, :, :, 129], func=AF.Ln), r=[a_], w=[ls])
            k.op("dve", lambda e, a_=a_: e.tensor_tensor(out=ls[:], in0=ls[:], in1=a_[:, :, :, 128], op=ALU.add), r=[ls, a_], w=[ls])
            k.op("dve", lambda e: e.tensor_tensor(out=mm_[:], in0=ls[:, 0, :], in1=ls[:, 1, :], op=ALU.max), r=[ls], w=[mm_])
            k.op("dve", lambda e: e.tensor_tensor(out=mm_[:], in0=mm_[:], in1=ls[:, 2, :], op=ALU.max), r=[ls, mm_], w=[mm_])
            for g in range(3):
                k.op("dve", lambda e, g=g: e.tensor_tensor(out=ex[:, g, :], in0=ls[:, g, :], in1=mm_[:], op=ALU.subtract), r=[ls, mm_], w=[ex])
            k.op("act", lambda e: e.activation(out=ex[:], in_=ex[:], func=AF.Exp), r=[ex], w=[ex])
            k.op("dve", lambda e: e.tensor_tensor(out=sm[:], in0=ex[:, 0, :], in1=ex[:, 1, :], op=ALU.add), r=[ex], w=[sm])
            k.op("dve", lambda e: e.tensor_tensor(out=sm[:], in0=sm[:], in1=ex[:, 2, :], op=ALU.add), r=[ex, sm], w=[sm])
            k.op("dve", lambda e: e.reciprocal(out=sm[:], in_=sm[:]), r=[sm], w=[sm])
            for g in range(3):
                k.op("dve", lambda e, g=g: e.tensor_tensor(out=ex[:, g, :], in0=ex[:, g, :], in1=sm[:], op=ALU.mult), r=[ex, sm], w=[ex])
            b_ = ab[i % 2]
            for h in range(4):
                k.op("dve", lambda e, h=h, a_=a_: e.tensor_scalar(out=acc[:, h, :], in0=a_[:, 0, h, 0:128], scalar1=ex[:, 0, h:h + 1], scalar2=None, op0=ALU.mult), r=[a_, ex], w=[acc])
                k.op("dve", lambda e, h=h, a_=a_: e.scalar_tensor_tensor(out=acc[:, h, :], in0=a_[:, 1, h, 0:128], scalar=ex[:, 1, h:h + 1], in1=acc[:, h, :], op0=ALU.mult, op1=ALU.add), r=[a_, ex, acc], w=[acc])
                k.op("dve", lambda e, h=h, a_=a_, b_=b_: e.scalar_tensor_tensor(out=b_[:, h * 128:(h + 1) * 128], in0=a_[:, 2, h, 0:128], scalar=ex[:, 2, h:h + 1], in1=acc[:, h, :], op0=ALU.mult, op1=ALU.add), r=[a_, ex, acc], w=[b_])
            p_ = pt[i % 2]
            for h in range(4):
                k.op("pe", lambda e, h=h, b_=b_, p_=p_: e.transpose(out=p_[:, h * 128:(h + 1) * 128], in_=b_[:, h * 128:(h + 1) * 128], identity=cst[:]), r=[b_, cst], w=[p_])
            s_ = st[(i // 4) % 2]
            t4 = i % 4
            k.op("act", lambda e, p_=p_, s_=s_, t4=t4: e.copy(out=s_[:, :, t4 * 128:(t4 + 1) * 128], in_=p_[:, 0:512].rearrange("p (h q) -> p h q", h=4)), r=[p_], w=[s_])
            if t4 == 3:
                t0 = (i - 3) * 128
                k.dma(S["attnT"].rearrange("c p t -> p c t")[:, :, t0:t0 + 512], s_[:], r=[s_])
        k.end_phase()


def ssd_sweep(k, nc, I, S, NSEG, dirn):
    T = NSEG * SEG
    NCH = T // 128
    tg = "f" if dirn == 0 else "b"
    iU, iL, iN = (2, 3, 4) if dirn == 0 else (5, 6, 7)
    with ExitStack() as pes:
        k.pes = pes
        cst = k.sb("cstS" + tg, [128, 4, 128], BF16)
        neg4 = k.sb("neg4" + tg, [128, 4, 128], BF16)
        ones = k.sb("onesS" + tg, [128, 128], BF16)
        flg = k.sb("flgS" + tg, [128, NF], F32)
        Abc = k.sb("Abc" + tg, [128, 128], F32)
        epst = k.sb("epsS" + tg, [128, 1], F32)
        xt = [k.sb(f"xtS{tg}{i}", [128, 64, 64], BF16) for i in range(2)]
        Bt = [k.sb(f"BtS{tg}{i}", [128, 1024], BF16) for i in range(2)]
        BTt = [k.sb(f"BTtS{tg}{i}", [128, 8, 128], BF16) for i in range(2)]
        CTt = [k.sb(f"CTtS{tg}{i}", [128, 8, 128], BF16) for i in range(2)]
        dtt = [k.sb(f"dttS{tg}{i}", [128, 64], F32) for i in range(2)]
        dtA = k.sb("dtA" + tg, [128, 64], F32)
        hi = [k.sb(f"hi{tg}{i}", [128, 64], BF16) for i in range(2)]
        lo = [k.sb(f"lo{tg}{i}", [128, 64], BF16) for i in range(2)]
        Rhi = [k.sb(f"Rhi{tg}{i}", [128, 64, 128], BF16) for i in range(2)]
        Etok = [k.sb(f"Etok{tg}{i}", [128, 64], F32) for i in range(2)]
        acs = k.sb("acs" + tg, [128, 64], F32)
        Ttot = [k.sb(f"Ttot{tg}{i}", [128, 64], F32) for i in range(2)]
        ELp = [k.sb(f"ELp{tg}{i}", [128, 64], F32) for i in range(2)]
        ELm = k.sb("ELm" + tg, [128, 64], F32)
        dec = k.sb("dec" + tg, [128, 64], F32)
        dtm = k.sb("dtm" + tg, [128, 64], F32)
        dd = k.sb("dd" + tg, [128, 64], F32)
        x1 = [k.sb(f"x1{tg}{i}", [128, 64, 64], BF16) for i in range(2)]
        x2 = [k.sb(f"x2{tg}{i}", [128, 64, 64], BF16) for i in range(2)]
        cbs = [k.sb(f"cbs{tg}{i}", [128, 8, 128], F32) for i in range(2)]
        dm = [k.sb(f"dm{tg}{i}", [128, 4, 128], F32) for i in range(2)]
        W = [k.sb(f"W{tg}{i}", [128, 4, 128], BF16) for i in range(2)]
        st = k.sb("st" + tg, [128, 64, 64], F32)
        stb = k.sb("stb" + tg, [128, 64, 64], BF16)
        stG = [k.view(f"stG{tg}{g}", st.t[:, 8 * g:8 * g + 8, :]) for g in range(8)]
        stbG = [k.view(f"stbG{tg}{g}", stb.t[:, 8 * g:8 * g + 8, :]) for g in range(8)]
        tmpL = [k.sb(f"tmpS{tg}{i}", [128, 8, 64], F32) for i in range(2)]
        tmp2L = [k.sb(f"tmp2S{tg}{i}", [128, 8, 64], F32) for i in range(2)]
        yo = [k.sb(f"yo{tg}{i}", [128, 8, 64], F32) for i in range(2)]
        pa = k.ps("pa" + tg, [128, 512], F32)
        pcb = [k.ps(f"pcb{tg}{i}", [128, 512], F32) for i in range(2)]
        pd = [k.ps(f"pd{tg}{i}", [128, 512], F32) for i in range(2)]
        pyi = k.ps("pyi" + tg, [128, 512], F32)
        pss = k.ps("pss" + tg, [128, 512], F32)
        psu = k.ps("psu" + tg, [128, 512], F32) if dirn == 0 else pss
        if dirn == 1:
            ptb = k.ps("ptbS", [128, 1024], BF16)
            Dbc = k.sb("DbcS", [128, 64], F32)
            nwg = [k.sb(f"nwgS{i}", [128, 512], F32) for i in range(2)]
            yfg = [k.sb(f"yfgS{i}", [128, 8, 64], F32) for i in range(2)]
            zsg = [k.sb(f"zsgS{i}", [128, 8, 64], F32) for i in range(2)]
            tD = k.sb("tDS", [128, 8, 64], F32)
            ssq = k.sb("ssqS", [128, 1], F32)
            rsd = k.sb("rsdS", [128, 1], F32)
            junk = k.sb("junkS", [128, 512], F32)
            ynb = k.sb("ynbS", [128, 4096], BF16)
            ynT = k.sb("ynTS", [128, 32, 128], BF16)
            k.dma(Dbc[:], I["ssd_d"][0:1, :].partition_broadcast(128), w=[Dbc])
        k.dma(cst[:, 0, :], S["cstb"][:, 0, :], w=[cst])
        k.dma(cst[:, 1, :], S["cstb"][:, iU, :], w=[cst])
        k.dma(cst[:, 2, :], S["cstb"][:, iL, :], w=[cst])
        for r4 in range(4):
            k.dma(neg4[:, r4, :], S["cstb"][:, iN, :], w=[neg4])
        k.dma(flg[:], I["flags"][0:1, :].partition_broadcast(128), w=[flg])
        k.dma(Abc[:], I["a_log"][0:1, :].partition_broadcast(128), w=[Abc])
        k.op("pool", lambda e: e.memset(ones[:], 1.0), w=[ones])
        k.op("pool", lambda e: e.memset(epst[:], 1e-6), w=[epst])
        k.op("act", lambda e: e.activation(out=Abc[:], in_=Abc[:], func=AF.Exp), r=[Abc], w=[Abc])
        k.op("dve", lambda e: e.tensor_scalar(out=Abc[:], in0=Abc[:], scalar1=-1.0, scalar2=None, op0=ALU.mult), r=[Abc], w=[Abc])
        ident = cst[:, 0, :]
        Um = cst[:, 1, :]
        Lm = cst[:, 2, :]
        k.op("dve", lambda e: e.memset(st[:], 0.0), w=[st] + stG)
        k.op("pool", lambda e: e.memset(stb[:], 0.0), w=[stb] + stbG)

        order = list(range(NCH)) if dirn == 0 else list(range(NCH - 1, -1, -1))

        def load(c, bi):
            t0 = c * 128
            k.dma(xt[bi][:], S["xtok"][t0:t0 + 128, :].rearrange("p (h q) -> p h q", q=64), w=[xt[bi]])
            k.dma(Bt[bi][:], S["Btok"][t0:t0 + 128, :], w=[Bt[bi]])
            k.dma(BTt[bi][:], S["BT"][:, :, t0:t0 + 128].rearrange("g n t -> n g t"), w=[BTt[bi]])
            k.dma(CTt[bi][:], S["CT"][:, :, t0:t0 + 128].rearrange("g n t -> n g t"), w=[CTt[bi]])
            k.dma(dtt[bi][:], S["dts"][t0:t0 + 128, dirn * 64:(dirn + 1) * 64], w=[dtt[bi]])

        def prep_parts(ci):
            b2 = ci % 2
            x_, BT_, CT_, dt_ = xt[b2], BTt[b2], CTt[b2], dtt[b2]
            hi_, lo_, R_, E_, T_, P_ = hi[b2], lo[b2], Rhi[b2], Etok[b2], Ttot[b2], ELp[b2]

            def sa():
                k.op("dve", lambda e: e.tensor_tensor(out=dtA[:], in0=dt_[:], in1=Abc[:, dirn * 64:(dirn + 1) * 64], op=ALU.mult), r=[dt_, Abc], w=[dtA])
                k.op("dve", lambda e: e.tensor_copy(out=hi_[:], in_=dtA[:]), r=[dtA], w=[hi_])
                k.op("dve", lambda e: e.tensor_tensor(out=lo_[:], in0=dtA[:], in1=hi_[:], op=ALU.subtract), r=[dtA, hi_], w=[lo_])
                k.op("pe", lambda e: e.matmul(pa[:, 0:64], Um, hi_[:], start=True, stop=False), r=[cst, hi_], w=[pa])
                k.op("pe", lambda e: e.matmul(pa[:, 0:64], Um, lo_[:], start=False, stop=True), r=[cst, lo_], w=[pa])
                k.op("pe", lambda e: e.matmul(pa[:, 64:128], ones[:], hi_[:], start=True, stop=False), r=[ones, hi_], w=[pa])
                k.op("pe", lambda e: e.matmul(pa[:, 64:128], ones[:], lo_[:], start=False, stop=True), r=[ones, lo_], w=[pa])
                k.op("pe", lambda e: e.matmul(pa[:, 128:192], Um, lo_[:], start=True, stop=True), r=[cst, lo_], w=[pa])

            def sb_():
                k.op("act", lambda e: e.activation(out=E_[:], in_=pa[:, 0:64], func=AF.Exp), r=[pa], w=[E_])
                k.op("act", lambda e: e.copy(out=acs[:], in_=pa[:, 0:64]), r=[pa], w=[acs])
                k.op("act", lambda e: e.activation(out=T_[:], in_=pa[:, 64:128], func=AF.Exp), r=[pa], w=[T_])
                k.op("act", lambda e: e.activation(out=P_[:], in_=pa[:, 128:192], func=AF.Exp), r=[pa], w=[P_])
                k.op("act", lambda e: e.activation(out=ELm[:], in_=pa[:, 128:192], func=AF.Exp, scale=-1.0), r=[pa], w=[ELm])
                k.op("dve", lambda e: e.tensor_tensor(out=dec[:], in0=pa[:, 64:128], in1=acs[:], op=ALU.subtract), r=[pa, acs], w=[dec])
                for g in range(8):
                    p_ = pcb[g // 4]
                    k.op("pe", lambda e, p_=p_, g=g: e.matmul(p_[:, (g % 4) * 128:(g % 4 + 1) * 128], BT_[:, g, :], CT_[:, g, :], start=True, stop=True), r=[BT_, CT_], w=[p_])

            def sc_():
                k.op("act", lambda e: e.activation(out=dec[:], in_=dec[:], func=AF.Exp), r=[dec], w=[dec])
                k.op("dve", lambda e: e.tensor_tensor(out=dtm[:], in0=dt_[:], in1=ELm[:], op=ALU.mult), r=[dt_, ELm], w=[dtm])
                k.op("dve", lambda e: e.tensor_tensor(out=dd[:], in0=dt_[:], in1=dec[:], op=ALU.mult), r=[dt_, dec], w=[dd])
                Um4 = bass.AP(cst.t, 128, [[512, 128], [0, 4], [1, 128]])
                for h2 in range(2):
                    k.op("dve", lambda e, h2=h2: e.tensor_tensor(out=cbs[b2][:, 4 * h2:4 * h2 + 4, :], in0=pcb[h2][:].rearrange("p (g t) -> p g t", g=4), in1=Um4, op=ALU.mult), r=[pcb[h2], cst], w=[cbs[b2]])

            def rp(g):
                Ub = bass.AP(cst.t, 128, [[512, 128], [0, 8], [1, 128]])
                hib = bass.AP(hi_.t, 8 * g, [[64, 128], [1, 8], [0, 128]])
                k.op("pool", lambda e: e.tensor_tensor(out=R_[:, 8 * g:8 * g + 8, :], in0=Ub, in1=hib, op=ALU.mult), r=[cst, hi_], w=[R_])

            def xp(g):
                dtmb = bass.AP(dtm.t, 8 * g, [[64, 128], [1, 8], [0, 64]])
                ddb = bass.AP(dd.t, 8 * g, [[64, 128], [1, 8], [0, 64]])
                k.op("pool", lambda e: e.tensor_tensor(out=x1[b2][:, 8 * g:8 * g + 8, :], in0=x_[:, 8 * g:8 * g + 8, :], in1=dtmb, op=ALU.mult), r=[x_, dtm], w=[x1[b2]])
                k.op("pool", lambda e: e.tensor_tensor(out=x2[b2][:, 8 * g:8 * g + 8, :], in0=x_[:, 8 * g:8 * g + 8, :], in1=ddb, op=ALU.mult), r=[x_, dd], w=[x2[b2]])
            slots = [[] for _ in range(8)]
            slots[0] = [sa]
            slots[1] = [sb_, lambda: rp(0), lambda: rp(1)]
            slots[2] = [sc_, lambda: rp(2), lambda: rp(3)]
            slots[3] = [lambda: rp(4), lambda: rp(5), lambda: xp(0), lambda: xp(1)]
            slots[4] = [lambda: rp(6), lambda: rp(7), lambda: xp(2), lambda: xp(3)]
            slots[5] = [lambda: xp(4), lambda: xp(5)]
            slots[6] = [lambda: xp(6), lambda: xp(7)]
            return slots

        load(order[0], 0)
        if NCH > 1:
            load(order[1], 1)
        for sl in prep_parts(0):
            for f_ in sl:
                f_()
        gcount = 0
        for ci, c in enumerate(order):
            bi = ci % 2
            t0 = c * 128
            nslots = prep_parts(ci + 1) if ci + 1 < NCH else [[] for _ in range(8)]
            x_, B_, BT_, CT_, dt_ = xt[bi], Bt[bi], BTt[bi], CTt[bi], dtt[bi]
            R_, E_, T_, P_, x1_, x2_, cbs_ = Rhi[bi], Etok[bi], Ttot[bi], ELp[bi], x1[bi], x2[bi], cbs[bi]
            seg = c // 16
            bnd = (c % 16 == 0) if dirn == 0 else (c % 16 == 15)
            if bnd and ci > 0:
                fcol = 2 * seg + dirn
                k.op("dve", lambda e, fcol=fcol: e.tensor_scalar(out=st[:], in0=st[:], scalar1=flg[:, fcol:fcol + 1], scalar2=None, op0=ALU.mult), r=[st, flg] + stG, w=[st] + stG)
                k.op("act", lambda e: e.copy(out=stb[:], in_=st[:]), r=[st] + stG, w=[stb] + stbG)
            def diffexp(hq):
                p_ = pd[hq % 2]
                d_ = dm[hq % 2]
                k.op("pe", lambda e, p_=p_, hq=hq, R_=R_: e.matmul(p_[:], Lm, R_[:, 4 * hq:4 * hq + 4, :], start=True, stop=True), r=[cst, R_], w=[p_])
                k.op("act", lambda e, p_=p_, d_=d_: e.activation(out=d_[:], in_=p_[:].rearrange("p (h t) -> p h t", h=4), func=AF.Exp), r=[p_], w=[d_])

            diffexp(0)
            for g in range(8):
                for q2 in range(2):
                    hq = 2 * g + q2
                    d_ = dm[hq % 2]
                    w_ = W[hq % 2]
                    if hq + 1 < 16:
                        diffexp(hq + 1)
                    cbb = bass.AP(cbs_.t, g * 128, [[1024, 128], [0, 4], [1, 128]])
                    k.op("dve", lambda e, d_=d_, w_=w_, cbb=cbb: e.tensor_tensor(out=w_[:], in0=d_[:], in1=cbb, op=ALU.mult), r=[d_, cbs_], w=[w_])
                    for hl in range(4):
                        h = 4 * hq + hl
                        k.op("pe", lambda e, w_=w_, hl=hl, h=h, x1_=x1_: e.matmul(pyi[:, (h % 8) * 64:(h % 8 + 1) * 64], w_[:, hl, :], x1_[:, h, :], start=True, stop=True), r=[w_, x1_], w=[pyi])
                k.op("pe", lambda e, g=g, CT_=CT_: e.matmul(pss[:], CT_[:, g, :], stb[:, 8 * g:8 * g + 8, :], start=True, stop=True), r=[CT_, stbG[g]], w=[pss])
                Eb = bass.AP(E_.t, 8 * g, [[64, 128], [1, 8], [0, 64]])
                Pb_ = bass.AP(P_.t, 8 * g, [[64, 128], [1, 8], [0, 64]])
                y_ = yo[gcount % 2]
                tmp, tmp2 = tmpL[gcount % 2], tmp2L[gcount % 2]
                k.op("dve", lambda e, Eb=Eb, tmp=tmp: e.tensor_tensor(out=tmp[:], in0=pss[:].rearrange("p (h q) -> p h q", h=8), in1=Eb, op=ALU.mult), r=[pss, E_], w=[tmp])
                k.op("dve", lambda e, Pb_=Pb_, tmp2=tmp2: e.tensor_tensor(out=tmp2[:], in0=pyi[:].rearrange("p (h q) -> p h q", h=8), in1=Pb_, op=ALU.mult), r=[pyi, P_], w=[tmp2])
                k.op("pool", lambda e, y_=y_, tmp=tmp, tmp2=tmp2: e.tensor_tensor(out=y_[:], in0=tmp[:], in1=tmp2[:], op=ALU.add), r=[tmp, tmp2], w=[y_])
                k.op("pe", lambda e, g=g, B_=B_, x2_=x2_: e.matmul(psu[:], B_[:, g * 128:(g + 1) * 128], x2_[:, 8 * g:8 * g + 8, :], start=True, stop=True), r=[B_, x2_], w=[psu])
                Tb = bass.AP(T_.t, 8 * g, [[64, 128], [1, 8], [0, 64]])
                stg = st[:, 8 * g:8 * g + 8, :]
                k.op("pool", lambda e, stg=stg, Tb=Tb: e.tensor_tensor(out=stg, in0=stg, in1=Tb, op=ALU.mult), r=[stG[g], T_, stbG[g]], w=[stG[g]])
                k.op("dve", lambda e, stg=stg: e.tensor_tensor(out=stg, in0=stg, in1=psu[:].rearrange("p (h q) -> p h q", h=8), op=ALU.add), r=[stG[g], psu], w=[stG[g]])
                k.op("act", lambda e, g=g, stg=stg: e.copy(out=stb[:, 8 * g:8 * g + 8, :], in_=stg), r=[stG[g]], w=[stbG[g]])
                if dirn == 0:
                    k.dma(S["yf"][t0:t0 + 128, g * 512:(g + 1) * 512].rearrange("p (h q) -> p h q", h=8), y_[:], r=[y_])
                else:
                    f_, z_ = yfg[gcount % 2], zsg[gcount % 2]
                    k.dma(f_[:], S["yf"][t0:t0 + 128, g * 512:(g + 1) * 512].rearrange("p (h q) -> p h q", h=8), w=[f_])
                    k.dma(z_[:], S["zs"][t0:t0 + 128, g * 512:(g + 1) * 512].rearrange("p (h q) -> p h q", h=8), w=[z_])
                    n_ = nwg[gcount % 2]
                    k.dma(n_[:], I["ssd_norm_w"][0:1, g * 512:(g + 1) * 512].partition_broadcast(128), w=[n_])
                    Db = bass.AP(Dbc.t, 8 * g, [[64, 128], [1, 8], [0, 64]])
                    k.op("pool", lambda e, x_=x_, g=g, Db=Db: e.tensor_tensor(out=tD[:], in0=x_[:, 8 * g:8 * g + 8, :], in1=Db, op=ALU.mult), r=[x_, Dbc], w=[tD])
                    k.op("pool", lambda e, f_=f_: e.tensor_tensor(out=tD[:], in0=tD[:], in1=f_[:], op=ALU.add), r=[tD, f_], w=[tD])
                    k.op("dve", lambda e, y_=y_: e.tensor_tensor(out=y_[:], in0=y_[:], in1=tD[:], op=ALU.add), r=[y_, tD], w=[y_])
                    k.op("dve", lambda e, y_=y_, z_=z_: e.tensor_tensor(out=y_[:], in0=y_[:], in1=z_[:], op=ALU.mult), r=[y_, z_], w=[y_])
                    yf2 = y_[:].rearrange("p h q -> p (h q)")
                    k.op("dve", lambda e, yf2=yf2: e.scalar_tensor_tensor(out=junk[:], in0=yf2, scalar=1.0, in1=yf2, op0=ALU.mult, op1=ALU.mult, accum_out=ssq[:]), r=[y_], w=[junk, ssq])
                    k.op("act", lambda e: e.activation(out=rsd[:], in_=ssq[:], func=AF.Ln, scale=1.0 / 512, bias=epst[:]), r=[ssq, epst], w=[rsd])
                    k.op("act", lambda e: e.activation(out=rsd[:], in_=rsd[:], func=AF.Exp, scale=-0.5), r=[rsd], w=[rsd])
                    k.op("dve", lambda e, yf2=yf2, g=g, n_=n_: e.scalar_tensor_tensor(out=ynb[:, g * 512:(g + 1) * 512], in0=yf2, scalar=rsd[:], in1=n_[:], op0=ALU.mult, op1=ALU.mult), r=[y_, rsd, n_], w=[ynb])
                    for q4 in range(4):
                        kc = 4 * g + q4
                        k.op("pe", lambda e, kc=kc: e.transpose(out=ptb[:, (kc % 8) * 128:(kc % 8 + 1) * 128], in_=ynb[:, kc * 128:(kc + 1) * 128], identity=ident), r=[ynb, cst], w=[ptb])
                    if g % 2 == 1:
                        k0 = 4 * (g - 1)
                        k.op("act", lambda e, k0=k0: e.copy(out=ynT[:, k0:k0 + 8, :], in_=ptb[:].rearrange("p (c t) -> p c t", c=8)), r=[ptb], w=[ynT])
                for f_ in nslots[g]:
                    f_()
                gcount += 1
            if dirn == 1:
                k.dma(S["ynT"].rearrange("c p t -> p c t")[:, :, t0:t0 + 128], ynT[:], r=[ynT])
            if ci + 2 < NCH:
                load(order[ci + 2], bi)
        k.end_phase()


def phase5(k, nc, I, S, NSEG):
    T = NSEG * SEG
    NBK = T // 512
    with ExitStack() as pes:
        k.pes = pes
        wat = k.sb("wat5", [128, 4, D], BF16)
        wss = [k.sb(f"wss5{i}", [128, 32, 256], BF16) for i in range(2)]
        wo = [k.sb(f"wo5{i}", [128, KC, 512], BF16) for i in range(2)]
        aT = [k.sb(f"aT5{i}", [128, 4, 512], BF16) for i in range(2)]
        yT = [k.sb(f"yT5{i}", [128, 32, 512], BF16) for i in range(2)]
        ga = [k.sb(f"ga5{i}", [128, 512], F32) for i in range(2)]
        gs = [k.sb(f"gs5{i}", [128, 512], F32) for i in range(2)]
        t1 = k.sb("t15", [128, 512], F32)
        t2 = k.sb("t25", [128, 512], F32)
        mT = k.sb("mT5", [128, KC, 512], BF16)
        gm = k.sb("gm5", [128, D], F32)
        xt = [k.sb(f"xt5{i}", [128, 512], F32) for i in range(2)]
        xo = [k.sb(f"xo5{i}", [128, 512], F32) for i in range(2)]
        pa = [k.ps(f"pa5{i}", [128, 512], F32) for i in range(2)]
        pb = [k.ps(f"pb5{i}", [128, 512], F32) for i in range(2)]
        po = [k.ps(f"po5{i}", [128, 512], F32) for i in range(4)]
        k.dma(wat[:], S["f_w_attn"].rearrange("(k p) n -> p k n", p=128), w=[wat])
        aTd = S["attnT"].rearrange("c p t -> p c t")
        yTd = S["ynT"].rearrange("c p t -> p c t")
        wsd = S["f_w_ssd"]
        wod = S["f_w_out"]
        nws = 0
        nwo = 0
        cnt = 0

        def load_blk(b, bi):
            t0 = b * 512
            k.dma(aT[bi][:], aTd[:, :, t0:t0 + 512], w=[aT[bi]])
            k.dma(yT[bi][:], yTd[:, :, t0:t0 + 512], w=[yT[bi]])

        load_blk(0, 0)
        k.dma(wss[0][:], wsd[:, 0:256].rearrange("(k p) n -> p k n", p=128), w=[wss[0]])
        k.dma(wo[0][:], wtile_ap(wod, 0, 512), w=[wo[0]])
        for b in range(NBK):
            bi = b % 2
            t0 = b * 512
            seg = t0 // SEG
            if b % 4 == 0:
                k.dma(gm[:], S["modv"][seg:seg + 1, 2 * D:3 * D].partition_broadcast(128), w=[gm])
            if b + 1 < NBK:
                load_blk(b + 1, (b + 1) % 2)
            a_, y_ = aT[bi], yT[bi]
            for w8 in range(8):
                ws_ = wss[nws % 2]
                nxt = (w8 + 1) % 8
                if not (b == NBK - 1 and w8 == 7):
                    k.dma(wss[(nws + 1) % 2][:], wsd[:, 256 * nxt:256 * nxt + 256].rearrange("(k p) n -> p k n", p=128), w=[wss[(nws + 1) % 2]])
                nws += 1
                for c2 in range(2):
                    dc = 2 * w8 + c2
                    pa_, pb_ = pa[cnt % 2], pb[cnt % 2]
                    ga_, gs_ = ga[cnt % 2], gs[cnt % 2]
                    cnt += 1
                    k.dma(ga_[:], S["gT"][0, dc, :, t0:t0 + 512], w=[ga_])
                    k.dma(gs_[:], S["gT"][1, dc, :, t0:t0 + 512], w=[gs_])
                    for kc in range(4):
                        k.op("pe", lambda e, pa_=pa_, kc=kc, dc=dc, a_=a_: e.matmul(pa_[:], wat[:, kc, dc * 128:(dc + 1) * 128], a_[:, kc, :], start=(kc == 0), stop=(kc == 3)), r=[wat, a_], w=[pa_])
                    for kc in range(32):
                        k.op("pe", lambda e, pb_=pb_, kc=kc, c2=c2, ws_=ws_, y_=y_: e.matmul(pb_[:], ws_[:, kc, c2 * 128:(c2 + 1) * 128], y_[:, kc, :], start=(kc == 0), stop=(kc == 31)), r=[ws_, y_], w=[pb_])
                    k.op("dve", lambda e, pa_=pa_, ga_=ga_: e.tensor_tensor(out=t1[:], in0=pa_[:], in1=ga_[:], op=ALU.mult), r=[pa_, ga_], w=[t1])
                    k.op("dve", lambda e, pb_=pb_, gs_=gs_: e.tensor_tensor(out=t2[:], in0=pb_[:], in1=gs_[:], op=ALU.mult), r=[pb_, gs_], w=[t2])
                    k.op("pool", lambda e, dc=dc: e.tensor_tensor(out=mT[:, dc, :], in0=t1[:], in1=t2[:], op=ALU.add), r=[t1, t2], w=[mT])
            for d4 in range(4):
                wo_ = wo[nwo % 2]
                nxt = (d4 + 1) % 4
                if not (b == NBK - 1 and d4 == 3):
                    k.dma(wo[(nwo + 1) % 2][:], wtile_ap(wod, 512 * nxt, 512), w=[wo[(nwo + 1) % 2]])
                nwo += 1
                for ts in range(4):
                    p_ = po[ts]
                    x_ = xt[ts % 2]
                    o_ = xo[ts % 2]
                    r0 = t0 + ts * 128
                    k.dma(x_[:], I["x"][r0:r0 + 128, d4 * 512:(d4 + 1) * 512], w=[x_])
                    for kc in range(KC):
                        k.op("pe", lambda e, p_=p_, kc=kc, ts=ts, wo_=wo_: e.matmul(p_[:], mT[:, kc, ts * 128:(ts + 1) * 128], wo_[:, kc, :], start=(kc == 0), stop=(kc == KC - 1)), r=[mT, wo_], w=[p_])
                    k.op("dve", lambda e, p_=p_, o_=o_, d4=d4: e.tensor_tensor(out=o_[:], in0=p_[:], in1=gm[:, d4 * 512:(d4 + 1) * 512], op=ALU.mult), r=[p_, gm], w=[o_])
                    k.op("pool", lambda e, o_=o_, x_=x_: e.tensor_tensor(out=o_[:], in0=o_[:], in1=x_[:], op=ALU.add), r=[o_, x_], w=[o_])
                    k.dma(S["x1"][r0:r0 + 128, d4 * 512:(d4 + 1) * 512], o_[:], r=[o_])
        k.end_phase()


def phase6(k, nc, I, S, NSEG):
    T = NSEG * SEG
    NBK = T // 512
    NJ = DFF // 128
    with ExitStack() as pes:
        k.pes = pes
        hT = k.sb("hT6", [128, KC, 514], BF16)
        flg = k.sb("flg6", [128, NF], F32)
        cw = k.sb("cw6", [128, NJ, 3], F32)
        cb = k.sb("cb6", [128, NJ], F32)
        wg = [k.sb(f"wg6{i}", [128, KC, 256], BF16) for i in range(2)]
        wv = [k.sb(f"wv6{i}", [128, KC, 256], BF16) for i in range(2)]
        wd = [k.sb(f"wd6{i}", [128, 22, 256], BF16) for i in range(4)]
        E = k.sb("E6", [128, 514], F32)
        a = k.sb("a6", [128, 512], F32)
        gl = k.sb("gl6", [128, 512], F32)
        uT = k.sb("uT6", [128, NJ, 512], BF16)
        gf = k.sb("gf6", [128, D], F32)
        xt = [k.sb(f"xt6{i}", [128, 256], F32) for i in range(2)]
        xo = [k.sb(f"xo6{i}", [128, 256], F32) for i in range(2)]
        pg = [k.ps(f"pg6{i}", [128, 512], F32) for i in range(2)]
        pv = [k.ps(f"pv6{i}", [128, 512], F32) for i in range(2)]
        ph = k.ps("ph6", [128, 512], F32)
        po = [k.ps(f"po6{i}", [128, 512], F32) for i in range(2)]
        k.dma(flg[:], I["flags"][0:1, :].partition_broadcast(128), w=[flg])
        k.dma(cw[:], I["ffn_conv_w"][:, :, :], w=[cw])
        k.dma(cb[:], I["ffn_conv_b"][:, :], w=[cb])
        hTd = S["h2T"].rearrange("k p t -> p k t")
        wud = S["f_w_up"]
        wdd = S["f_w_down"]
        nwu = 0
        nwd = 0
        cnt = 0

        def load_up(i2, slot):
            k.dma(wg[slot][:], wtile_ap(wud, 256 * i2, 256), w=[wg[slot]])
            k.dma(wv[slot][:], wtile_ap(wud, DFF + 256 * i2, 256), w=[wv[slot]])

        def load_down(idx, slot):
            d8, kh = idx // 2, idx % 2
            k.dma(wd[slot][:], wdd[kh * 2816:(kh + 1) * 2816, d8 * 256:(d8 + 1) * 256].rearrange("(k p) n -> p k n", p=128), w=[wd[slot]])

        load_up(0, 0)
        load_down(0, 0)
        load_down(1, 1)
        for b in range(NBK):
            t0 = b * 512
            seg = t0 // SEG
            if b % 4 == 0:
                k.dma(gf[:], S["modv"][seg:seg + 1, 5 * D:6 * D].partition_broadcast(128), w=[gf])
            k.dma(hT[:], hTd[:, :, t0 + 1:t0 + 515], w=[hT])
            if b % 4 == 0:
                k.op("dve", lambda e, seg=seg: e.tensor_scalar(out=hT[:, :, 0:1], in0=hT[:, :, 0:1], scalar1=flg[:, 2 * seg:2 * seg + 1], scalar2=None, op0=ALU.mult), r=[hT, flg], w=[hT])
            if b % 4 == 3:
                k.op("dve", lambda e, seg=seg: e.tensor_scalar(out=hT[:, :, 513:514], in0=hT[:, :, 513:514], scalar1=flg[:, 2 * seg + 1:2 * seg + 2], scalar2=None, op0=ALU.mult), r=[hT, flg], w=[hT])
            for i2 in range(22):
                slot = nwu % 2
                nxt = (i2 + 1) % 22
                if not (b == NBK - 1 and i2 == 21):
                    load_up(nxt, (nwu + 1) % 2)
                nwu += 1
                for c2 in range(2):
                    j = 2 * i2 + c2
                    pg_, pv_ = pg[cnt % 2], pv[cnt % 2]
                    cnt += 1
                    for kc in range(KC):
                        k.op("pe", lambda e, pg_=pg_, kc=kc, c2=c2, slot=slot: e.matmul(pg_[:], wg[slot][:, kc, c2 * 128:(c2 + 1) * 128], hT[:, kc, 1:513], start=(kc == 0), stop=(kc == KC - 1)), r=[wg[slot], hT], w=[pg_])
                    for kc in range(KC):
                        rhs = bass.AP(hT.t, kc * 514, [[KC * 514, 128], [513, 2]])
                        k.op("pe", lambda e, kc=kc, c2=c2, slot=slot, rhs=rhs: e.matmul(ph[:, 0:2], wg[slot][:, kc, c2 * 128:(c2 + 1) * 128], rhs, start=(kc == 0), stop=(kc == KC - 1)), r=[wg[slot], hT], w=[ph])
                    for kc in range(KC):
                        k.op("pe", lambda e, pv_=pv_, kc=kc, c2=c2, slot=slot: e.matmul(pv_[:], wv[slot][:, kc, c2 * 128:(c2 + 1) * 128], hT[:, kc, 1:513], start=(kc == 0), stop=(kc == KC - 1)), r=[wv[slot], hT], w=[pv_])
                    k.op("act", lambda e, pg_=pg_: e.copy(out=E[:, 1:513], in_=pg_[:]), r=[pg_], w=[E])
                    k.op("act", lambda e: e.copy(out=E[:, 0:1], in_=ph[:, 0:1]), r=[ph], w=[E])
                    k.op("act", lambda e: e.copy(out=E[:, 513:514], in_=ph[:, 1:2]), r=[ph], w=[E])
                    k.op("dve", lambda e, j=j: e.tensor_scalar(out=a[:], in0=E[:, 0:512], scalar1=cw[:, j, 0:1], scalar2=cb[:, j:j + 1], op0=ALU.mult, op1=ALU.add), r=[E, cw, cb], w=[a])
                    k.op("dve", lambda e, j=j: e.scalar_tensor_tensor(out=a[:], in0=E[:, 1:513], scalar=cw[:, j, 1:2], in1=a[:], op0=ALU.mult, op1=ALU.add), r=[E, cw, a], w=[a])
                    k.op("dve", lambda e, j=j: e.scalar_tensor_tensor(out=a[:], in0=E[:, 2:514], scalar=cw[:, j, 2:3], in1=a[:], op0=ALU.mult, op1=ALU.add), r=[E, cw, a], w=[a])
                    k.op("act", lambda e: e.activation(out=gl[:], in_=a[:], func=AF.Gelu), r=[a], w=[gl])
                    k.op("dve", lambda e, j=j, pv_=pv_: e.tensor_tensor(out=uT[:, j, :], in0=gl[:], in1=pv_[:], op=ALU.mult), r=[gl, pv_], w=[uT])
            for d8 in range(8):
                s0 = (2 * nwd) % 4
                if not (b == NBK - 1 and d8 == 7):
                    nd = (d8 + 1) % 8
                    load_down(2 * nd, (s0 + 2) % 4)
                    load_down(2 * nd + 1, (s0 + 3) % 4)
                nwd += 1
                for ts in range(4):
                    p_ = po[ts % 2]
                    for kh in range(2):
                        w_ = wd[s0 + kh]
                        for kc in range(22):
                            k.op("pe", lambda e, p_=p_, kc=kc, ts=ts, kh=kh, w_=w_: e.matmul(p_[:, 0:256], uT[:, kh * 22 + kc, ts * 128:(ts + 1) * 128], w_[:, kc, :], start=(kh == 0 and kc == 0), stop=(kh == 1 and kc == 21)), r=[uT, w_], w=[p_])
                    x_, o_ = xt[ts % 2], xo[ts % 2]
                    r0 = t0 + ts * 128
                    k.dma(x_[:], S["x1"][r0:r0 + 128, d8 * 256:(d8 + 1) * 256], w=[x_])
                    k.op("dve", lambda e, p_=p_, o_=o_, d8=d8: e.tensor_tensor(out=o_[:], in0=p_[:, 0:256], in1=gf[:, d8 * 256:(d8 + 1) * 256], op=ALU.mult), r=[p_, gf], w=[o_])
                    k.op("pool", lambda e, o_=o_, x_=x_: e.tensor_tensor(out=o_[:], in0=o_[:], in1=x_[:], op=ALU.add), r=[o_, x_], w=[o_])
                    k.dma(S["x2"][r0:r0 + 128, d8 * 256:(d8 + 1) * 256], o_[:], r=[o_])
        k.end_phase()


def final_norm(k, nc, I, S, y, NSEG):
    T = NSEG * SEG
    with ExitStack() as pes:
        k.pes = pes
        nw = k.sb("nwF", [128, D], F32)
        epst = k.sb("epsF", [128, 1], F32)
        xt = [k.sb(f"xtF{i}", [128, D], F32) for i in range(3)]
        yo = [k.sb(f"yoF{i}", [128, D], F32) for i in range(3)]
        junk = k.sb("junkF", [128, D], F32)
        ss = [k.sb(f"ssF{i}", [128, 1], F32) for i in range(3)]
        k.dma(nw[:], I["norm_f_w"][0:1, :].partition_broadcast(128), w=[nw])
        k.op("pool", lambda e: e.memset(epst[:], 1e-6), w=[epst])
        for i in range(T // 128):
            x_, o_, s_ = xt[i % 3], yo[i % 3], ss[i % 3]
            k.dma(x_[:], S["x2"][i * 128:(i + 1) * 128, :], w=[x_])
            k.op("pool", lambda e, x_=x_: e.tensor_tensor(out=junk[:], in0=x_[:], in1=x_[:], op=ALU.mult), r=[x_], w=[junk])
            k.op("dve", lambda e, s_=s_: e.reduce_sum(out=s_[:], in_=junk[:], axis=AX.X), r=[junk], w=[s_])
            k.op("act", lambda e, s_=s_: e.activation(out=s_[:], in_=s_[:], func=AF.Ln, scale=1.0 / D, bias=epst[:]), r=[s_, epst], w=[s_])
            k.op("act", lambda e, s_=s_: e.activation(out=s_[:], in_=s_[:], func=AF.Exp, scale=-0.5), r=[s_], w=[s_])
            k.op("dve", lambda e, x_=x_, o_=o_, s_=s_: e.scalar_tensor_tensor(out=o_[:], in0=x_[:], scalar=s_[:], in1=nw[:], op0=ALU.mult, op1=ALU.mult), r=[x_, s_, nw], w=[o_])
            k.dma(y[i * 128:(i + 1) * 128, :], o_[:], r=[o_])
        k.end_phase()


SEG = 2048
NEG = -30000.0
NF = 32
WNAMES = ["w_ada", "w_in", "w_attn_proj", "w_ssd_proj", "w_out", "w_up", "w_down"]
WKEY = dict(w_ada="w_ada", w_in="w_in", w_attn_proj="w_attn", w_ssd_proj="w_ssd", w_out="w_out", w_up="w_up", w_down="w_down")


def make_consts():
    cst = np.zeros((128, 8, 128), np.float32)
    i = np.arange(128)
    cst[:, 0, :] = np.eye(128)
    cst[i, 1, (i + 64) % 128] = 1.0
    kk, tt = np.meshgrid(i, i, indexing="ij")
    cst[:, 2, :] = (kk <= tt)
    cst[:, 3, :] = (kk > tt)
    cst[:, 4, :] = np.where(tt < kk, NEG, 0.0)
    cst[:, 5, :] = (kk >= tt)
    cst[:, 6, :] = (kk < tt)
    cst[:, 7, :] = np.where(tt > kk, NEG, 0.0)
    qq = np.arange(128)[:, None]
    k2 = np.arange(256)[None, :]
    band = np.where((k2 >= qq) & (k2 <= qq + 128), 0.0, NEG).astype(np.float32)
    return cst, band


def rope_tables(pos):
    inv_freq = (10000.0 ** (-(np.arange(0, 128, 2, dtype=np.float32) / np.float32(128)))).astype(np.float32)
    ang = (pos.astype(np.float32)[None, :] * inv_freq[:, None]).astype(np.float32)
    c = np.cos(ang.astype(np.float64)).astype(np.float32)
    s = np.sin(ang.astype(np.float64)).astype(np.float32)
    cos = np.concatenate([c, c], 0)
    sin = np.concatenate([-s, s], 0)
    return np.ascontiguousarray(cos), np.ascontiguousarray(sin)


def core_inputs(x_core, c_core, stype, params, nseg, wshard=None, ncores=8):
    T = nseg * SEG
    m = {}
    m["x"] = np.ascontiguousarray(x_core, dtype=np.float32)
    m["cT"] = np.asarray(c_core, dtype=np.float32).reshape(nseg, 16, 128).transpose(2, 1, 0)
    for name in WNAMES:
        w = params[name]
        if wshard is not None:
            r = w.shape[0] // ncores
            w = w[wshard * r:(wshard + 1) * r]
        m[WKEY[name]] = np.ascontiguousarray(w, dtype=np.float32)
    m["b_ada"] = params["b_ada"].reshape(1, -1)
    m["norm_mix_w"] = params["norm_mix_w"].reshape(1, -1)
    m["norm_ffn_w"] = params["norm_ffn_w"].reshape(1, -1)
    m["norm_f_w"] = params["norm_f_w"].reshape(1, -1)
    m["ssd_conv_w"] = params["ssd_conv_w"].reshape(5, 48, 128).transpose(2, 1, 0)
    m["ssd_conv_b"] = params["ssd_conv_b"].reshape(48, 128).T
    m["dt_bias"] = np.concatenate([params["dt_bias_fwd"].reshape(-1), params["dt_bias_bwd"].reshape(-1)]).reshape(1, 128)
    m["a_log"] = np.concatenate([params["a_log_fwd"].reshape(-1), params["a_log_bwd"].reshape(-1)]).reshape(1, 128)
    m["ssd_d"] = params["ssd_d"].reshape(1, 64)
    m["ssd_norm_w"] = params["ssd_norm_w"].reshape(1, -1)
    m["ffn_conv_w"] = params["ffn_conv_w"].reshape(3, 44, 128).transpose(2, 1, 0)
    m["ffn_conv_b"] = params["ffn_conv_b"].reshape(44, 128).T
    t = np.arange(T)
    pos = t if stype else (t % SEG)
    m["cosT"], m["sinT"] = rope_tables(pos.astype(np.float32))
    cst, band = make_consts()
    m["cst"] = cst
    m["band"] = band
    fl = np.zeros((1, NF), np.float32)
    if stype:
        for s in range(nseg):
            fl[0, 2 * s] = 1.0 if s > 0 else 0.0
            fl[0, 2 * s + 1] = 1.0 if s < nseg - 1 else 0.0
    nblk = 2 * nseg
    for b in range(nblk):
        s = b // 2
        fl[0, 8 + 2 * b] = 1.0 if (b % 2 == 1) else fl[0, 2 * s]
        fl[0, 8 + 2 * b + 1] = 1.0 if (b % 2 == 0) else fl[0, 2 * s + 1]
    m["flags"] = fl
    kb = np.zeros((4, 256), np.float32)
    nb = 0.0 if stype else NEG
    kb[0, :64] = nb
    kb[1, 192:] = nb
    kb[2, :64] = NEG
    kb[3, 192:] = NEG
    m["kb"] = kb
    return {k_: np.ascontiguousarray(v, dtype=np.float32) for k_, v in m.items()}


_PNAMES = ["w_ada", "b_ada", "norm_mix_w", "w_in", "ssd_conv_w", "ssd_conv_b", "dt_bias_fwd", "dt_bias_bwd", "a_log_fwd",
           "a_log_bwd", "ssd_d", "ssd_norm_w", "w_attn_proj", "w_ssd_proj", "w_out", "norm_ffn_w", "w_up", "ffn_conv_w",
           "ffn_conv_b", "w_down"]


def kernel(x_prompt, x_sample, c_prompt, c_sample, **params):
    p = {}
    for n in _PNAMES:
        p[n] = np.asarray(params[n])[0]
    p["norm_f_w"] = np.asarray(params["norm_f_w"])
    x_prompt = np.asarray(x_prompt)
    x_sample = np.asarray(x_sample)
    c_prompt = np.asarray(c_prompt)
    c_sample = np.asarray(c_sample)
    nseg = 4
    in_maps = []
    for core in range(8):
        if core < 4:
            xc = x_prompt[4 * core:4 * core + 4].reshape(nseg * SEG, D)
            cc = c_prompt[4 * core:4 * core + 4]
            st = False
        else:
            xc = x_sample[core - 4].reshape(nseg * SEG, D)
            cc = np.repeat(c_sample[core - 4:core - 3], nseg, axis=0)
            st = True
        in_maps.append(core_inputs(xc, cc, st, p, nseg, wshard=core))
    nc, _ = build(NSEG=nseg, gather=True)
    res = run_bass_kernel_spmd(nc, in_maps, core_ids=list(range(8)))
    y_prompt = np.empty((16, SEG, D), np.float32)
    y_sample = np.empty((4, 4 * SEG, D), np.float32)
    for core in range(8):
        yv = np.asarray(res.results[core]["y"], dtype=np.float32)
        if core < 4:
            y_prompt[4 * core:4 * core + 4] = yv.reshape(4, SEG, D)
        else:
            y_sample[core - 4] = yv
    return (y_prompt, y_sample)
```
